# Optimizing a Trainium2 kernel written in Bass

```python
import math
import jax, jax.numpy as jnp
from jax import lax
import numpy as np

D_MODEL = 1024
BATCH = 4
SEQ = 4096
DEPTH = 4

N_MIXERS = 4
N_S5 = (DEPTH + 3) // 4
N_HGRN = (DEPTH + 2) // 4
N_GDN = (DEPTH + 1) // 4
N_MOBA = DEPTH // 4
D_FF = 2816
RMS_EPS = 1e-6
S5_GROUP = 16
S5_GROUPS = D_MODEL // S5_GROUP
S5_STATE = 64
HG_HEADS = 8
HG_DK = D_MODEL // HG_HEADS
HG_DV = D_MODEL // HG_HEADS
HG_CHUNK = 32
HG_IN = 2 * HG_HEADS * HG_DK + 2 * HG_HEADS * HG_DV
GDN_QK_HEADS = 8
GDN_V_HEADS = 16
GDN_DK = 128
GDN_DV = 128
GDN_CONV = 4
GDN_CHUNK = 64
GDN_QKV = 2 * GDN_QK_HEADS * GDN_DK + GDN_V_HEADS * GDN_DV
GDN_IN = GDN_QKV + GDN_V_HEADS * GDN_DV + 2 * GDN_V_HEADS
MOBA_HEADS = 8
MOBA_DH = 128
MOBA_BLOCK = 256
MOBA_TOPK = 3
MOBA_QCHUNK = 32
REL_BUCKETS = 32
REL_MAX_DIST = 2048

kernel_name = 'hybrid_interleaved_s5_hgrn2_gdn_moba'


def _rms(x, g):
    xf = x.astype(jnp.float32)
    y = xf * lax.rsqrt(jnp.mean(xf * xf, axis=-1, keepdims=True) + RMS_EPS)
    return (y * g.astype(jnp.float32)).astype(x.dtype)


def _l2norm(t):
    return t * lax.rsqrt(jnp.sum(t * t, axis=-1, keepdims=True) + 1e-6)


def _swiglu(x, w_gate, w_up, w_down):
    return (jax.nn.silu(x @ w_gate) * (x @ w_up)) @ w_down


def _causal_conv(x, w):
    k_len, ch = w.shape
    return lax.conv_general_dilated(x, w[:, None, :].astype(x.dtype), window_strides=(1,),
                                    padding=[(k_len - 1, 0)],
                                    dimension_numbers=('NWC', 'WIO', 'NWC'),
                                    feature_group_count=ch)


def _t5_bucket(rel):
    n = jnp.maximum(rel, 0)
    max_exact = REL_BUCKETS // 2
    nf = jnp.maximum(n, 1).astype(jnp.float32)
    large = max_exact + (jnp.log(nf / max_exact) / math.log(REL_MAX_DIST / max_exact)
                         * (REL_BUCKETS - max_exact)).astype(jnp.int32)
    large = jnp.minimum(large, REL_BUCKETS - 1)
    return jnp.where(n < max_exact, n, large)


def _s5_mixer(u, lam_re, lam_im, log_dt, b_re, b_im, c_re, c_im, d_skip, w_glu):
    f32 = jnp.float32
    bsz, L, _ = u.shape
    lam = lax.complex(jnp.minimum(lam_re.astype(f32), -1e-4), lam_im.astype(f32))
    dt = jnp.exp(log_dt.astype(f32))[:, None]
    lam_bar = jnp.exp(lam * dt)
    b_bar = ((lam_bar - 1.0) / lam)[..., None] * lax.complex(b_re.astype(f32), b_im.astype(f32))
    ug = u.astype(f32).reshape(bsz, L, S5_GROUPS, S5_GROUP)
    bu = jnp.einsum('blgh,gph->lbgp', ug, b_bar)
    a = jnp.broadcast_to(lam_bar, (L, 1) + lam_bar.shape)

    def combine(e1, e2):
        a1, b1 = e1
        a2, b2 = e2
        return a1 * a2, a2 * b1 + b2

    _, h = lax.associative_scan(combine, (a, bu), axis=0)
    c = lax.complex(c_re.astype(f32), c_im.astype(f32))
    y = jnp.einsum('lbgp,ghp->blgh', h, c).real.reshape(bsz, L, D_MODEL)
    y = y + d_skip.astype(f32) * u.astype(f32)
    y = jax.nn.gelu(y).astype(u.dtype)
    val, gate = jnp.split(y @ w_glu, 2, axis=-1)
    return val * jax.nn.sigmoid(gate)


def _hgrn2_mixer(x, w_in, lb, norm_g, w_out):
    f32 = jnp.float32
    bsz, L, _ = x.shape
    H, dk, dv, C = HG_HEADS, HG_DK, HG_DV, HG_CHUNK
    N = L // C
    q, f, i, g = jnp.split(x @ w_in, [H * dk, 2 * H * dk, 2 * H * dk + H * dv], axis=-1)
    lb = lb.astype(f32)
    f = f.astype(f32)
    log_f = jnp.logaddexp(jnp.log(lb), jnp.log1p(-lb) + jax.nn.log_sigmoid(f))
    k = (1.0 - lb) * jax.nn.sigmoid(-f)
    q = jax.nn.silu(q.astype(f32)) * dk ** -0.5

    def chunk(t, d):
        return t.reshape(bsz, N, C, H, d).transpose(1, 0, 3, 2, 4)

    q, k, log_f = chunk(q, dk), chunk(k, dk), chunk(log_f, dk)
    v = chunk(i.astype(f32), dv)
    b = jnp.cumsum(log_f, axis=3)
    b_last = b[:, :, :, -1:, :]
    q_t = q * jnp.exp(b)
    k_t = k * jnp.exp(-b)
    k_state = k * jnp.exp(b_last - b)
    causal = jnp.tril(jnp.ones((C, C), bool))
    attn = jnp.where(causal, jnp.einsum('nbhtd,nbhsd->nbhts', q_t, k_t), 0.0)
    o_intra = jnp.einsum('nbhts,nbhsv->nbhtv', attn, v)

    def step(S, inp):
        qc, kc, vc, dc = inp
        o = jnp.einsum('bhtd,bhdv->bhtv', qc, S)
        S = S * dc[..., None] + jnp.einsum('bhsd,bhsv->bhdv', kc, vc)
        return S, o

    S0 = jnp.zeros((bsz, H, dk, dv), f32)
    _, o_inter = lax.scan(step, S0, (q_t, k_state, v, jnp.exp(b_last[:, :, :, 0, :])))
    o = (o_intra + o_inter).transpose(1, 0, 3, 2, 4).reshape(bsz, L, H, dv)
    o = _rms(o, norm_g).reshape(bsz, L, H * dv) * jax.nn.silu(g.astype(f32))
    return o.astype(x.dtype) @ w_out


def _gdn_mixer(x, w_in, conv_w, a_log, dt_bias, norm_g, w_out):
    f32 = jnp.float32
    bsz, L, _ = x.shape
    Hk, Hv, dk, dv, C = GDN_QK_HEADS, GDN_V_HEADS, GDN_DK, GDN_DV, GDN_CHUNK
    N = L // C
    qkv, z, beta_raw, a_raw = jnp.split(
        x @ w_in, [GDN_QKV, GDN_QKV + Hv * dv, GDN_QKV + Hv * dv + Hv], axis=-1)
    qkv = jax.nn.silu(_causal_conv(qkv, conv_w).astype(f32))
    q, k, v = jnp.split(qkv, [Hk * dk, 2 * Hk * dk], axis=-1)
    q = jnp.repeat(_l2norm(q.reshape(bsz, L, Hk, dk)), Hv // Hk, axis=2) * dk ** -0.5
    k = jnp.repeat(_l2norm(k.reshape(bsz, L, Hk, dk)), Hv // Hk, axis=2)
    v = v.reshape(bsz, L, Hv, dv)
    beta = jax.nn.sigmoid(beta_raw.astype(f32))
    g = -jnp.exp(a_log.astype(f32)) * jax.nn.softplus(a_raw.astype(f32) + dt_bias.astype(f32))

    def chunk(t):
        return t.reshape(bsz, N, C, Hv, -1).transpose(1, 0, 3, 2, 4)

    q, k, v = chunk(q), chunk(k), chunk(v)
    beta = chunk(beta[..., None])[..., 0]
    b = jnp.cumsum(chunk(g[..., None])[..., 0], axis=-1)
    incl = jnp.tril(jnp.ones((C, C), bool))
    strict = jnp.tril(jnp.ones((C, C), bool), -1)
    decay = jnp.exp(jnp.where(incl, b[..., :, None] - b[..., None, :], -jnp.inf))
    kb = k * beta[..., None]
    m = jnp.where(strict, jnp.einsum('nbhtd,nbhsd->nbhts', kb, k) * decay, 0.0)
    a_mat = m + jnp.eye(C, dtype=f32)
    rhs = jnp.concatenate([v * beta[..., None], kb * jnp.exp(b)[..., None]], axis=-1)
    sol = lax.linalg.triangular_solve(a_mat, rhs, left_side=True, lower=True, unit_diagonal=True)
    u_val, w_cum = sol[..., :dv], sol[..., dv:]
    qk = jnp.einsum('nbhtd,nbhsd->nbhts', q, k) * decay
    q_dec = q * jnp.exp(b)[..., None]
    b_last = b[..., -1:]
    k_state = k * jnp.exp(b_last - b)[..., None]

    def step(S, inp):
        qd, qkc, uc, wc, ksc, dl = inp
        v_new = uc - jnp.einsum('bhtd,bhdv->bhtv', wc, S)
        o = jnp.einsum('bhtd,bhdv->bhtv', qd, S) + jnp.einsum('bhts,bhsv->bhtv', qkc, v_new)
        S = S * dl[..., None] + jnp.einsum('bhsd,bhsv->bhdv', ksc, v_new)
        return S, o

    S0 = jnp.zeros((bsz, Hv, dk, dv), f32)
    _, o = lax.scan(step, S0, (q_dec, qk, u_val, w_cum, k_state, jnp.exp(b_last)))
    o = o.transpose(1, 0, 3, 2, 4).reshape(bsz, L, Hv, dv)
    o = _rms(o, norm_g) * jax.nn.silu(z.astype(f32)).reshape(bsz, L, Hv, dv)
    return o.reshape(bsz, L, Hv * dv).astype(x.dtype) @ w_out


def _moba_mixer(x, w_in, rel_table, w_out):
    f32 = jnp.float32
    bsz, L, _ = x.shape
    H, dh, BS, QC = MOBA_HEADS, MOBA_DH, MOBA_BLOCK, MOBA_QCHUNK
    q, k, v = jnp.split((x @ w_in).astype(f32), 3, axis=-1)

    def heads(t):
        return t.reshape(bsz, L, H, dh).transpose(0, 2, 1, 3)

    q, k, v = heads(q), heads(k), heads(v)
    NB = -(-L // BS)
    pad = NB * BS - L
    k = jnp.pad(k, ((0, 0), (0, 0), (0, pad), (0, 0)))
    v = jnp.pad(v, ((0, 0), (0, 0), (0, pad), (0, 0)))
    kb = k.reshape(bsz, H, NB, BS, dh)
    vb = v.reshape(bsz, H, NB, BS, dh)
    k_mean = kb.mean(axis=3)
    q_blk = jnp.arange(L) // BS
    gate = jnp.einsum('bhld,bhnd->bhln', q, k_mean)
    past = jnp.arange(NB)[None, :] < q_blk[:, None]
    gate = jnp.where(past, gate, -1e30)
    topk = min(MOBA_TOPK, NB)
    _, top_idx = lax.top_k(gate, topk)
    top_valid = top_idx < q_blk[:, None]
    scale = dh ** -0.5
    table_t = rel_table.astype(f32).T
    h_ix = jnp.arange(H)[None, :, None, None]
    b_ix3 = jnp.arange(bsz)[:, None, None]
    h_ix3 = jnp.arange(H)[None, :, None]

    def chunk_fn(c):
        s = c * QC
        qc = lax.dynamic_slice_in_dim(q, s, QC, axis=2)
        idx = lax.dynamic_slice_in_dim(top_idx, s, QC, axis=2)
        valid = lax.dynamic_slice_in_dim(top_valid, s, QC, axis=2)
        qpos = s + jnp.arange(QC)
        flat = idx.reshape(bsz, H, QC * topk)
        k_sel = kb[b_ix3, h_ix3, flat].reshape(bsz, H, QC, topk * BS, dh)
        v_sel = vb[b_ix3, h_ix3, flat].reshape(bsz, H, QC, topk * BS, dh)
        kpos_sel = (idx[..., None] * BS + jnp.arange(BS)).reshape(bsz, H, QC, topk * BS)
        mask_sel = jnp.repeat(valid, BS, axis=-1)
        ob = (s // BS) * BS
        k_own = lax.dynamic_slice_in_dim(k, ob, BS, axis=2)
        v_own = lax.dynamic_slice_in_dim(v, ob, BS, axis=2)
        kpos_own = ob + jnp.arange(BS)
        mask_own = kpos_own[None, :] <= qpos[:, None]
        logit_sel = (jnp.einsum('bhqd,bhqkd->bhqk', qc, k_sel) * scale
                     + table_t[h_ix, _t5_bucket(qpos[:, None] - kpos_sel)])
        logit_own = (jnp.einsum('bhqd,bhkd->bhqk', qc, k_own) * scale
                     + table_t[:, _t5_bucket(qpos[:, None] - kpos_own[None, :])][None])
        logits = jnp.concatenate([jnp.where(mask_sel, logit_sel, -jnp.inf),
                                  jnp.where(mask_own, logit_own, -jnp.inf)], axis=-1)
        p = jax.nn.softmax(logits, axis=-1)
        return (jnp.einsum('bhqk,bhqkd->bhqd', p[..., :topk * BS], v_sel)
                + jnp.einsum('bhqk,bhkd->bhqd', p[..., topk * BS:], v_own))

    outs = lax.map(chunk_fn, jnp.arange(L // QC))
    o = outs.transpose(1, 0, 3, 2, 4).reshape(bsz, L, H * dh)
    return o.astype(x.dtype) @ w_out


def setup_inputs(seed: int = 0) -> dict:
    key = jax.random.key(seed)
    ks = jax.random.split(key, 32)
    f32 = jnp.float32

    def nrm(i, shape, sc):
        return sc * jax.random.normal(ks[i], shape, f32)

    def gain(i, shape):
        return 1.0 + 0.02 * jax.random.normal(ks[i], shape, f32)

    D, G, P, Gs = D_MODEL, S5_GROUPS, S5_STATE, S5_GROUP
    Hv = GDN_V_HEADS
    x = nrm(0, (BATCH, SEQ, D), 1.0)
    norm_g = gain(1, (DEPTH, 3, D))
    final_norm_g = gain(2, (D,))
    ffn_w_gate = nrm(3, (DEPTH, 2, D, D_FF), D ** -0.5)
    ffn_w_up = nrm(4, (DEPTH, 2, D, D_FF), D ** -0.5)
    ffn_w_down = nrm(5, (DEPTH, 2, D_FF, D), D_FF ** -0.5)
    s5_lam_re = -0.5 + nrm(6, (N_S5, G, P), 0.01)
    s5_lam_im = math.pi * jnp.arange(P, dtype=f32) + nrm(7, (N_S5, G, P), 0.01)
    s5_log_dt = jax.random.uniform(ks[8], (N_S5, G), f32, math.log(1e-3), math.log(1e-1))
    s5_b_re = nrm(9, (N_S5, G, P, Gs), (2 * Gs) ** -0.5)
    s5_b_im = nrm(10, (N_S5, G, P, Gs), (2 * Gs) ** -0.5)
    s5_c_re = nrm(11, (N_S5, G, Gs, P), P ** -0.5)
    s5_c_im = nrm(12, (N_S5, G, Gs, P), P ** -0.5)
    s5_d = nrm(13, (N_S5, D), 1.0)
    s5_w_glu = nrm(14, (N_S5, D, 2 * D), D ** -0.5)
    hg_w_in = nrm(15, (N_HGRN, D, HG_IN), D ** -0.5)
    hg_lower_bounds = nrm(16, (DEPTH, HG_HEADS * HG_DK), 0.1)
    hg_norm_g = gain(17, (N_HGRN, HG_DV))
    hg_w_out = nrm(18, (N_HGRN, HG_HEADS * HG_DV, D), (HG_HEADS * HG_DV) ** -0.5)
    gdn_w_in = nrm(19, (N_GDN, D, GDN_IN), D ** -0.5)
    gdn_conv_w = nrm(20, (N_GDN, GDN_CONV, GDN_QKV), GDN_CONV ** -0.5)
    gdn_a_log = jnp.log(jax.random.uniform(ks[21], (N_GDN, Hv), f32, 1.0, 16.0))
    dt = jnp.exp(jax.random.uniform(ks[22], (N_GDN, Hv), f32, math.log(1e-3), math.log(1e-1)))
    gdn_dt_bias = dt + jnp.log(-jnp.expm1(-dt))
    gdn_norm_g = gain(23, (N_GDN, GDN_DV))
    gdn_w_out = nrm(24, (N_GDN, Hv * GDN_DV, D), (Hv * GDN_DV) ** -0.5)
    moba_w_in = nrm(25, (N_MOBA, D, 3 * MOBA_HEADS * MOBA_DH), D ** -0.5)
    moba_w_out = nrm(26, (N_MOBA, MOBA_HEADS * MOBA_DH, D), (MOBA_HEADS * MOBA_DH) ** -0.5)
    rel_bias_table = nrm(27, (REL_BUCKETS, MOBA_HEADS), 0.5)
    return {'x': x, 'norm_g': norm_g, 'final_norm_g': final_norm_g,
            'ffn_w_gate': ffn_w_gate, 'ffn_w_up': ffn_w_up, 'ffn_w_down': ffn_w_down,
            's5_lam_re': s5_lam_re, 's5_lam_im': s5_lam_im, 's5_log_dt': s5_log_dt,
            's5_b_re': s5_b_re, 's5_b_im': s5_b_im, 's5_c_re': s5_c_re, 's5_c_im': s5_c_im,
            's5_d': s5_d, 's5_w_glu': s5_w_glu,
            'hg_w_in': hg_w_in, 'hg_lower_bounds': hg_lower_bounds, 'hg_norm_g': hg_norm_g,
            'hg_w_out': hg_w_out,
            'gdn_w_in': gdn_w_in, 'gdn_conv_w': gdn_conv_w, 'gdn_a_log': gdn_a_log,
            'gdn_dt_bias': gdn_dt_bias, 'gdn_norm_g': gdn_norm_g, 'gdn_w_out': gdn_w_out,
            'moba_w_in': moba_w_in, 'moba_w_out': moba_w_out, 'rel_bias_table': rel_bias_table}


def reference(x, norm_g, final_norm_g, ffn_w_gate, ffn_w_up, ffn_w_down,
              s5_lam_re, s5_lam_im, s5_log_dt, s5_b_re, s5_b_im, s5_c_re, s5_c_im, s5_d, s5_w_glu,
              hg_w_in, hg_lower_bounds, hg_norm_g, hg_w_out,
              gdn_w_in, gdn_conv_w, gdn_a_log, gdn_dt_bias, gdn_norm_g, gdn_w_out,
              moba_w_in, moba_w_out, rel_bias_table):
    lbs = jnp.cumsum(jax.nn.softmax(hg_lower_bounds.astype(jnp.float32), axis=0), axis=0)
    lbs = lbs - lbs[0]
    for i in range(DEPTH):
        m, j = i % N_MIXERS, i // N_MIXERS
        x = x + 0.5 * _swiglu(_rms(x, norm_g[i, 0]), ffn_w_gate[i, 0], ffn_w_up[i, 0], ffn_w_down[i, 0])
        h = _rms(x, norm_g[i, 1])
        if m == 0:
            y = _s5_mixer(h, s5_lam_re[j], s5_lam_im[j], s5_log_dt[j], s5_b_re[j], s5_b_im[j],
                          s5_c_re[j], s5_c_im[j], s5_d[j], s5_w_glu[j])
        elif m == 1:
            y = _hgrn2_mixer(h, hg_w_in[j], lbs[i], hg_norm_g[j], hg_w_out[j])
        elif m == 2:
            y = _gdn_mixer(h, gdn_w_in[j], gdn_conv_w[j], gdn_a_log[j], gdn_dt_bias[j],
                           gdn_norm_g[j], gdn_w_out[j])
        else:
            y = _moba_mixer(h, moba_w_in[j], rel_bias_table, moba_w_out[j])
        x = x + y.astype(x.dtype)
        x = x + 0.5 * _swiglu(_rms(x, norm_g[i, 2]), ffn_w_gate[i, 1], ffn_w_up[i, 1], ffn_w_down[i, 1])
    return _rms(x, final_norm_g)
```

```python
import math
from contextlib import ExitStack
import numpy as np
import ml_dtypes
import concourse.bass as bass
import concourse.mybir as mybir
from concourse.bass_utils import run_bass_kernel_spmd

F32 = mybir.dt.float32
BF16 = mybir.dt.bfloat16
AF = mybir.ActivationFunctionType
ALU = mybir.AluOpType
AX = mybir.AxisListType

D = 1024
DFF = 2816
L = 4096
NB = 4
TOK = 2048
EPS = 1e-6
SAME_ENG_SYNC = True


def I(name, **kw):
    return (name, kw)


class Rec:
    def __init__(self, nc):
        self.nc = nc
        self.ops = []
        self.count = {}
        self.last_w = {}
        self.readers = {}
        self.waited = {}

    def _deps(self, eng, reads, writes):
        deps = {}

        def need(tok):
            if tok is None:
                return
            s, v = tok
            if s.startswith('dma_'):
                v = self.count[s]
            if s == eng and (eng == 'pe' or not SAME_ENG_SYNC):
                return
            if deps.get(s, 0) < v:
                deps[s] = v

        for k in reads:
            need(self.last_w.get(k))
        for k in writes:
            need(self.last_w.get(k))
            for t in self.readers.get(k, ()):
                need(t)
        out = []
        for s, v in deps.items():
            if self.waited.get((eng, s), 0) < v:
                self.waited[(eng, s)] = v
                out.append((s, v))
        return out

    def add(self, eng, fns, reads=(), writes=(), dma=None, inc=16):
        if not isinstance(fns, list):
            fns = [fns]
        waits = self._deps(eng, reads, writes)
        if dma is not None:
            sem = 'dma_' + dma
            self.count[sem] = self.count.get(sem, 0) + inc
        else:
            sem = eng
            self.count[sem] = self.count.get(sem, 0) + 1
        tok = (sem, self.count[sem])
        for k in reads:
            self.readers.setdefault(k, []).append(tok)
        for k in writes:
            self.last_w[k] = tok
            self.readers[k] = []
        self.ops.append((eng, list(fns), waits, sem, (inc if dma is not None else 0)))
        return tok

    def flush(self, final=False):
        nc = self.nc
        if not hasattr(self, 'sems'):
            self.sems = {}
            self.semstack = ExitStack()
            self.barrier_tot = {}
        for n in sorted(self.count.keys()):
            if n not in self.sems:
                self.sems[n] = self.semstack.enter_context(nc.semaphore('s_' + n))
        sems = self.sems
        barrier = sorted(self.barrier_tot.items())
        totals = sorted(self.count.items())
        per = {e: [] for e in ('pe', 'dve', 'act', 'pool', 'sp')}
        for op in self.ops:
            per[op[0]].append(op)
        with nc.Block() as block:
            def run(e, ops, fin):
                for (s, v) in barrier:
                    e.wait_ge(sems[s], v)
                for (_, fns, waits, sem, is_dma) in ops:
                    for (s, v) in waits:
                        e.wait_ge(sems[s], v)
                    ins = None
                    for f in fns:
                        ins = getattr(e, f[0])(**f[1]) if isinstance(f, tuple) else f(e)
                    ins.then_inc(sems[sem], is_dma if is_dma else 1)
                if fin:
                    for (s, v) in totals:
                        e.wait_ge(sems[s], v)

            @block.tensor
            def _(e):
                run(e, per['pe'], False)

            @block.vector
            def _(e):
                run(e, per['dve'], False)

            @block.scalar
            def _(e):
                run(e, per['act'], False)

            @block.gpsimd
            def _(e):
                run(e, per['pool'], False)

            @block.sync
            def _(e):
                run(e, per['sp'], final)
        self.barrier_tot = dict(self.count)
        self.ops = []
        self.last_w = {}
        self.readers = {}
        for e in per:
            for sname, v in self.count.items():
                self.waited[(e, sname)] = v

    def emit(self, final_waits):
        self.flush(final=True)


class Prog:
    def __init__(self):
        self.nc = bass.Bass("TRN2", target_bir_lowering=False)
        self.rec = Rec(self.nc)
        self.ext = {}

    def scratch(self, name, shape, dt):
        return self.nc.dram_tensor(name, list(shape), dt)


class Ctx:
    def __init__(self, P=None, prefix='', bind=None):
        self.P = P or Prog()
        self.standalone = P is None
        self.nc = self.P.nc
        self.rec = self.P.rec
        self.prefix = prefix
        self.bind = bind or {}
        self.st = ExitStack()
        self.n = 0

    def sb(self, shape, dt, name=None):
        self.n += 1
        return self.st.enter_context(self.nc.sbuf_tensor(f"sb_{self.prefix}{name or self.n}", list(shape), dt))

    def ps(self, shape, dt=F32, name=None):
        self.n += 1
        return self.st.enter_context(self.nc.psum_tensor(f"ps_{self.prefix}{name or self.n}", list(shape), dt))

    def dram_in(self, name, shape, dt=F32):
        if name in self.bind:
            return self.bind[name]
        self.P.ext[self.prefix + name] = (self.prefix, name)
        return self.nc.dram_tensor(self.prefix + name, list(shape), dt, kind="ExternalInput").ap()

    def dram_out(self, name, shape, dt=F32):
        if name in self.bind:
            return self.bind[name]
        return self.nc.dram_tensor(self.prefix + name, list(shape), dt, kind="ExternalOutput").ap()

    def scratch(self, name, shape, dt):
        return self.nc.dram_tensor(self.prefix + name, list(shape), dt)

    def finish(self):
        self.rec.flush(final=self.standalone)
        self.st.close()
        return self.nc


class Ring:
    def __init__(self, C, n, shape, dt, name, psum=False):
        self.t = [(C.ps(shape, dt, f"{name}{i}") if psum else C.sb(shape, dt, f"{name}{i}")) for i in range(n)]
        self.k = [f"{name}{i}" for i in range(n)]
        self.i = 0

    def next(self):
        j = self.i % len(self.t)
        self.i += 1
        return self.t[j], self.k[j]


def build_token_phase(cfg, P=None, prefix='', bind=None):
    C = Ctx(P, prefix, bind)
    nc, R = C.nc, C.rec
    mix, mix_k, n_ffn, tail = cfg['mix'], cfg.get('mix_k', 0), cfg['n_ffn'], cfg['tail']
    NT = TOK // 512

    x_d = C.dram_in("xT", [D, TOK])
    if mix:
        o_d = C.dram_in("oT", [mix_k, L], BF16)
        hfm_d = C.dram_in("hfm", [128, 2])
        wm_d = C.dram_in("w_mix", [mix_k, 2048 if mix == 'glu' else 1024])
    wg_d = [C.dram_in(f"wg{f}", [D, DFF]) for f in range(n_ffn)]
    wu_d = [C.dram_in(f"wu{f}", [D, DFF]) for f in range(n_ffn)]
    wd_d = [C.dram_in(f"wd{f}", [DFF, D]) for f in range(n_ffn)]
    ng_d = [C.dram_in(f"ng{f}", [128, 8]) for f in range(n_ffn)]
    if tail != 'x':
        ngt_d = C.dram_in("ngt", [128, 8])
    if tail == 'final':
        out_d = C.dram_out("out", [D, TOK])
    else:
        xo_d = C.dram_out("xo", [D, TOK])
        if tail == 'h':
            ho_d = C.dram_out("ho", [D, TOK], BF16)

    xT = C.sb([128, 8, TOK], F32, "xT")
    hT = C.sb([128, 8, TOK], BF16, "hT")
    ones = C.sb([128, 128], BF16, "ones")
    ng_s = [C.sb([128, 8], F32, f"ng_s{f}") for f in range(n_ffn)]
    if tail != 'x':
        ngt_s = C.sb([128, 8], F32, "ngt_s")
    sq = [C.sb([128, 512], BF16, f"sq{i}") for i in range(4)]
    rstd = C.sb([128, 512], F32, "rstd")
    GS = 2
    NG = DFF // 128 // GS
    wgs = [C.sb([128, 8, GS * 128], BF16, f"wgs{i}") for i in range(2)]
    wus = [C.sb([128, 8, GS * 128], BF16, f"wus{i}") for i in range(2)]
    wds = [C.sb([128, GS, D], BF16, f"wds{i}") for i in range(2)]
    sg = [C.sb([128, 512], F32, f"sg{i}") for i in range(2)]
    act = [C.sb([128, 512], BF16, f"act{i}") for i in range(4)]
    pg = [C.ps([128, 512], F32, f"pg{i}") for i in range(2)]
    pu = [C.ps([128, 512], F32, f"pu{i}") for i in range(2)]
    pd = [C.ps([128, 512], F32, f"pd{i}") for i in range(4)]

    R.add('pool', I('memset', ap=ones[:], constant=1.0), writes=['ones'])
    for kc in range(8):
        R.add('sp', I('dma_start', out=xT[:, kc, :], in_=x_d[kc * 128:(kc + 1) * 128, :]),
              writes=[f'x{kc}_{t}' for t in range(NT)], dma=f'x{kc}')
    for f in range(n_ffn):
        R.add('sp', I('dma_start', out=ng_s[f][:], in_=ng_d[f]), writes=[f'ng{f}'], dma='ng')
    if tail != 'x':
        R.add('sp', I('dma_start', out=ngt_s[:], in_=ngt_d), writes=['ngt'], dma='ng')

    cnt = {'sq': 0, 'pd': 0, 'gu': 0, 'act': 0, 'w': 0}

    def rms_to_h(t, g_tile, gkey, out_tile=None):
        ts = slice(t * 512, (t + 1) * 512)
        b = cnt['pd'] % 4
        cnt['pd'] += 1
        for kc in range(8):
            i = cnt['sq'] % 4
            cnt['sq'] += 1
            R.add('act', I('activation', out=sq[i][:], in_=xT[:, kc, ts], func=AF.Square),
                  reads=[f'x{kc}_{t}'], writes=[f'sq{i}'])
            R.add('pe', I('matmul', out=pd[b][:], lhsT=ones[:], rhs=sq[i][:], start=(kc == 0), stop=(kc == 7)),
                  reads=[f'sq{i}', 'ones'], writes=[f'pd{b}'])
        R.add('act', I('activation', out=rstd[:], in_=pd[b][:], func=AF.Ln, scale=1.0 / D, bias=EPS), writes=[f'pd{b}', 'rstd'])
        R.add('act', I('activation', out=rstd[:], in_=rstd[:], func=AF.Exp, scale=-0.5), writes=['rstd'])
        for kc in range(8):
            dst = (hT if out_tile is None else out_tile)[:, kc, ts]
            R.add('dve', I('scalar_tensor_tensor', out=dst, in0=xT[:, kc, ts], scalar=g_tile[:, kc:kc + 1],
                           in1=rstd[:], op0=ALU.mult, op1=ALU.mult),
                  reads=[f'x{kc}_{t}', 'rstd', gkey], writes=[f'h{kc}_{t}' if out_tile is None else f'x{kc}_{t}'])

    if mix:
        nk = mix_k // 128
        ncol = 2048 if mix == 'glu' else 1024
        oTr = Ring(C, 2 if nk <= 8 else 1, [128, nk, 512], BF16, "oTr")
        oTr2 = Ring(C, 1, [128, nk, 512], BF16, "oTrB")
        hfm = C.sb([128, 2], F32, "hfm")
        R.add('sp', I('dma_start', out=hfm[:], in_=hfm_d), writes=['hfm'], dma='ng')
        wm = C.sb([128, nk, ncol], BF16, "wm")
        for k in range(nk):
            R.add('pool', I('dma_start', out=wm[:, k, :], in_=wm_d[k * 128:(k + 1) * 128, :]), writes=[f'wm{k}'], dma='wm')
        wkeys = [f'wm{k}' for k in range(nk)]
        for t in range(NT):
            ts = slice(t * 512, (t + 1) * 512)
            oT, okey = oTr.next()
            oB, okeyB = oTr2.next()
            for k in range(nk):
                rr = orow(k, nk)
                R.add('sp', I('dma_start', out=oT[:, k, :], in_=o_d[rr:rr + 128, ts]), writes=[okey], dma='o' + okey)
                R.add('sp', I('dma_start', out=oB[:, k, :], in_=o_d[rr:rr + 128, TOK + t * 512:TOK + (t + 1) * 512]),
                      writes=[okeyB], dma='o' + okeyB)
            R.add('dve', I('tensor_scalar', out=oT[:], in0=oT[:], scalar1=hfm[:, 0:1], scalar2=None, op0=ALU.mult),
                  reads=['hfm'], writes=[okey])
            R.add('dve', I('scalar_tensor_tensor', out=oT[:], in0=oB[:], scalar=hfm[:, 1:2], in1=oT[:], op0=ALU.mult, op1=ALU.add),
                  reads=['hfm', okeyB], writes=[okey])
            okeys = wkeys + [okey]
            for c in range(8):
                b = cnt['pd'] % 4
                cnt['pd'] += 1
                R.add('pe', [I('matmul', out=pd[b][:], lhsT=wm[:, k, c * 128:(c + 1) * 128], rhs=oT[:, k, :],
                               start=(k == 0), stop=(k == nk - 1)) for k in range(nk)],
                      reads=okeys, writes=[f'pd{b}'])
                if mix == 'lin':
                    R.add('dve', I('tensor_tensor', out=xT[:, c, ts], in0=pd[b][:], in1=xT[:, c, ts], op=ALU.add),
                          reads=[f'x{c}_{t}'], writes=[f'pd{b}', f'x{c}_{t}'])
                else:
                    b2 = cnt['pd'] % 4
                    cnt['pd'] += 1
                    R.add('pe', [I('matmul', out=pd[b2][:], lhsT=wm[:, k, 1024 + c * 128:1024 + (c + 1) * 128], rhs=oT[:, k, :],
                                   start=(k == 0), stop=(k == nk - 1)) for k in range(nk)],
                          reads=okeys, writes=[f'pd{b2}'])
                    i = cnt['gu'] % 2
                    cnt['gu'] += 1
                    R.add('act', I('activation', out=sg[i][:], in_=pd[b2][:], func=AF.Sigmoid),
                          writes=[f'pd{b2}', f'sg{i}'])
                    R.add('dve', I('tensor_tensor', out=sg[i][:], in0=pd[b][:], in1=sg[i][:], op=ALU.mult),
                          writes=[f'pd{b}', f'sg{i}'])
                    R.add('dve', I('tensor_tensor', out=xT[:, c, ts], in0=sg[i][:], in1=xT[:, c, ts], op=ALU.add),
                          reads=[f'sg{i}', f'x{c}_{t}'], writes=[f'x{c}_{t}'])

    for f in range(n_ffn):
        for t in range(NT):
            rms_to_h(t, ng_s[f], f'ng{f}')
        for g in range(NG):
            wi = cnt['w'] % 2
            cnt['w'] += 1
            cs = slice(g * GS * 128, (g + 1) * GS * 128)
            R.add('pool', I('dma_start', out=wgs[wi][:], in_=wg_d[f][:, cs].rearrange("(kc p) c -> p kc c", p=128)),
                  writes=[f'wg{wi}'], dma=f'wg{wi}')
            R.add('pool', I('dma_start', out=wus[wi][:], in_=wu_d[f][:, cs].rearrange("(kc p) c -> p kc c", p=128)),
                  writes=[f'wu{wi}'], dma=f'wu{wi}')
            R.add('pool', I('dma_start', out=wds[wi][:], in_=wd_d[f][cs, :].rearrange("(j p) c -> p j c", p=128)),
                  writes=[f'wd{wi}'], dma=f'wd{wi}')
            for t in range(NT):
                ts = slice(t * 512, (t + 1) * 512)
                hk = [f'h{kc}_{t}' for kc in range(8)]
                acts = []
                for j in range(GS):
                    gi = cnt['gu'] % 2
                    cnt['gu'] += 1
                    R.add('pe', [I('matmul', out=pg[gi][:], lhsT=wgs[wi][:, kc, j * 128:(j + 1) * 128], rhs=hT[:, kc, ts],
                                   start=(kc == 0), stop=(kc == 7)) for kc in range(8)],
                          reads=hk + [f'wg{wi}'], writes=[f'pg{gi}'])
                    R.add('pe', [I('matmul', out=pu[gi][:], lhsT=wus[wi][:, kc, j * 128:(j + 1) * 128], rhs=hT[:, kc, ts],
                                   start=(kc == 0), stop=(kc == 7)) for kc in range(8)],
                          reads=hk + [f'wu{wi}'], writes=[f'pu{gi}'])
                    R.add('act', I('activation', out=sg[gi][:], in_=pg[gi][:], func=AF.Silu),
                          reads=[f'pg{gi}'], writes=[f'sg{gi}'])
                    ai = cnt['act'] % 4
                    cnt['act'] += 1
                    R.add('dve', I('tensor_tensor', out=act[ai][:], in0=pu[gi][:], in1=sg[gi][:], op=ALU.mult),
                          reads=[f'pu{gi}', f'sg{gi}'], writes=[f'act{ai}'])
                    acts.append(ai)
                for c in range(8):
                    b = cnt['pd'] % 4
                    cnt['pd'] += 1
                    R.add('pe', [I('matmul', out=pd[b][:], lhsT=wds[wi][:, j, c * 128:(c + 1) * 128], rhs=act[acts[j]][:],
                                   start=(j == 0), stop=(j == GS - 1)) for j in range(GS)],
                          reads=[f'act{a}' for a in acts] + [f'wd{wi}'], writes=[f'pd{b}'])
                    R.add('dve', I('scalar_tensor_tensor', out=xT[:, c, ts], in0=pd[b][:], scalar=0.5, in1=xT[:, c, ts],
                                   op0=ALU.mult, op1=ALU.add),
                          reads=[f'pd{b}', f'x{c}_{t}'], writes=[f'x{c}_{t}'])

    fin = []
    if tail in ('h', 'x'):
        for kc in range(8):
            tok = R.add('sp', I('dma_start', out=xo_d[kc * 128:(kc + 1) * 128, :], in_=xT[:, kc, :]),
                        reads=[f'x{kc}_{t}' for t in range(NT)], dma='xo')
        fin.append(tok)
    if tail == 'h':
        for t in range(NT):
            rms_to_h(t, ngt_s, 'ngt')
        for kc in range(8):
            tok = R.add('sp', I('dma_start', out=ho_d[kc * 128:(kc + 1) * 128, :], in_=hT[:, kc, :]),
                        reads=[f'h{kc}_{t}' for t in range(NT)], dma='ho')
        fin.append(tok)
    if tail == 'final':
        for t in range(NT):
            rms_to_h(t, ngt_s, 'ngt', out_tile=xT)
        for kc in range(8):
            tok = R.add('sp', I('dma_start', out=out_d[kc * 128:(kc + 1) * 128, :], in_=xT[:, kc, :]),
                        reads=[f'x{kc}_{t}' for t in range(NT)], dma='fo')
        fin.append(tok)
    return C.finish()


def hrow(r, kc):
    return (kc // 4) * 1024 + r * 512 + (kc % 4) * 128


def orow(k, nk):
    half = nk // 2
    r, lk = k // half, k % half
    return (lk // 2) * 512 + r * 256 + (lk % 2) * 128


def h_gather_layout(h0, h1):
    a = np.stack([h0.reshape(2, 512, -1), h1.reshape(2, 512, -1)], axis=1)
    return np.ascontiguousarray(a.reshape(2048, -1))


def o_gather_layout(o0, o1):
    half = o0.shape[0]
    a = np.stack([o0.reshape(half // 256, 256, -1), o1.reshape(half // 256, 256, -1)], axis=1)
    return np.ascontiguousarray(a.reshape(2 * half, -1))


def hfm_arr(hf):
    a = np.zeros((128, 2), np.float32)
    a[:, hf] = 1.0
    return a


def fm(v):
    return np.ascontiguousarray(np.asarray(v, np.float32).reshape(8, 128).T)


def run(nc, in_maps):
    res = run_bass_kernel_spmd(nc, in_maps, core_ids=list(range(8)))
    return res.results


MAGIC = 12582912.0
TWO_PI = 2.0 * math.pi
PI_SAFE = 3.1415925


def emit_sin(R, eng, out, okey, ang, akey, tmp, tkey, shift=0.0):
    R.add(eng, I('tensor_scalar', out=tmp, in0=ang, scalar1=shift, scalar2=1.0 / TWO_PI, op0=ALU.add, op1=ALU.mult),
          reads=[akey], writes=[tkey])
    R.add(eng, I('tensor_scalar', out=tmp, in0=tmp, scalar1=MAGIC, scalar2=-MAGIC, op0=ALU.add, op1=ALU.add),
          reads=[tkey], writes=[tkey])
    R.add(eng, I('scalar_tensor_tensor', out=tmp, in0=tmp, scalar=-TWO_PI, in1=ang, op0=ALU.mult, op1=ALU.add),
          reads=[tkey, akey], writes=[tkey])
    if shift != 0.0:
        R.add(eng, I('tensor_scalar', out=tmp, in0=tmp, scalar1=shift, scalar2=None, op0=ALU.add),
              reads=[tkey], writes=[tkey])
    R.add(eng, I('tensor_scalar', out=tmp, in0=tmp, scalar1=-PI_SAFE, scalar2=PI_SAFE, op0=ALU.max, op1=ALU.min),
          reads=[tkey], writes=[tkey])
    R.add('act', I('activation', out=out, in_=tmp, func=AF.Sin), reads=[tkey], writes=[okey])


def build_s5(P=None, prefix='', bind=None):
    C = Ctx(P, prefix, bind)
    nc, R = C.nc, C.rec
    T = 512
    NCH = L // T
    u_d = C.dram_in("hT", [2 * D, TOK], BF16)
    hfm_d = C.dram_in("hfm", [128, 2])
    lre_d = C.dram_in("lam_re", [128, 16])
    lim_d = C.dram_in("lam_im", [128, 16])
    ldt_d = C.dram_in("log_dt", [128, 16])
    bre_d = C.dram_in("bre", [16, 128, 128])
    bim_d = C.dram_in("bim", [16, 128, 128])
    cre_d = C.dram_in("cre", [16, 128, 128])
    cim_d = C.dram_in("cim", [16, 128, 128])
    d_d = C.dram_in("dsk", [128, 4])
    id_d = C.dram_in("ident", [128, 128])
    y_d = C.dram_out("ygT", [512, L], BF16)

    uT = C.sb([128, 4, L], F32, "uT")
    ident = C.sb([128, 128], F32, "ident")
    dsk = C.sb([128, 4], F32, "dsk")
    P = {}
    for n in ('lre', 'lim', 'ldt', 'dt', 'a', 'th', 'r', 'sn', 'cs', 'lbr', 'lbi', 'den', 'cr', 'ci', 'thT', 'snT', 'csT',
              'tmp', 'tmp2'):
        P[n] = C.sb([128, 16], F32, "p_" + n)
    bre = C.sb([128, 16, 128], F32, "bre")
    bim = C.sb([128, 16, 128], F32, "bim")
    creb = C.sb([128, 16, 128], F32, "creb")
    cimb = C.sb([128, 16, 128], F32, "cimb")
    bbre = C.sb([128, 16, 128], F32, "bbre")
    bbim = C.sb([128, 16, 128], F32, "bbim")
    io = C.sb([128, T], F32, "io")
    cosT = [C.sb([128, T], F32, f"cost{q}") for q in range(4)]
    sinT = [C.sb([128, T], F32, f"sint{q}") for q in range(4)]
    st_re = [C.sb([128, 1], F32, f"stre{q}") for q in range(16)]
    st_im = [C.sb([128, 1], F32, f"stim{q}") for q in range(16)]
    sc = Ring(C, 14, [128, T], F32, "sc")
    gg = Ring(C, 6, [128, T], F32, "gg")
    hb = Ring(C, 4, [128, T], F32, "hb")
    yb = Ring(C, 2, [128, T], F32, "yb")
    yo = Ring(C, 2, [128, T], BF16, "yo")
    c1 = Ring(C, 4, [128, 1], F32, "c1")
    pb = Ring(C, 4, [128, T], F32, "pb", psum=True)
    py = Ring(C, 2, [128, T], F32, "py", psum=True)
    pt = Ring(C, 2, [128, 512], F32, "pt", psum=True)

    hfm = C.sb([128, 2], F32, "hfm")
    uhi = Ring(C, 2, [128, L], BF16, "uhi")
    R.add('sp', I('dma_start', out=hfm[:], in_=hfm_d), writes=['hfm'], dma='c')
    for k in range(4):
        hi, khi = uhi.next()
        for r in range(2):
            R.add('pool', I('dma_start', out=uT[:, k, r * 2048:(r + 1) * 2048], in_=u_d[hrow(r, k):hrow(r, k) + 128, :]),
                  writes=[f'u{k}'], dma=f'u{k}')
            R.add('sp', I('dma_start', out=hi[:, r * 2048:(r + 1) * 2048], in_=u_d[hrow(r, 4 + k):hrow(r, 4 + k) + 128, :]),
                  writes=[khi], dma=f'uh{k}')
        R.add('dve', I('tensor_scalar', out=uT[:, k, :], in0=uT[:, k, :], scalar1=hfm[:, 0:1], scalar2=None, op0=ALU.mult),
              reads=['hfm'], writes=[f'u{k}'])
        R.add('dve', I('scalar_tensor_tensor', out=uT[:, k, :], in0=hi[:], scalar=hfm[:, 1:2], in1=uT[:, k, :], op0=ALU.mult, op1=ALU.add),
              reads=['hfm', khi], writes=[f'u{k}'])
    R.add('sp', I('dma_start', out=ident[:], in_=id_d), writes=['ident'], dma='c')
    R.add('sp', I('dma_start', out=dsk[:], in_=d_d), writes=['dsk'], dma='c')
    R.add('sp', I('dma_start', out=P['lre'][:], in_=lre_d), writes=['lre'], dma='c')
    R.add('sp', I('dma_start', out=P['lim'][:], in_=lim_d), writes=['lim'], dma='c')
    R.add('sp', I('dma_start', out=P['ldt'][:], in_=ldt_d), writes=['ldt'], dma='c')
    R.add('sp', I('dma_start', out=bre[:], in_=bre_d.rearrange("t p c -> p t c")), writes=['bre'], dma='c')
    R.add('sp', I('dma_start', out=bim[:], in_=bim_d.rearrange("t p c -> p t c")), writes=['bim'], dma='c')
    R.add('pool', I('dma_start', out=creb[:], in_=cre_d.rearrange("t p c -> p t c")), writes=['creb'], dma='cb')
    R.add('pool', I('dma_start', out=cimb[:], in_=cim_d.rearrange("t p c -> p t c")), writes=['cimb'], dma='cb')
    R.add('pool', I('iota', out=io[:], pattern=[[1, T]], base=0, channel_multiplier=0, allow_small_or_imprecise_dtypes=True),
          writes=['io'])

    def v(n):
        return P[n][:]

    def op2(out, a, b, op, eng='dve'):
        R.add(eng, I('tensor_tensor', out=v(out), in0=v(a), in1=v(b), op=op), reads=[a, b], writes=[out])

    R.add('act', I('activation', out=v('dt'), in_=v('ldt'), func=AF.Exp), reads=['ldt'], writes=['dt'])
    R.add('dve', I('tensor_scalar', out=v('lre'), in0=v('lre'), scalar1=-1e-4, scalar2=None, op0=ALU.min),
          reads=['lre'], writes=['lre'])
    op2('a', 'lre', 'dt', ALU.mult)
    op2('th', 'lim', 'dt', ALU.mult)
    R.add('act', I('activation', out=v('r'), in_=v('a'), func=AF.Exp), reads=['a'], writes=['r'])
    emit_sin(R, 'dve', v('sn'), 'sn', v('th'), 'th', v('tmp'), 'tmp')
    emit_sin(R, 'dve', v('cs'), 'cs', v('th'), 'th', v('tmp2'), 'tmp2', shift=math.pi / 2)
    op2('lbr', 'r', 'cs', ALU.mult)
    op2('lbi', 'r', 'sn', ALU.mult)
    R.add('dve', I('tensor_scalar', out=v('thT'), in0=v('th'), scalar1=float(T), scalar2=None, op0=ALU.mult),
          reads=['th'], writes=['thT'])
    emit_sin(R, 'dve', v('snT'), 'snT', v('thT'), 'thT', v('tmp'), 'tmp')
    emit_sin(R, 'dve', v('csT'), 'csT', v('thT'), 'thT', v('tmp2'), 'tmp2', shift=math.pi / 2)
    op2('den', 'lre', 'lre', ALU.mult)
    op2('tmp', 'lim', 'lim', ALU.mult)
    op2('den', 'den', 'tmp', ALU.add)
    R.add('dve', I('reciprocal', out=v('den'), in_=v('den')), reads=['den'], writes=['den'])
    R.add('dve', I('tensor_scalar', out=v('tmp'), in0=v('lbr'), scalar1=-1.0, scalar2=None, op0=ALU.add),
          reads=['lbr'], writes=['tmp'])
    op2('cr', 'tmp', 'lre', ALU.mult)
    op2('tmp2', 'lbi', 'lim', ALU.mult)
    op2('cr', 'cr', 'tmp2', ALU.add)
    op2('cr', 'cr', 'den', ALU.mult)
    op2('ci', 'lbi', 'lre', ALU.mult)
    op2('tmp2', 'tmp', 'lim', ALU.mult)
    op2('ci', 'ci', 'tmp2', ALU.subtract)
    op2('ci', 'ci', 'den', ALU.mult)

    for tl in range(16):
        t1, k1 = sc.next()
        t2, k2 = sc.next()
        R.add('dve', I('tensor_scalar', out=t1[:, 0:128], in0=bim[:, tl, :], scalar1=P['ci'][:, tl:tl + 1], scalar2=None,
                       op0=ALU.mult), reads=['bim', 'ci'], writes=[k1])
        R.add('dve', I('scalar_tensor_tensor', out=t1[:, 0:128], in0=bre[:, tl, :], scalar=P['cr'][:, tl:tl + 1],
                       in1=t1[:, 0:128], op0=ALU.mult, op1=ALU.subtract), reads=['bre', 'cr', k1], writes=[k1])
        R.add('dve', I('tensor_scalar', out=t2[:, 0:128], in0=bre[:, tl, :], scalar1=P['ci'][:, tl:tl + 1], scalar2=None,
                       op0=ALU.mult), reads=['bre', 'ci'], writes=[k2])
        R.add('dve', I('scalar_tensor_tensor', out=t2[:, 0:128], in0=bim[:, tl, :], scalar=P['cr'][:, tl:tl + 1],
                       in1=t2[:, 0:128], op0=ALU.mult, op1=ALU.add), reads=['bim', 'cr', k2], writes=[k2])
        for (src, ks, dst, kd) in ((t1, k1, bbre, 'bbre'), (t2, k2, bbim, 'bbim')):
            p, kp = pt.next()
            R.add('pe', I('transpose', out=p[:, 0:128], in_=src[:, 0:128], identity=ident[:]), reads=[ks, 'ident'], writes=[kp])
            R.add('act', I('activation', out=dst[:, tl, :], in_=p[:, 0:128], func=AF.Copy), writes=[kp, f'{kd}{tl}'])

    fin = []
    for cq in range(4):
        for q in range(4):
            tl = cq * 4 + q
            a_t, a_k = sc.next()
            R.add('dve', I('tensor_scalar', out=a_t[:], in0=io[:], scalar1=P['th'][:, tl:tl + 1], scalar2=None, op0=ALU.mult),
                  reads=['io', 'th'], writes=[a_k])
            t_t, t_k = sc.next()
            emit_sin(R, 'dve', sinT[q][:], f'sint{q}', a_t[:], a_k, t_t[:], t_k)
            t_t, t_k = sc.next()
            emit_sin(R, 'dve', cosT[q][:], f'cost{q}', a_t[:], a_k, t_t[:], t_k, shift=math.pi / 2)
            R.add('pool', I('memset', ap=st_re[tl][:], constant=0.0), writes=[f'stre{tl}'])
            R.add('pool', I('memset', ap=st_im[tl][:], constant=0.0), writes=[f'stim{tl}'])
        items = [(c, q) for c in range(NCH) for q in range(4)]
        st8 = {}

        def front_a(i):
            c, q = items[i]
            ts = slice(c * T, (c + 1) * T)
            tl = cq * 4 + q
            cs_, sn_ = cosT[q][:], sinT[q][:]
            ck, sk = f'cost{q}', f'sint{q}'
            p_re, k_pre = pb.next()
            p_im, k_pim = pb.next()
            R.add('pe', I('matmul', out=p_re[:], lhsT=bbre[:, tl, :], rhs=uT[:, cq, ts], start=True, stop=True),
                  reads=[f'bbre{tl}', f'u{cq}'], writes=[k_pre])
            R.add('pe', I('matmul', out=p_im[:], lhsT=bbim[:, tl, :], rhs=uT[:, cq, ts], start=True, stop=True),
                  reads=[f'bbim{tl}', f'u{cq}'], writes=[k_pim])
            b_re, k_bre = sc.next()
            b_im, k_bim = sc.next()
            R.add('act', I('activation', out=b_re[:], in_=p_re[:], func=AF.Copy), writes=[k_pre, k_bre])
            R.add('act', I('activation', out=b_im[:], in_=p_im[:], func=AF.Copy), writes=[k_pim, k_bim])
            t1, k1 = sc.next()
            t2, k2 = sc.next()
            R.add('pool', I('tensor_tensor', out=t1[:], in0=b_re[:], in1=cs_, op=ALU.mult), reads=[k_bre, ck], writes=[k1])
            R.add('pool', I('tensor_tensor', out=t2[:], in0=b_im[:], in1=sn_, op=ALU.mult), reads=[k_bim, sk], writes=[k2])
            st8[i] = (t1, k1, t2, k2, b_re, k_bre, b_im, k_bim)

        def front_b(i):
            c, q = items[i]
            cs_, sn_ = cosT[q][:], sinT[q][:]
            ck, sk = f'cost{q}', f'sint{q}'
            t1, k1, t2, k2, b_re, k_bre, b_im, k_bim = st8[i]
            R.add('dve', I('tensor_tensor', out=t1[:], in0=t1[:], in1=t2[:], op=ALU.add), reads=[k1, k2], writes=[k1])
            R.add('dve', I('tensor_tensor', out=b_im[:], in0=b_im[:], in1=cs_, op=ALU.mult), reads=[ck, k2], writes=[k_bim])
            R.add('dve', I('tensor_tensor', out=b_re[:], in0=b_re[:], in1=sn_, op=ALU.mult), reads=[sk, k1], writes=[k_bre])
            R.add('dve', I('tensor_tensor', out=t2[:], in0=b_im[:], in1=b_re[:], op=ALU.subtract),
                  reads=[k_bim, k_bre, k1], writes=[k2])

        ybank = {}
        st9 = {}

        def back_a(i):
            c, q = items[i]
            tl = cq * 4 + q
            t1, k1, t2, k2 = st8.pop(i)[:4]
            i_re, k_ire = c1.next()
            i_im, k_iim = c1.next()
            tt, k_tt = c1.next()
            csT_, snT_ = P['csT'][:, tl:tl + 1], P['snT'][:, tl:tl + 1]
            R.add('dve', I('tensor_tensor', out=tt[:], in0=st_im[tl][:], in1=snT_, op=ALU.mult),
                  reads=[f'stim{tl}', 'snT'], writes=[k_tt])
            R.add('dve', I('scalar_tensor_tensor', out=i_re[:], in0=st_re[tl][:], scalar=csT_, in1=tt[:],
                           op0=ALU.mult, op1=ALU.subtract), reads=[f'stre{tl}', 'csT', k_tt], writes=[k_ire])
            R.add('dve', I('tensor_tensor', out=tt[:], in0=st_im[tl][:], in1=csT_, op=ALU.mult),
                  reads=[f'stim{tl}', 'csT', k_tt], writes=[k_tt])
            R.add('dve', I('scalar_tensor_tensor', out=i_im[:], in0=st_re[tl][:], scalar=snT_, in1=tt[:],
                           op0=ALU.mult, op1=ALU.add), reads=[f'stre{tl}', 'snT', k_tt], writes=[k_iim])
            g_re, k_gre = gg.next()
            g_im, k_gim = gg.next()
            rb = P['r'][:, tl:tl + 1].to_broadcast([128, T])
            R.add('dve', I('tensor_tensor_scan', out=g_re[:], data0=rb, data1=t1[:], initial=i_re[:, 0:1],
                           op0=ALU.mult, op1=ALU.add), reads=['r', k1, k_ire], writes=[k_gre])
            R.add('dve', I('tensor_tensor_scan', out=g_im[:], data0=rb, data1=t2[:], initial=i_im[:, 0:1],
                           op0=ALU.mult, op1=ALU.add), reads=['r', k2, k_iim], writes=[k_gim])
            R.add('act', I('activation', out=st_re[tl][:], in_=g_re[:, T - 1:T], func=AF.Copy),
                  reads=[k_gre, k_ire, k_iim], writes=[f'stre{tl}'])
            R.add('act', I('activation', out=st_im[tl][:], in_=g_im[:, T - 1:T], func=AF.Copy),
                  reads=[k_gim, k_ire, k_iim], writes=[f'stim{tl}'])
            st9[i] = (t1, k1, t2, k2, g_re, k_gre, g_im, k_gim)

        def back_b(i):
            c, q = items[i]
            ts = slice(c * T, (c + 1) * T)
            tl = cq * 4 + q
            cs_, sn_ = cosT[q][:], sinT[q][:]
            ck, sk = f'cost{q}', f'sint{q}'
            t1, k1, t2, k2, g_re, k_gre, g_im, k_gim = st9.pop(i)
            if q == 0:
                ybank[c] = py.next()
            y_p, y_k = ybank[c]
            h_re, k_hre = hb.next()
            h_imn, k_him = hb.next()
            R.add('pool', I('tensor_tensor', out=t1[:], in0=g_re[:], in1=cs_, op=ALU.mult), reads=[k_gre, ck], writes=[k1])
            R.add('pool', I('tensor_tensor', out=t2[:], in0=g_im[:], in1=sn_, op=ALU.mult), reads=[k_gim, sk], writes=[k2])
            R.add('dve', I('tensor_tensor', out=g_re[:], in0=g_re[:], in1=sn_, op=ALU.mult), reads=[sk, k1], writes=[k_gre])
            R.add('dve', I('tensor_tensor', out=g_im[:], in0=g_im[:], in1=cs_, op=ALU.mult), reads=[ck, k2], writes=[k_gim])
            R.add('dve', I('scalar_tensor_tensor', out=h_imn[:], in0=g_re[:], scalar=-1.0, in1=g_im[:],
                           op0=ALU.mult, op1=ALU.subtract), reads=[k_gre, k_gim], writes=[k_him])
            R.add('dve', I('tensor_tensor', out=h_re[:], in0=t1[:], in1=t2[:], op=ALU.subtract), reads=[k1, k2], writes=[k_hre])
            R.add('pe', [I('matmul', out=y_p[:], lhsT=creb[:, tl, :], rhs=h_re[:], start=(q == 0), stop=False),
                         I('matmul', out=y_p[:], lhsT=cimb[:, tl, :], rhs=h_imn[:], start=False, stop=(q == 3))],
                  reads=['creb', 'cimb', k_hre, k_him], writes=[y_k])
            if q == 3:
                y_s, ys_k = yb.next()
                R.add('dve', I('scalar_tensor_tensor', out=y_s[:], in0=uT[:, cq, ts], scalar=dsk[:, cq:cq + 1], in1=y_p[:],
                               op0=ALU.mult, op1=ALU.add), reads=[f'u{cq}', 'dsk'], writes=[y_k, ys_k])
                y_o, yo_k = yo.next()
                R.add('act', I('activation', out=y_o[:], in_=y_s[:], func=AF.Gelu_apprx_tanh), reads=[ys_k], writes=[yo_k])
                fin.append(R.add('sp', I('dma_start', out=y_d[cq * 128:(cq + 1) * 128, ts], in_=y_o[:]), reads=[yo_k], dma='yout'))

        n_it = len(items)
        front_a(0)
        front_a(1)
        front_b(0)
        for i in range(n_it):
            if i + 2 < n_it:
                front_a(i + 2)
            back_a(i)
            if i + 1 < n_it:
                front_b(i + 1)
            back_b(i)
    return C.finish()


def s5_inputs(inp, b, hf, hT_full):
    G0 = 32 * hf
    j = 0
    lam_re = inp['s5_lam_re'][j][G0:G0 + 32]
    lam_im = inp['s5_lam_im'][j][G0:G0 + 32]
    log_dt = inp['s5_log_dt'][j][G0:G0 + 32]
    bre = inp['s5_b_re'][j][G0:G0 + 32]
    bim = inp['s5_b_im'][j][G0:G0 + 32]
    cre = inp['s5_c_re'][j][G0:G0 + 32]
    cim = inp['s5_c_im'][j][G0:G0 + 32]

    def cols(a):
        return np.ascontiguousarray(a.reshape(16, 128).T.astype(np.float32))

    def padB(a):
        out = np.zeros((16, 128, 128), np.float32)
        for tl in range(16):
            q = tl % 4
            for s in range(2):
                g = 2 * tl + s
                out[tl, 64 * s:64 * s + 64, 32 * q + 16 * s:32 * q + 16 * s + 16] = a[g]
        return out

    def padC(a):
        return padB(np.ascontiguousarray(a.transpose(0, 2, 1)))

    return dict(hT=hT_full, hfm=hfm_arr(hf),
                lam_re=cols(lam_re), lam_im=cols(lam_im), log_dt=cols(np.repeat(log_dt[:, None], 64, axis=1)),
                bre=padB(bre), bim=padB(bim), cre=padC(cre), cim=padC(cim),
                dsk=np.ascontiguousarray(inp['s5_d'][j][512 * hf:512 * hf + 512].reshape(4, 128).T.astype(np.float32)),
                ident=np.eye(128, dtype=np.float32))


def build_hgrn(li, P=None, prefix='', bind=None):
    C = Ctx(P, prefix, bind)
    nc, R = C.nc, C.rec
    NBLK = L // 512
    CH = 32
    h_d = C.dram_in("hT", [2 * D, TOK], BF16)
    w_d = {n: C.dram_in("w_" + n, [D, 512]) for n in ('q', 'f', 'i', 'g')}
    lb_d = C.dram_in("lbraw", [128, 4, 4])
    ng_d = C.dram_in("hng", [128, 1])
    cm_d = C.dram_in("cmask", [128, 512])
    am_d = C.dram_in("amask", [128, 512])
    id_d = C.dram_in("identb", [128, 128], BF16)
    chm_d = C.dram_in("chm", [128, 4])
    o_d = C.dram_out("oT", [512, L], BF16)

    hT = C.sb([128, 8, L], BF16, "hT")
    W = {n: C.sb([128, 8, 512], BF16, "w_" + n) for n in ('q', 'f', 'i', 'g')}
    lbraw = C.sb([128, 4, 4], F32, "lbraw")
    lbs = C.sb([128, 4], F32, "lbs")
    lb = C.sb([128, 4], F32, "lb")
    omlb = C.sb([128, 4], F32, "omlb")
    hng = C.sb([128, 1], F32, "hng")
    cmask = C.sb([128, 512], F32, "cmask")
    amask = C.sb([128, 512], F32, "amask")
    identb = C.sb([128, 128], BF16, "identb")
    chm = C.sb([128, 4], F32, "chm")
    ones = C.sb([128, 128], F32, "ones")
    S = [C.sb([128, 128], F32, f"S{h}") for h in range(4)]
    Sb = [C.sb([128, 128], BF16, f"Sb{h}") for h in range(4)]
    qt = [C.sb([128, 512], BF16, f"qt{h}") for h in range(4)]
    kt = [C.sb([128, 512], BF16, f"kt{h}") for h in range(4)]
    ks = [C.sb([128, 512], BF16, f"ks{h}") for h in range(4)]
    eb = [C.sb([128, 512], F32, f"eb{h}") for h in range(4)]
    gs = [C.sb([128, 512], F32, f"gs{h}") for h in range(4)]
    oblk = C.sb([128, 4, 512], F32, "oblk")
    sc = Ring(C, 6, [128, 512], F32, "sc")
    vtm = Ring(C, 2, [128, 512], BF16, "vtm")
    atm = Ring(C, 2, [128, 512], BF16, "atm")
    ksm = Ring(C, 8, [128, 512], BF16, "ksm")
    ofin = Ring(C, 2, [128, 512], BF16, "ofin")
    sqbr = Ring(C, 2, [128, 512], BF16, "sqbr")
    onesb = C.sb([128, 128], BF16, "onesb")
    pp = Ring(C, 2, [128, 512], F32, "pp", psum=True)
    pk = C.ps([128, 1024], BF16, "pk")
    po = C.ps([128, 512], F32, "po")
    psu = [C.ps([128, 512], F32, f"psu{h}") for h in range(4)]

    for kc in range(8):
        for r in range(2):
            R.add('sp', I('dma_start', out=hT[:, kc, r * 2048:(r + 1) * 2048], in_=h_d[hrow(r, kc):hrow(r, kc) + 128, :]),
                  writes=[f'h{kc}'], dma='h')
    for n in ('q', 'f', 'i', 'g'):
        R.add('pool', I('dma_start', out=W[n][:], in_=w_d[n].rearrange("(kc p) c -> p kc c", p=128)), writes=['w_' + n], dma='w')
    R.add('sp', I('dma_start', out=lbraw[:], in_=lb_d), writes=['lbraw'], dma='c')
    R.add('sp', I('dma_start', out=hng[:], in_=ng_d), writes=['hng'], dma='c')
    R.add('sp', I('dma_start', out=cmask[:], in_=cm_d), writes=['cmask'], dma='c')
    R.add('sp', I('dma_start', out=amask[:], in_=am_d), writes=['amask'], dma='c')
    R.add('sp', I('dma_start', out=identb[:], in_=id_d), writes=['identb'], dma='c')
    R.add('sp', I('dma_start', out=chm[:], in_=chm_d), writes=['chm'], dma='c')
    R.add('pool', I('memset', ap=ones[:], constant=1.0), writes=['ones'])
    R.add('pool', I('memset', ap=onesb[:], constant=1.0), writes=['onesb'])
    for h in range(4):
        R.add('pool', I('memset', ap=S[h][:], constant=0.0), writes=[f'S{h}'])
        R.add('pool', I('memset', ap=Sb[h][:], constant=0.0), writes=[f'Sb{h}'])
    R.add('act', I('activation', out=lbraw[:], in_=lbraw[:], func=AF.Exp), reads=['lbraw'], writes=['lbraw'])
    R.add('dve', I('tensor_tensor', out=lbs[:], in0=lbraw[:, 0, :], in1=lbraw[:, 1, :], op=ALU.add), reads=['lbraw'], writes=['lbs'])
    R.add('dve', I('tensor_tensor', out=lbs[:], in0=lbs[:], in1=lbraw[:, 2, :], op=ALU.add), reads=['lbraw', 'lbs'], writes=['lbs'])
    R.add('dve', I('tensor_tensor', out=lbs[:], in0=lbs[:], in1=lbraw[:, 3, :], op=ALU.add), reads=['lbraw', 'lbs'], writes=['lbs'])
    R.add('dve', I('reciprocal', out=lbs[:], in_=lbs[:]), reads=['lbs'], writes=['lbs'])
    R.add('dve', I('tensor_copy', out=lb[:], in_=lbraw[:, 1, :]), reads=['lbraw'], writes=['lb'])
    for k in range(2, li + 1):
        R.add('dve', I('tensor_tensor', out=lb[:], in0=lb[:], in1=lbraw[:, k, :], op=ALU.add), reads=['lbraw', 'lb'], writes=['lb'])
    R.add('dve', I('tensor_tensor', out=lb[:], in0=lb[:], in1=lbs[:], op=ALU.mult), reads=['lb', 'lbs'], writes=['lb'])
    R.add('dve', I('tensor_scalar', out=omlb[:], in0=lb[:], scalar1=-1.0, scalar2=1.0, op0=ALU.mult, op1=ALU.add),
          reads=['lb'], writes=['omlb'])

    hk = [f'h{kc}' for kc in range(8)]
    fin = []
    for blk in range(NBLK):
        bs = slice(blk * 512, (blk + 1) * 512)
        for h in range(4):
            hs = slice(h * 128, (h + 1) * 128)

            def proj(n):
                p, kp = pp.next()
                R.add('pe', [I('matmul', out=p[:], lhsT=W[n][:, kc, hs], rhs=hT[:, kc, bs], start=(kc == 0), stop=(kc == 7))
                             for kc in range(8)], reads=hk + ['w_' + n], writes=[kp])
                return p, kp

            p, kp = proj('f')
            fg, kfg = sc.next()
            R.add('act', I('activation', out=fg[:], in_=p[:], func=AF.Sigmoid), writes=[kp, kfg])
            R.add('dve', I('tensor_scalar', out=fg[:], in0=fg[:], scalar1=omlb[:, h:h + 1], scalar2=lb[:, h:h + 1],
                           op0=ALU.mult, op1=ALU.add), reads=[kfg, 'omlb', 'lb'], writes=[kfg])
            lf, klf = sc.next()
            R.add('act', I('activation', out=lf[:], in_=fg[:], func=AF.Ln), reads=[kfg], writes=[klf])
            R.add('pool', I('tensor_scalar', out=fg[:], in0=fg[:], scalar1=-1.0, scalar2=1.0, op0=ALU.mult, op1=ALU.add),
                  reads=[kfg, klf], writes=[kfg])
            bb, kbb = sc.next()
            R.add('dve', I('tensor_tensor_scan', out=bb[:], data0=cmask[:], data1=lf[:], initial=0.0,
                           op0=ALU.mult, op1=ALU.add), reads=['cmask', klf], writes=[kbb])
            R.add('act', I('activation', out=eb[h][:], in_=bb[:], func=AF.Exp), reads=[kbb], writes=[f'eb{h}'])
            R.add('act', I('activation', out=lf[:], in_=bb[:], func=AF.Exp, scale=-1.0), reads=[kbb, klf], writes=[klf])
            R.add('pool', I('tensor_tensor', out=kt[h][:], in0=fg[:], in1=lf[:], op=ALU.mult), reads=[kfg, klf], writes=[f'kt{h}'])
            b3 = bb[:].rearrange("p (c t) -> p c t", t=CH)
            R.add('dve', I('tensor_tensor', out=lf[:].rearrange("p (c t) -> p c t", t=CH),
                           in0=b3[:, :, CH - 1:CH].to_broadcast([128, 512 // CH, CH]), in1=b3, op=ALU.subtract),
                  reads=[kbb, klf, f'kt{h}'], writes=[klf])
            R.add('act', I('activation', out=lf[:], in_=lf[:], func=AF.Exp), reads=[klf], writes=[klf])
            R.add('pool', I('tensor_tensor', out=ks[h][:], in0=fg[:], in1=lf[:], op=ALU.mult), reads=[kfg, klf], writes=[f'ks{h}'])
            p, kp = proj('q')
            qs_, kqs = sc.next()
            R.add('act', I('activation', out=qs_[:], in_=p[:], func=AF.Silu), writes=[kp, kqs])
            R.add('dve', I('scalar_tensor_tensor', out=qt[h][:], in0=qs_[:], scalar=128.0 ** -0.5, in1=eb[h][:],
                           op0=ALU.mult, op1=ALU.mult), reads=[kqs, f'eb{h}'], writes=[f'qt{h}'])
            p, kp = proj('g')
            R.add('act', I('activation', out=gs[h][:], in_=p[:], func=AF.Silu), writes=[kp, f'gs{h}'])
        for tt in range(4):
            t0 = blk * 512 + tt * 128
            ls = slice(tt * 128, (tt + 1) * 128)
            gsl = slice(t0, t0 + 128)
            pv, kpv = pp.next()
            R.add('pe', [I('matmul', out=pv[:], lhsT=hT[:, kc, gsl], rhs=W['i'][:, kc, :], start=(kc == 0), stop=(kc == 7))
                         for kc in range(8)], reads=hk + ['w_i'], writes=[kpv])
            v_t, kv = vtm.next()
            R.add('act', I('activation', out=v_t[:], in_=pv[:], func=AF.Copy), writes=[kpv, kv])
            pa, kpa = pp.next()
            R.add('pe', [I('matmul', out=pa[:, h * 128:(h + 1) * 128], lhsT=kt[h][:, ls], rhs=qt[h][:, ls], start=True, stop=True)
                         for h in range(4)], reads=[f'kt{h}' for h in range(4)] + [f'qt{h}' for h in range(4)], writes=[kpa])
            a_t, ka = atm.next()
            R.add('dve', I('tensor_tensor', out=a_t[:], in0=pa[:], in1=amask[:], op=ALU.mult), reads=['amask'], writes=[kpa, ka])
            R.add('pe', [I('transpose', out=pk[:, h * 128:(h + 1) * 128], in_=ks[h][:, ls], identity=identb[:]) for h in range(4)],
                  reads=[f'ks{h}' for h in range(4)] + ['identb'], writes=['pk'])
            km, kkm = [], []
            for c in range(4):
                m_t, km_k = ksm.next()
                R.add('act', I('activation', out=m_t[:], in_=pk[:, 0:512], func=AF.Copy, scale=chm[:, c:c + 1]),
                      reads=['chm'], writes=['pk', km_k])
                km.append(m_t)
                kkm.append(km_k)
            for c in range(4):
                cl = slice(tt * 128 + c * CH, tt * 128 + (c + 1) * CH)
                for h in range(4):
                    hs = slice(h * 128, (h + 1) * 128)
                    oc = slice(h * 128 + c * CH, h * 128 + (c + 1) * CH)
                    R.add('pe', [I('matmul', out=po[:, oc], lhsT=Sb[h][:], rhs=qt[h][:, cl], start=True, stop=False),
                                 I('matmul', out=po[:, oc], lhsT=v_t[:, hs], rhs=a_t[:, oc], start=False, stop=True)],
                          reads=[f'Sb{h}', f'qt{h}', kv, ka], writes=['po'])
                    R.add('pe', I('matmul', out=psu[h][:, 0:128], lhsT=km[c][:, hs], rhs=v_t[:, hs], start=True, stop=True),
                          reads=[kkm[c], kv], writes=[f'psu{h}'])
                    dcol = eb[h][:, tt * 128 + (c + 1) * CH - 1:tt * 128 + (c + 1) * CH]
                    R.add('dve', I('scalar_tensor_tensor', out=S[h][:], in0=S[h][:], scalar=dcol, in1=psu[h][:, 0:128],
                                   op0=ALU.mult, op1=ALU.add), reads=[f'eb{h}'], writes=[f'S{h}', f'psu{h}'])
                    R.add('pool', I('tensor_copy', out=Sb[h][:], in_=S[h][:]), reads=[f'S{h}'], writes=[f'Sb{h}'])
            R.add('act', I('activation', out=oblk[:, :, ls], in_=po[:].rearrange("p (h t) -> p h t", h=4), func=AF.Copy),
                  writes=['po', 'oblk'])
        for h in range(4):
            sq, ksq = sc.next()
            sqb, ksqb = sqbr.next()
            R.add('act', I('activation', out=sqb[:], in_=oblk[:, h, :], func=AF.Square), reads=['oblk'], writes=[ksqb])
            pq, kpq = pp.next()
            R.add('pe', I('matmul', out=pq[:], lhsT=onesb[:], rhs=sqb[:], start=True, stop=True), reads=[ksqb, 'onesb'], writes=[kpq])
            R.add('act', I('activation', out=sq[:], in_=pq[:], func=AF.Ln, scale=1.0 / 128, bias=EPS), writes=[kpq, ksq])
            R.add('act', I('activation', out=sq[:], in_=sq[:], func=AF.Exp, scale=-0.5), writes=[ksq])
            R.add('dve', I('scalar_tensor_tensor', out=sq[:], in0=oblk[:, h, :], scalar=hng[:, 0:1], in1=sq[:],
                           op0=ALU.mult, op1=ALU.mult), reads=['oblk', 'hng', ksq], writes=[ksq])
            of, kof = ofin.next()
            R.add('dve', I('tensor_tensor', out=of[:], in0=sq[:], in1=gs[h][:], op=ALU.mult), reads=[ksq, f'gs{h}'], writes=[kof])
            fin.append(R.add('sp', I('dma_start', out=o_d[h * 128:(h + 1) * 128, bs], in_=of[:]), reads=[kof], dma='oout'))
    return C.finish()


def hgrn_consts():
    t = np.arange(512)
    cmask = np.broadcast_to((t % 32 != 0).astype(np.float32)[None, :], (128, 512)).copy()
    s = np.arange(128)
    amask = ((s[:, None] // 32 == s[None, :] // 32) & (s[:, None] <= s[None, :])).astype(np.float32)
    return cmask, np.ascontiguousarray(np.tile(amask, (1, 4)))


def hgrn_inputs(inp, hf, hT_full):
    j = 0
    w = inp['hg_w_in'][j]
    c0 = 512 * hf
    cmask, amask = hgrn_consts()
    lbr = inp['hg_lower_bounds'][:, c0:c0 + 512].reshape(4, 4, 128).transpose(2, 0, 1)
    return dict(hT=hT_full,
                w_q=np.ascontiguousarray(w[:, c0:c0 + 512]), w_f=np.ascontiguousarray(w[:, 1024 + c0:1024 + c0 + 512]),
                w_i=np.ascontiguousarray(w[:, 2048 + c0:2048 + c0 + 512]), w_g=np.ascontiguousarray(w[:, 3072 + c0:3072 + c0 + 512]),
                lbraw=np.ascontiguousarray(lbr.astype(np.float32)),
                hng=np.ascontiguousarray(inp['hg_norm_g'][j].reshape(128, 1).astype(np.float32)),
                cmask=cmask, amask=amask, identb=np.eye(128, dtype=np.float32).astype(ml_dtypes.bfloat16),
                chm=(np.arange(128)[:, None] // 32 == np.arange(4)[None, :]).astype(np.float32))


MOBA_NR = 4608
MOBA_WH = 4480
NEG = -1.0e30


def build_moba(P=None, prefix='', bind=None):
    C = Ctx(P, prefix, bind)
    nc, R = C.nc, C.rec
    NR, WH = MOBA_NR, MOBA_WH
    h_d = C.dram_in("hT", [2 * D, TOK], BF16)
    w_d = {n: C.dram_in("w_" + n, [D, 512]) for n in ('q', 'k', 'v')}
    tab_d = C.dram_in("tab", [33, 4])
    oh_d = C.dram_in("oh", [33, NR])
    selh_d = C.dram_in("selh", [4, 512])
    J_d = C.dram_in("Jb", [128, 128], BF16)
    E_d = C.dram_in("Eb", [17, 16 * 128], BF16)
    idb_d = C.dram_in("identb", [128, 128], BF16)
    pneg_d = C.dram_in("pneg", [128, 32, 16])
    p01_d = C.dram_in("p01", [128, 32, 16])
    o_d = C.dram_out("oT", [512, L], BF16)
    fs_h = nc.dram_tensor("fscr", [4, NR], F32)
    fs = fs_h.ap()

    hT = C.sb([128, 8, L], BF16, "hT")
    W = {n: C.sb([128, 8, 512], BF16, "w_" + n) for n in ('q', 'k', 'v')}
    tab = C.sb([33, 4], F32, "tab")
    selh = C.sb([4, 512], F32, "selh")
    Jb = C.sb([128, 128], BF16, "Jb")
    Eb = C.sb([17, 16 * 128], BF16, "Eb")
    identb = C.sb([128, 128], BF16, "identb")
    ones = C.sb([128, 128], F32, "ones")
    onesb = C.sb([128, 1], BF16, "onesb")
    fmx = C.sb([4, 16], F32, "fmx")
    bmax = C.sb([128, 4], F32, "bmax")
    vtm = C.sb([128, 32, 4, 129], BF16, "vtm")
    Hk = C.sb([128, WH], BF16, "Hk")
    qT = C.sb([128, L], BF16, "qT")
    kT = C.sb([128, L], BF16, "kT")
    q32 = C.sb([128, L], F32, "q32")
    rowsT = C.sb([17, L], BF16, "rowsT")
    kmean = C.sb([128, 16], F32, "kmean")
    kmx = C.sb([128, 8], F32, "kmx")
    kmax2 = C.sb([128, 1], F32, "kmax2")
    sc = Ring(C, 3, [128, 512], F32, "sc")
    ohr = Ring(C, 1, [33, 512], F32, "ohr")
    fr = Ring(C, 1, [4, 512], F32, "fr")
    gB = [C.sb([128, 16 * 17], F32, f"gB{i}") for i in range(2)]
    gBb = [C.sb([128, 16 * 17], BF16, f"gBb{i}") for i in range(2)]
    gW = Ring(C, 3, [128, 256], F32, "gW")
    gM = Ring(C, 2, [128, 16], F32, "gM")
    pneg = C.sb([128, 32, 16], F32, "pneg")
    p01 = C.sb([128, 32, 16], F32, "p01")
    ptile = Ring(C, 6, [128, 512], BF16, "ptile")
    rl = Ring(C, 2, [128, 1], F32, "rl")
    on = Ring(C, 2, [128, 128], BF16, "on")
    osb = Ring(C, 2, [128, 512], BF16, "osb")
    ps = Ring(C, 2, [128, 512], F32, "ps", psum=True)
    pp = Ring(C, 2, [128, 512], F32, "pp", psum=True)
    po = C.ps([128, 512], F32, "po")
    pl = C.ps([128, 512], F32, "pl")
    pg = C.ps([128, 512], F32, "pg")
    ptb = C.ps([128, 1024], BF16, "ptb")

    class _MixRing:
        def __init__(self, t, k):
            self.t, self.k, self.i = t, k, 0

        def next(self):
            j = self.i % len(self.t)
            self.i += 1
            return self.t[j], self.k[j]

    stR = _MixRing(ps.t + pp.t + [pg], ps.k + pp.k + ['pg'])

    for kc in range(8):
        for r in range(2):
            R.add('sp', I('dma_start', out=hT[:, kc, r * 2048:(r + 1) * 2048], in_=h_d[hrow(r, kc):hrow(r, kc) + 128, :]),
                  writes=[f'h{kc}'], dma='h')
    for n in ('q', 'k', 'v'):
        R.add('pool', I('dma_start', out=W[n][:], in_=w_d[n].rearrange("(kc p) c -> p kc c", p=128)), writes=['w_' + n], dma='w')
    R.add('sp', I('dma_start', out=tab[:], in_=tab_d), writes=['tab'], dma='c')
    R.add('sp', I('dma_start', out=selh[:], in_=selh_d), writes=['selh'], dma='c')
    R.add('sp', I('dma_start', out=Jb[:], in_=J_d), writes=['Jb'], dma='c')
    R.add('sp', I('dma_start', out=Eb[:], in_=E_d), writes=['Eb'], dma='c')
    R.add('sp', I('dma_start', out=identb[:], in_=idb_d), writes=['identb'], dma='c')
    R.add('sp', I('dma_start', out=pneg[:], in_=pneg_d), writes=['pneg'], dma='c')
    R.add('sp', I('dma_start', out=p01[:], in_=p01_d), writes=['p01'], dma='c')
    R.add('pool', I('memset', ap=ones[:], constant=1.0), writes=['ones'])
    R.add('pool', I('memset', ap=onesb[:], constant=1.0), writes=['onesb'])
    R.add('pool', I('memset', ap=vtm[:, :, :, 128:129], constant=1.0), writes=[f'v{t}' for t in range(32)])
    hk = [f'h{kc}' for kc in range(8)]

    for ch in range(NR // 512):
        o_t, o_k = ohr.next()
        R.add('sp', I('dma_start', out=o_t[:], in_=oh_d[:, ch * 512:(ch + 1) * 512]), writes=[o_k], dma=o_k)
        p, kp = pp.next()
        R.add('pe', I('matmul', out=p[0:4, :], lhsT=tab[:], rhs=o_t[:], start=True, stop=True), reads=['tab', o_k], writes=[kp])
        f_t, f_k = fr.next()
        R.add('dve', I('tensor_copy', out=f_t[:], in_=p[0:4, :]), writes=[kp, f_k])
        R.add('dve', I('tensor_reduce', out=fmx[:, ch:ch + 1], in_=f_t[:], axis=AX.X, op=ALU.max), reads=[f_k], writes=['fmx'])
        R.add('sp', I('dma_start', out=fs[:, ch * 512:(ch + 1) * 512], in_=f_t[:]), reads=[f_k], writes=['fs'], dma='fs')
    R.add('dve', I('tensor_reduce', out=fmx[:, 15:16], in_=fmx[:, 0:NR // 512], axis=AX.X, op=ALU.max), reads=['fmx'], writes=['fmx'])
    p, kp = pp.next()
    R.add('pe', [I('matmul', out=p[:, h:h + 1], lhsT=selh[:, h * 128:(h + 1) * 128], rhs=fmx[:, 15:16], start=True, stop=True)
                 for h in range(4)], reads=['selh', 'fmx'], writes=[kp])
    R.add('dve', I('tensor_copy', out=bmax[:], in_=p[:, 0:4]), writes=[kp, 'bmax'])

    for t in range(32):
        p, kp = pp.next()
        R.add('pe', [I('matmul', out=p[:], lhsT=hT[:, kc, t * 128:(t + 1) * 128], rhs=W['v'][:, kc, :], start=(kc == 0), stop=(kc == 7))
                     for kc in range(8)], reads=hk + ['w_v'], writes=[kp])
        R.add('act', I('activation', out=vtm[:, t, :, 0:128], in_=p[:].rearrange("p (h d) -> p h d", h=4), func=AF.Copy), writes=[kp, f'v{t}'])
    vkeys = [f'v{t}' for t in range(32)]

    fin = []
    for h in range(4):
        hs = slice(h * 128, (h + 1) * 128)
        R.add('pool', I('dma_start', out=Hk[:], in_=bass.AP(fs_h, h * NR, [[1, 128], [1, WH]])),
              reads=['fs'], writes=['Hk'], dma='hk')
        for blk in range(8):
            bs = slice(blk * 512, (blk + 1) * 512)
            p, kp = pp.next()
            R.add('pe', [I('matmul', out=p[:], lhsT=W['q'][:, kc, hs], rhs=hT[:, kc, bs], start=(kc == 0), stop=(kc == 7))
                         for kc in range(8)], reads=hk + ['w_q'], writes=[kp])
            R.add('act', I('activation', out=q32[:, bs], in_=p[:], func=AF.Copy, scale=128.0 ** -0.5), writes=[kp, f'q32_{blk}'])
            R.add('pool', I('tensor_copy', out=qT[:, bs], in_=q32[:, bs]), reads=[f'q32_{blk}'], writes=[f'qT{blk}'])
        for blk in range(8):
            bs = slice(blk * 512, (blk + 1) * 512)
            p, kp = pp.next()
            R.add('pe', [I('matmul', out=p[:], lhsT=W['k'][:, kc, hs], rhs=hT[:, kc, bs], start=(kc == 0), stop=(kc == 7))
                         for kc in range(8)], reads=hk + ['w_k'], writes=[kp])
            k32, kk32 = sc.next()
            R.add('act', I('activation', out=k32[:], in_=p[:], func=AF.Copy), writes=[kp, kk32])
            R.add('pool', I('tensor_copy', out=kT[:, bs], in_=k32[:]), reads=[kk32], writes=[f'kT{blk}'])
            R.add('dve', I('tensor_reduce', out=kmean[:, 2 * blk:2 * blk + 2], in_=k32[:].rearrange("p (j t) -> p j t", t=256),
                           axis=AX.X, op=ALU.add), reads=[kk32], writes=['kmean'])
            R.add('pool', I('tensor_tensor', out=k32[:], in0=k32[:], in1=k32[:], op=ALU.mult), reads=[kk32, f'kT{blk}', 'kmean'], writes=[kk32])
            p2, kp2 = pp.next()
            R.add('pe', I('matmul', out=p2[:], lhsT=ones[:], rhs=k32[:], start=True, stop=True), reads=['ones', kk32], writes=[kp2])
            R.add('dve', I('tensor_reduce', out=kmx[:, blk:blk + 1], in_=p2[:], axis=AX.X, op=ALU.max), writes=[kp2, 'kmx'])
        R.add('dve', I('tensor_reduce', out=kmax2[:], in_=kmx[:], axis=AX.X, op=ALU.max), reads=['kmx'], writes=['kmax2'])
        R.add('dve', I('tensor_scalar', out=kmean[:], in0=kmean[:], scalar1=1.0 / 256, scalar2=None, op0=ALU.mult),
              reads=['kmean'], writes=['kmean'])
        for half in range(2):
            pgb, kpgb = (pg, 'pg') if half == 0 else pp.next()
            mm = []
            rk = set()
            for tl_ in range(16):
                t = half * 16 + tl_
                tsl = slice(t * 128, (t + 1) * 128)
                blk = t // 4
                q2, kq2 = sc.next() if tl_ % 4 == 0 else (q2, kq2)
                if tl_ % 4 == 0:
                    bsl = slice(blk * 512, (blk + 1) * 512)
                    R.add('pool', I('tensor_tensor', out=q2[:], in0=q32[:, bsl], in1=q32[:, bsl], op=ALU.mult),
                          reads=[f'q32_{blk}'], writes=[kq2])
                    R.add('pe', [I('matmul', out=pgb[:, (tl_ + u) * 17:(tl_ + u) * 17 + 16], lhsT=q32[:, (t + u) * 128:(t + u + 1) * 128],
                                   rhs=kmean[:], start=True, stop=True) for u in range(4)] +
                          [I('matmul', out=pgb[:, (tl_ + u) * 17 + 16:(tl_ + u) * 17 + 17], lhsT=q2[:, u * 128:(u + 1) * 128],
                             rhs=ones[:, 0:1], start=True, stop=True) for u in range(4)],
                          reads=[f'q32_{blk}', 'kmean', kq2, 'ones'], writes=[kpgb])
            G = gB[half]
            kG = f'gB{half}'
            G3 = G[:].rearrange("p (t j) -> p t j", j=17)
            R.add('dve', I('tensor_copy', out=G[:], in_=pgb[:, 0:272]), writes=[kpgb, kG])
            Gg = G3[:, :, 0:16]
            R.add('dve', I('tensor_tensor', out=Gg, in0=Gg, in1=pneg[:, half * 16:(half + 1) * 16, :], op=ALU.add), reads=['pneg'], writes=[kG])
            W1, kW1 = gW.next()
            W13 = W1[:].rearrange("p (t j) -> p t j", j=16)
            mx, kmx_ = gM.next()
            eq, keq = gW.next()
            eq3 = eq[:].rearrange("p (t j) -> p t j", j=16)
            cur = Gg
            kcur = kG
            for it in range(3):
                R.add('dve', I('tensor_reduce', out=mx[:], in_=cur, axis=AX.X, op=ALU.max), reads=[kcur], writes=[kmx_])
                if it == 2:
                    break
                R.add('dve', I('tensor_tensor', out=eq3, in0=cur, in1=mx[:].unsqueeze(2).to_broadcast([128, 16, 16]), op=ALU.is_ge),
                      reads=[kcur, kmx_], writes=[keq])
                R.add('dve', I('scalar_tensor_tensor', out=W13, in0=eq3, scalar=NEG, in1=cur, op0=ALU.mult, op1=ALU.add),
                      reads=[keq, kcur], writes=[kW1])
                cur = W13
                kcur = kW1
            R.add('dve', I('tensor_tensor', out=eq3, in0=Gg, in1=mx[:].unsqueeze(2).to_broadcast([128, 16, 16]), op=ALU.is_ge),
                  reads=[kG, kmx_], writes=[keq])
            R.add('dve', I('tensor_scalar', out=eq3, in0=eq3, scalar1=-NEG, scalar2=NEG, op0=ALU.mult, op1=ALU.add), writes=[keq])
            R.add('dve', I('tensor_tensor', out=Gg, in0=eq3, in1=p01[:, half * 16:(half + 1) * 16, :], op=ALU.mult),
                  reads=[keq, 'p01'], writes=[kG])
            Gb = G3[:, :, 16:17]
            R.add('dve', I('tensor_scalar', out=Gb, in0=Gb, scalar1=kmax2[:, 0:1], scalar2=None, op0=ALU.mult), reads=['kmax2'], writes=[kG])
            R.add('act', I('activation', out=Gb, in_=Gb, func=AF.Sqrt), writes=[kG])
            R.add('dve', I('tensor_scalar', out=Gb, in0=Gb, scalar1=bmax[:, h:h + 1], scalar2=-1.0, op0=ALU.add, op1=ALU.mult),
                  reads=['bmax'], writes=[kG])
            Gh = gBb[half]
            R.add('dve', I('tensor_copy', out=Gh[:], in_=G[:]), reads=[kG], writes=[kG + 'b'])
            for rnd in range(2):
                R.add('pe', [I('transpose', out=ptb[0:17, u * 128:(u + 1) * 128], in_=Gh[:, (rnd * 8 + u) * 17:(rnd * 8 + u + 1) * 17],
                               identity=identb[:]) for u in range(8)], reads=[kG + 'b', 'identb'], writes=['ptb'])
                t0_ = (half * 16 + rnd * 8) * 128
                R.add('act', I('activation', out=rowsT[:, t0_:t0_ + 1024], in_=ptb[0:17, 0:1024], func=AF.Copy),
                      writes=['ptb'] + [f'rows{(t0_ // 512) + u}' for u in range(2)])
        tiles = [(qb, kj) for qb in range(8) for kj in range(4 * qb + 4)]
        pend = {}

        def st_tile(idx):
            qb, kj = tiles[idx]
            qs = slice(qb * 512, (qb + 1) * 512)
            u0 = 512 * qb - 128 * kj + 384
            j = kj // 2
            s_p, s_k = stR.next()
            R.add('pe', [I('matmul', out=s_p[:], lhsT=kT[:, kj * 128:(kj + 1) * 128], rhs=qT[:, qs], start=True, stop=False),
                         I('matmul', out=s_p[:], lhsT=Jb[:], rhs=Hk[:, u0:u0 + 512], start=False, stop=False),
                         I('matmul', out=s_p[:], lhsT=Eb[:, j * 128:(j + 1) * 128], rhs=rowsT[:, qs], start=False, stop=True)],
                  reads=[f'kT{kj // 4}', f'qT{qb}', 'Jb', 'Hk', 'Eb', f'rows{qb}'], writes=[s_k])
            p_t, p_k = ptile.next()
            R.add('act', I('activation', out=p_t[:], in_=s_p[:], func=AF.Exp), writes=[s_k, p_k])
            pend[idx] = (p_t, p_k)

        def pv_tile(idx):
            qb, kj = tiles[idx]
            qs = slice(qb * 512, (qb + 1) * 512)
            p_t, p_k = pend.pop(idx)
            mms0, mms1 = [], []
            for qt in range(4):
                gq = 4 * qb + qt
                if kj > gq:
                    continue
                cs_ = slice(qt * 128, (qt + 1) * 128)
                if qt < 3:
                    mms0.append(I('matmul', out=po[:, qt * 129:(qt + 1) * 129], lhsT=p_t[:, cs_], rhs=vtm[:, kj, h, :], start=(kj == 0 and qt == 0),
                                  stop=(kj == gq), skip_group_check=True))
                else:
                    mms1.append(I('matmul', out=pl[:, 0:129], lhsT=p_t[:, cs_], rhs=vtm[:, kj, h, :], start=(kj == 0), stop=(kj == gq),
                                  skip_group_check=True))
            if mms0:
                R.add('pe', mms0, reads=[p_k, f'v{kj}'], writes=['po'])
            if mms1:
                R.add('pe', mms1, reads=[p_k, f'v{kj}'], writes=['pl'])
            if kj == 4 * qb + 3:
                o_s, o_k = osb.next()
                for qt in range(4):
                    cs_ = slice(qt * 128, (qt + 1) * 128)
                    r_t, r_k = rl.next()
                    bank, bkey, c0_ = (po, 'po', qt * 129) if qt < 3 else (pl, 'pl', 0)
                    R.add('dve', I('reciprocal', out=r_t[:], in_=bank[:, c0_ + 128:c0_ + 129]), writes=[bkey, r_k])
                    n_t, n_k = on.next()
                    R.add('dve', I('tensor_scalar', out=n_t[:], in0=bank[:, c0_:c0_ + 128], scalar1=r_t[:, 0:1], scalar2=None, op0=ALU.mult),
                          reads=[r_k], writes=[bkey, n_k])
                    R.add('pe', I('transpose', out=ptb[:, cs_], in_=n_t[:], identity=identb[:]), reads=[n_k, 'identb'], writes=['ptb'])
                R.add('act', I('activation', out=o_s[:], in_=ptb[:, 0:512], func=AF.Copy), writes=['ptb', o_k])
                fin.append(R.add('sp', I('dma_start', out=o_d[hs, qs], in_=o_s[:]), reads=[o_k], dma='oout'))

        LOOK = 3
        for idx in range(min(LOOK, len(tiles))):
            st_tile(idx)
        for idx in range(len(tiles)):
            if idx + LOOK < len(tiles):
                st_tile(idx + LOOK)
            pv_tile(idx)
    return C.finish()


def t5_bucket_np(rel):
    n = np.maximum(rel, 0)
    max_exact = 16
    nf = np.maximum(n, 1).astype(np.float32)
    large = max_exact + (np.log(nf / np.float32(max_exact)) / np.float32(math.log(2048 / max_exact)) * np.float32(32 - max_exact)).astype(np.int32)
    large = np.minimum(large, 31)
    return np.where(n < max_exact, n, large)


def moba_consts():
    NR = MOBA_NR
    w = np.arange(NR)
    dist = w - 511
    oh = np.zeros((33, NR), np.float32)
    bk = t5_bucket_np(dist)
    valid = dist >= 0
    oh[bk[valid], w[valid]] = 1.0
    oh[32, ~valid] = NEG
    selh = np.zeros((4, 512), np.float32)
    for h in range(4):
        selh[h, h * 128:(h + 1) * 128] = 1.0
    J = np.eye(128, dtype=np.float32)[::-1].copy().astype(ml_dtypes.bfloat16)
    E = np.zeros((17, 16, 128), np.float32)
    for j in range(16):
        E[j, j, :] = 1.0
        E[16, j, :] = 1.0
    return oh, selh, J, E.reshape(17, 16 * 128).astype(ml_dtypes.bfloat16)


def moba_inputs(inp, hf, hT_full):
    j = 0
    w = inp['moba_w_in'][j]
    c0 = 512 * hf
    oh, selh, J, E = moba_consts()
    tab = np.ones((33, 4), np.float32)
    tab[:32] = inp['rel_bias_table'][:, 4 * hf:4 * hf + 4]
    tl = np.arange(32)[:, None] // 2
    jj = np.arange(16)[None, :]
    p01 = np.broadcast_to((jj < tl).astype(np.float32)[None], (128, 32, 16)).copy()
    pneg = np.broadcast_to(np.where(jj >= tl, NEG, 0.0).astype(np.float32)[None], (128, 32, 16)).copy()
    return dict(hT=hT_full, w_q=np.ascontiguousarray(w[:, c0:c0 + 512]), w_k=np.ascontiguousarray(w[:, 1024 + c0:1024 + c0 + 512]),
                w_v=np.ascontiguousarray(w[:, 2048 + c0:2048 + c0 + 512]), tab=tab, oh=oh, selh=selh, Jb=J, Eb=E, pneg=pneg, p01=p01,
                identb=np.eye(128, dtype=np.float32).astype(ml_dtypes.bfloat16))


def build_gdn(P=None, prefix='', bind=None):
    C = Ctx(P, prefix, bind)
    nc, R = C.nc, C.rec
    h_d = C.dram_in("hT", [2 * D, TOK], BF16)
    wq_d = C.dram_in("w_q", [D, 512])
    wk_d = C.dram_in("w_k", [D, 512])
    wv_d = C.dram_in("w_v", [D, 1024])
    wz_d = C.dram_in("w_z", [D, 1024])
    wba_d = C.dram_in("w_ba", [D, 16])
    cw_d = C.dram_in("convw", [128, 16, 4])
    na_d = C.dram_in("alog", [128, 8])
    dtb_d = C.dram_in("dtb", [128, 8])
    ng_d = C.dram_in("gng", [128, 1])
    msk_d = C.dram_in("masks", [128, 6, 128])
    idb_d = C.dram_in("identb", [128, 128], BF16)
    chm_d = C.dram_in("chm2", [128, 2])
    o_d = C.dram_out("oT", [1024, L], BF16)

    W = {'q': C.sb([128, 8, 512], BF16, "w_q"), 'k': C.sb([128, 8, 512], BF16, "w_k"),
         'v': C.sb([128, 8, 1024], BF16, "w_v"), 'z': C.sb([128, 8, 1024], BF16, "w_z"),
         'ba': C.sb([128, 8, 16], BF16, "w_ba")}
    hTb = Ring(C, 2, [128, 8, 512], BF16, "hTb")
    cw = C.sb([128, 16, 4], F32, "cw")
    alog = C.sb([128, 8], F32, "alog")
    dtb = C.sb([128, 8], F32, "dtb")
    gng = C.sb([128, 1], F32, "gng")
    msk = C.sb([128, 6, 128], F32, "msk")
    identb = C.sb([128, 128], BF16, "identb")
    chm = C.sb([128, 2], F32, "chm")
    ones = C.sb([128, 128], F32, "ones")
    halo = C.sb([128, 16, 4], F32, "halo")
    xs = Ring(C, 4, [128, 516], F32, "xs")
    cv = Ring(C, 4, [128, 512], F32, "cv")
    sc = Ring(C, 2, [128, 512], F32, "sc")
    qT = C.sb([128, 4, 512], BF16, "qT")
    kT = C.sb([128, 4, 512], BF16, "kT")
    vT = C.sb([128, 8, 512], BF16, "vT")
    zs = C.sb([128, 8, 512], BF16, "zs")
    oblk = C.sb([128, 8, 512], F32, "oblk")
    S4 = [C.sb([128, 4, 128], F32, f"S4_{g}") for g in range(2)]
    Sb4 = [C.sb([128, 4, 128], BF16, f"Sb4_{g}") for g in range(2)]
    vnew = [C.sb([128, 4, 128], BF16, f"vnew{g}") for g in range(2)]
    cols = {n: Ring(C, 2, [128, 8], F32, "c_" + n) for n in
            ('ba', 'beta', 'nbeta', 'g', 'ax', 'b', 'bl', 'eb', 'bexp', 'eblb', 'e0', 'e1')}
    ba16 = Ring(C, 2, [128, 16], F32, "ba16")
    brep = Ring(C, 1, [128, 8, 128], F32, "brep")
    ebrep = Ring(C, 2, [128, 8, 128], F32, "ebrep")
    bns = Ring(C, 1, [128, 8, 128], F32, "bns")
    dg = Ring(C, 2, [128, 512], F32, "dg")
    f4 = Ring(C, 5, [128, 4, 128], F32, "f4")
    b4 = Ring(C, 16, [128, 4, 128], BF16, "b4")
    ded = {(n, g): (C.sb([128, 4, 128], BF16, f"d_{n}{g}"), f"d_{n}{g}") for n in ("QK", "vb", "kbe", "kst0", "kst1", "qd", "WT") for g in range(2)}
    usb = Ring(C, 2, [128, 4, 128], F32, "usb")
    ofin = Ring(C, 2, [128, 512], BF16, "ofin")
    sqbr = Ring(C, 2, [128, 512], BF16, "sqbr")
    onesb = C.sb([128, 128], BF16, "onesb")
    pp = Ring(C, 2, [128, 512], F32, "pp", psum=True)
    pC = C.ps([128, 512], F32, "pC")
    pD = C.ps([128, 1024], BF16, "pD")
    pE = Ring(C, 2, [128, 512], F32, "pE", psum=True)
    pG = C.ps([128, 512], F32, "pG")
    pH = C.ps([128, 512], F32, "pH")

    class _MixRing:
        def __init__(self, t, k):
            self.t, self.k, self.i = t, k, 0

        def next(self):
            j = self.i % len(self.t)
            self.i += 1
            return self.t[j], self.k[j]

    bigR = _MixRing(pE.t + [pG, pH, pC] + pp.t, pE.k + ['pG', 'pH', 'pC'] + pp.k)

    for n, d_, in (('q', wq_d), ('k', wk_d), ('v', wv_d), ('z', wz_d), ('ba', wba_d)):
        R.add('pool', I('dma_start', out=W[n][:], in_=d_.rearrange("(kc p) c -> p kc c", p=128)), writes=['w_' + n], dma='w')
    for t_, d_, k_ in ((cw, cw_d, 'cw'), (alog, na_d, 'alog'), (dtb, dtb_d, 'dtb'), (gng, ng_d, 'gng'), (msk, msk_d, 'msk'),
                       (identb, idb_d, 'identb'), (chm, chm_d, 'chm')):
        R.add('sp', I('dma_start', out=t_[:], in_=d_), writes=[k_], dma='c')
    R.add('pool', I('memset', ap=ones[:], constant=1.0), writes=['ones'])
    R.add('pool', I('memset', ap=onesb[:], constant=1.0), writes=['onesb'])
    R.add('pool', I('memset', ap=halo[:], constant=0.0), writes=['halo'])
    for g in range(2):
        R.add('pool', I('memset', ap=S4[g][:], constant=0.0), writes=[f'S4_{g}'])
        R.add('pool', I('memset', ap=Sb4[g][:], constant=0.0), writes=[f'Sb4_{g}'])
        R.add('pool', I('memset', ap=vnew[g][:], constant=0.0), writes=[f'vnew{g}'])
    R.add('act', I('activation', out=alog[:], in_=alog[:], func=AF.Exp), reads=['alog'], writes=['alog'])
    R.add('dve', I('tensor_scalar', out=alog[:], in0=alog[:], scalar1=-1.0, scalar2=None, op0=ALU.mult), reads=['alog'], writes=['alog'])
    M_NEGT, M_POS, M_SNEG, M_ID, M_TRI, M_CH = range(6)

    def bc4(ap2d):
        return ap2d.unsqueeze(1).to_broadcast([128, 4, 128])

    def colbc(col4):
        return col4.unsqueeze(2).to_broadcast([128, 4, 128])

    fin = []
    eng_rr = [0]

    def evac(out, in_, pkey, okey, extra_reads=()):
        e = 'act'
        eng_rr[0] += 1
        if e == 'act':
            R.add('act', I('activation', out=out, in_=in_, func=AF.Copy), reads=list(extra_reads), writes=[pkey, okey])
        else:
            R.add('dve', I('tensor_copy', out=out, in_=in_), reads=list(extra_reads), writes=[pkey, okey])

    for sb_ in range(L // 512):
        bs = slice(sb_ * 512, (sb_ + 1) * 512)
        hT, hkey = hTb.next()
        for kc in range(8):
            r_ = sb_ // 4
            R.add('sp', I('dma_start', out=hT[:, kc, :], in_=h_d[hrow(r_, kc):hrow(r_, kc) + 128, (sb_ % 4) * 512:(sb_ % 4 + 1) * 512]),
                  writes=[hkey], dma='h' + hkey)

        jobs = ([('q', h, h) for h in range(4)] + [('k', h, 4 + h) for h in range(4)] + [('v', h, 8 + h) for h in range(8)]
                + [('z', h, None) for h in range(8)])
        stg = {}

        def stage_a(n):
            wname, h, ci = jobs[n]
            p, kp = pp.next()
            R.add('pe', [I('matmul', out=p[:], lhsT=W[wname][:, kc, h * 128:(h + 1) * 128], rhs=hT[:, kc, :], start=(kc == 0), stop=(kc == 7))
                         for kc in range(8)], reads=[hkey, 'w_' + wname], writes=[kp])
            if wname == 'z':
                R.add('act', I('activation', out=zs[:, h, :], in_=p[:], func=AF.Silu), writes=[kp, 'zs'])
                return
            x, kx = xs.next()
            R.add('pool', I('tensor_copy', out=x[:, 0:3], in_=halo[:, ci, 0:3]), reads=['halo'], writes=[kx])
            R.add('act', I('activation', out=x[:, 3:515], in_=p[:], func=AF.Copy), writes=[kp, kx])
            R.add('pool', I('tensor_copy', out=halo[:, ci, 0:3], in_=x[:, 512:515]), reads=[kx], writes=['halo'])
            stg[n] = (x, kx)

        def stage_b(n):
            wname, h, ci = jobs[n]
            if wname == 'z':
                return
            x, kx = stg[n]
            y, ky = cv.next()
            R.add('dve', I('tensor_scalar', out=y[:], in0=x[:, 3:515], scalar1=cw[:, ci, 3:4], scalar2=None, op0=ALU.mult),
                  reads=[kx, 'cw'], writes=[ky])
            for j in (2, 1, 0):
                R.add('dve', I('scalar_tensor_tensor', out=y[:], in0=x[:, j:j + 512], scalar=cw[:, ci, j:j + 1], in1=y[:],
                               op0=ALU.mult, op1=ALU.add), reads=[kx, 'cw'], writes=[ky])
            stg[n] = (y, ky)

        def stage_c(n):
            wname, h, ci = jobs[n]
            if wname == 'z':
                return
            y, ky = stg.pop(n)
            if wname == 'v':
                R.add('act', I('activation', out=vT[:, h, :], in_=y[:], func=AF.Silu), reads=[ky], writes=['vT'])
                return
            R.add('act', I('activation', out=y[:], in_=y[:], func=AF.Silu), writes=[ky])
            sq, ksq = sc.next()
            sqb, ksqb = sqbr.next()
            R.add('act', I('activation', out=sqb[:], in_=y[:], func=AF.Square), reads=[ky], writes=[ksqb])
            p, kp = pp.next()
            R.add('pe', I('matmul', out=p[:], lhsT=onesb[:], rhs=sqb[:], start=True, stop=True), reads=['onesb', ksqb], writes=[kp])
            R.add('act', I('activation', out=sq[:], in_=p[:], func=AF.Ln, bias=1e-6), writes=[kp, ksq])
            R.add('act', I('activation', out=sq[:], in_=sq[:], func=AF.Exp, scale=-0.5), writes=[ksq])
            dst, dkey, scale = (qT[:, h, :], 'qT', 128.0 ** -0.5) if wname == 'q' else (kT[:, h, :], 'kT', 1.0)
            R.add('dve', I('scalar_tensor_tensor', out=dst, in0=y[:], scalar=scale, in1=sq[:], op0=ALU.mult, op1=ALU.mult),
                  reads=[ky, ksq], writes=[dkey])

        nj = len(jobs)
        for n in range(nj + 2):
            if n < nj:
                stage_a(n)
            if 0 <= n - 1 < nj:
                stage_b(n - 1)
            if 0 <= n - 2 < nj:
                stage_c(n - 2)

        def prep(tt):
            ls = slice(tt * 128, (tt + 1) * 128)
            p, kp = pp.next()
            R.add('pe', [I('matmul', out=p[:, 0:16], lhsT=hT[:, kc, ls], rhs=W['ba'][:, kc, :], start=(kc == 0), stop=(kc == 7))
                         for kc in range(8)], reads=[hkey, 'w_ba'], writes=[kp])
            ba, kba = ba16.next()
            R.add('dve', I('tensor_copy', out=ba[:], in_=p[:, 0:16]), writes=[kp, kba])
            cc = {n: cols[n].next() for n in cols}
            beta, kbeta = cc['beta']
            R.add('act', I('activation', out=beta[:], in_=ba[:, 0:8], func=AF.Exp, scale=-1.0), reads=[kba], writes=[kbeta])
            R.add('dve', I('tensor_scalar', out=beta[:], in0=beta[:], scalar1=1.0, scalar2=None, op0=ALU.add), writes=[kbeta])
            R.add('dve', I('reciprocal', out=beta[:], in_=beta[:]), writes=[kbeta])
            nbeta, knbeta = cc['nbeta']
            R.add('dve', I('tensor_scalar', out=nbeta[:], in0=beta[:], scalar1=-1.0, scalar2=None, op0=ALU.mult), reads=[kbeta], writes=[knbeta])
            ax, kax = cc['ax']
            gg, kgg = cc['g']
            R.add('dve', I('tensor_tensor', out=ax[:], in0=ba[:, 8:16], in1=dtb[:], op=ALU.add), reads=[kba, 'dtb'], writes=[kax])
            R.add('act', I('activation', out=gg[:], in_=ax[:], func=AF.Abs), reads=[kax], writes=[kgg])
            R.add('act', I('activation', out=gg[:], in_=gg[:], func=AF.Exp, scale=-1.0), writes=[kgg])
            R.add('act', I('activation', out=gg[:], in_=gg[:], func=AF.Ln, bias=1.0), writes=[kgg])
            R.add('dve', I('scalar_tensor_tensor', out=gg[:], in0=ax[:], scalar=0.0, in1=gg[:], op0=ALU.max, op1=ALU.add),
                  reads=[kax], writes=[kgg])
            R.add('dve', I('tensor_tensor', out=gg[:], in0=gg[:], in1=alog[:], op=ALU.mult), reads=['alog'], writes=[kgg])
            p, kp = pp.next()
            R.add('pe', [I('matmul', out=p[:, 0:8], lhsT=msk[:, M_TRI, :], rhs=gg[:], start=True, stop=True),
                         I('matmul', out=p[:, 8:16], lhsT=msk[:, M_CH, :], rhs=gg[:], start=True, stop=True)],
                  reads=['msk', kgg], writes=[kp])
            bcol, kb = cc['b']
            blc, kbl = cc['bl']
            R.add('dve', I('tensor_copy', out=bcol[:], in_=p[:, 0:8]), writes=[kp, kb])
            R.add('dve', I('tensor_tensor', out=blc[:], in0=p[:, 8:16], in1=bcol[:], op=ALU.subtract), reads=[kb], writes=[kp, kbl])
            ebc, keb = cc['eb']
            R.add('act', I('activation', out=ebc[:], in_=bcol[:], func=AF.Exp), reads=[kb], writes=[keb])
            bexp, kbexp = cc['bexp']
            R.add('dve', I('tensor_tensor', out=bexp[:], in0=ebc[:], in1=beta[:], op=ALU.mult), reads=[keb, kbeta], writes=[kbexp])
            eblb, keblb = cc['eblb']
            R.add('act', I('activation', out=eblb[:], in_=blc[:], func=AF.Exp), reads=[kbl], writes=[keblb])
            e01 = []
            for c in range(2):
                e_, ke_ = cc[f'e{c}']
                R.add('dve', I('tensor_scalar', out=e_[:], in0=eblb[:], scalar1=chm[:, c:c + 1], scalar2=None, op0=ALU.mult),
                      reads=[keblb, 'chm'], writes=[ke_])
                e01.append((e_, ke_))
            br, kbr = brep.next()
            bn, kbn = bns.next()
            for (src, ksrc, dst, kdst) in ((bcol, kb, br, kbr), (beta, kbeta, bn, kbn)):
                for g in range(2):
                    d_, kd_ = dg.next()
                    R.add('dve', I('tensor_tensor', out=d_[:].rearrange("p (h t) -> p h t", h=4), in0=bc4(msk[:, M_ID, :]),
                                   in1=colbc(src[:, g * 4:(g + 1) * 4]), op=ALU.mult), reads=['msk', ksrc], writes=[kd_])
                    p, kp = pp.next()
                    R.add('pe', I('matmul', out=p[:], lhsT=ones[:], rhs=d_[:], start=True, stop=True), reads=['ones', kd_], writes=[kp])
                    evac(dst[:, g * 4:(g + 1) * 4, :], p[:].rearrange("p (h t) -> p h t", h=4), kp, kdst)
            er, ker = ebrep.next()
            R.add('act', I('activation', out=er[:], in_=br[:], func=AF.Exp), reads=[kbr], writes=[ker])
            for g in range(2):
                R.add('pool', I('tensor_tensor', out=bn[:, g * 4:(g + 1) * 4, :], in0=bn[:, g * 4:(g + 1) * 4, :],
                                in1=bc4(msk[:, M_SNEG, :]), op=ALU.mult), reads=['msk'], writes=[kbn])
            return dict(ls=ls, bcol=bcol, kb=kb, beta=beta, kbeta=kbeta, nbeta=nbeta, knbeta=knbeta, bexp=bexp, kbexp=kbexp,
                        e01=e01, br=br, kbr=kbr, bn=bn, kbn=kbn, er=er, ker=ker)

        for tt in range(4):
            cur_ = prep(tt)
            ls, bcol, kb, beta, kbeta, nbeta, knbeta = (cur_[k_] for k_ in ('ls', 'bcol', 'kb', 'beta', 'kbeta', 'nbeta', 'knbeta'))
            bexp, kbexp, e01, br, kbr, bn, kbn, er, ker = (cur_[k_] for k_ in ('bexp', 'kbexp', 'e01', 'br', 'kbr', 'bn', 'kbn', 'er', 'ker'))
            grp = []
            for g in range(2):
                hv4 = slice(g * 4, (g + 1) * 4)
                mm = []
                for kk_ in range(2):
                    kh = 2 * g + kk_
                    mm.append(I('matmul', out=pC[:, kk_ * 128:(kk_ + 1) * 128], lhsT=kT[:, kh, ls], rhs=kT[:, kh, ls], start=True, stop=True))
                    mm.append(I('matmul', out=pC[:, 256 + kk_ * 128:256 + (kk_ + 1) * 128], lhsT=kT[:, kh, ls], rhs=qT[:, kh, ls],
                                start=True, stop=True))
                R.add('pe', mm, reads=['kT', 'qT'], writes=['pC'])
                kk_bc = pC[:, 0:256].rearrange("p (k t) -> p k t", k=2).unsqueeze(2).to_broadcast([128, 2, 2, 128])
                qk_bc = pC[:, 256:512].rearrange("p (k t) -> p k t", k=2).unsqueeze(2).to_broadcast([128, 2, 2, 128])

                def v4(t):
                    return t.rearrange("p (k j) t -> p k j t", k=2)

                Dm, kDm = f4.next()
                R.add('dve', I('tensor_tensor', out=Dm[:], in0=br[:, hv4, :], in1=colbc(bcol[:, hv4]), op=ALU.subtract),
                      reads=[kbr, kb], writes=[kDm])
                E1, kE1 = f4.next()
                R.add('dve', I('tensor_tensor', out=E1[:], in0=Dm[:], in1=bc4(msk[:, M_NEGT, :]), op=ALU.add), reads=[kDm, 'msk'], writes=[kE1])
                R.add('act', I('activation', out=E1[:], in_=E1[:], func=AF.Exp), writes=[kE1])
                E2, kE2 = f4.next()
                R.add('pool', I('tensor_tensor', out=E2[:], in0=Dm[:], in1=bc4(msk[:, M_POS, :]), op=ALU.add), reads=[kDm, 'msk'], writes=[kE2])
                R.add('act', I('activation', out=E2[:], in_=E2[:], func=AF.Exp, scale=-1.0), writes=[kE2])
                QK, kQK = ded[('QK', g)]
                R.add('dve', I('tensor_tensor', out=v4(QK[:]), in0=qk_bc, in1=v4(E1[:]), op=ALU.mult), reads=[kE1], writes=['pC', kQK])
                tmp, ktmp = f4.next()
                R.add('dve', I('tensor_tensor', out=v4(tmp[:]), in0=kk_bc, in1=v4(E1[:]), op=ALU.mult), reads=[kE1], writes=['pC', ktmp])
                NT, kNT = b4.next()
                R.add('pool', I('tensor_tensor', out=NT[:], in0=tmp[:], in1=bn[:, hv4, :], op=ALU.mult), reads=[ktmp, kbn], writes=[kNT])
                AT, kAT = b4.next()
                R.add('pool', I('tensor_tensor', out=AT[:], in0=NT[:], in1=bc4(msk[:, M_ID, :]), op=ALU.add), reads=[kNT, 'msk'], writes=[kAT])
                t2, kt2 = f4.next()
                R.add('dve', I('tensor_tensor', out=v4(t2[:]), in0=kk_bc, in1=v4(colbc(nbeta[:, hv4])), op=ALU.mult),
                      reads=[knbeta], writes=['pC', kt2])
                N_, kN = b4.next()
                R.add('pool', I('tensor_tensor', out=N_[:], in0=t2[:], in1=E2[:], op=ALU.mult), reads=[kt2, kE2], writes=[kN])
                mm = [I('transpose', out=pD[:, kk_ * 128:(kk_ + 1) * 128], in_=kT[:, 2 * g + kk_, ls], identity=identb[:]) for kk_ in range(2)]
                mm += [I('transpose', out=pD[:, 256 + j * 128:256 + (j + 1) * 128], in_=vT[:, g * 4 + j, ls], identity=identb[:]) for j in range(4)]
                R.add('pe', mm, reads=['kT', 'vT', 'identb'], writes=['pD'])
                ktm_bc = pD[:, 0:256].rearrange("p (k t) -> p k t", k=2).unsqueeze(2).to_broadcast([128, 2, 2, 128])
                vb, kvb = ded[('vb', g)]
                R.add('dve', I('tensor_tensor', out=vb[:], in0=pD[:, 256:768].rearrange("p (h t) -> p h t", h=4),
                               in1=colbc(beta[:, hv4]), op=ALU.mult), reads=[kbeta], writes=['pD', kvb])
                kbe, kkbe = ded[('kbe', g)]
                R.add('dve', I('tensor_tensor', out=v4(kbe[:]), in0=ktm_bc, in1=v4(colbc(bexp[:, hv4])), op=ALU.mult),
                      reads=[kbexp], writes=['pD', kkbe])
                kst = []
                for c in range(2):
                    k_, kk2 = ded[(f'kst{c}', g)]
                    R.add('dve', I('tensor_tensor', out=v4(k_[:]), in0=ktm_bc, in1=v4(colbc(e01[c][0][:, hv4])), op=ALU.mult),
                          reads=[e01[c][1]], writes=['pD', kk2])
                    kst.append((k_, kk2))
                qd, kqd = ded[('qd', g)]
                R.add('pool', I('tensor_tensor', out=v4(qd[:]), in0=qT[:, 2 * g:2 * g + 2, ls].unsqueeze(2).to_broadcast([128, 2, 2, 128]),
                                in1=v4(er[:, hv4, :]), op=ALU.mult), reads=['qT', ker], writes=[kqd])
                grp.append(dict(N=(N_, kN), NT=(NT, kNT), AT=(AT, kAT), QK=(QK, kQK), vb=(vb, kvb), kbe=(kbe, kkbe), kst=kst, qd=(qd, kqd)))

            def r4(t):
                return t[:].rearrange("p (h t) -> p h t", h=4)

            for lev in range(5):
                st_ = []
                for g in range(2):
                    G_ = grp[g]
                    (N_, kN), (NT, kNT) = G_['N'], G_['NT']
                    p1, kp1 = bigR.next()
                    R.add('pe', [I('matmul', out=p1[:, j * 128:(j + 1) * 128], lhsT=NT[:, j, :], rhs=N_[:, j, :], start=True, stop=True)
                                 for j in range(4)], reads=[kN, kNT], writes=[kp1])
                    p2 = kp2 = None
                    if lev < 4:
                        p2, kp2 = bigR.next()
                        R.add('pe', [I('matmul', out=p2[:, j * 128:(j + 1) * 128], lhsT=N_[:, j, :], rhs=NT[:, j, :], start=True, stop=True)
                                     for j in range(4)], reads=[kN, kNT], writes=[kp2])
                    st_.append((p1, kp1, p2, kp2))
                n2s = []
                for g in range(2):
                    p1, kp1, p2, kp2 = st_[g]
                    N2, kN2 = b4.next()
                    R.add('act', I('activation', out=N2[:], in_=r4(p1), func=AF.Copy), writes=[kp1, kN2])
                    n2s.append((N2, kN2))
                for g in range(2):
                    p1, kp1, p2, kp2 = st_[g]
                    if p2 is not None:
                        NT2, kNT2 = b4.next()
                        R.add('act', I('activation', out=NT2[:], in_=r4(p2), func=AF.Copy), writes=[kp2, kNT2])
                        grp[g]['NT'] = (NT2, kNT2)
                pas = []
                for g in range(2):
                    (AT, kAT) = grp[g]['AT']
                    N2, kN2 = n2s[g]
                    p3, kp3 = bigR.next()
                    R.add('pe', [I('matmul', out=p3[:, j * 128:(j + 1) * 128], lhsT=N2[:, j, :], rhs=AT[:, j, :], start=True, stop=True)
                                 for j in range(4)], reads=[kN2, kAT], writes=[kp3])
                    pas.append((p3, kp3))
                for g in range(2):
                    (AT, kAT) = grp[g]['AT']
                    p3, kp3 = pas[g]
                    AT2, kAT2 = b4.next()
                    R.add('dve', I('tensor_tensor', out=AT2[:], in0=r4(p3), in1=AT[:], op=ALU.add), reads=[kAT], writes=[kp3, kAT2])
                    grp[g]['N'] = n2s[g]
                    grp[g]['AT'] = (AT2, kAT2)

            pu_ = []
            for g in range(2):
                (AT, kAT), (vb, kvb) = grp[g]['AT'], grp[g]['vb']
                p, kp = bigR.next()
                R.add('pe', [I('matmul', out=p[:, j * 128:(j + 1) * 128], lhsT=AT[:, j, :], rhs=vb[:, j, :], start=True, stop=True)
                             for j in range(4)], reads=[kAT, kvb], writes=[kp])
                pu_.append((p, kp))
            pw_ = []
            for g in range(2):
                (AT, kAT), (kbe, kkbe) = grp[g]['AT'], grp[g]['kbe']
                p, kp = bigR.next()
                R.add('pe', [I('matmul', out=p[:, j * 128:(j + 1) * 128], lhsT=kbe[:, j, :], rhs=AT[:, j, :], start=True, stop=True)
                             for j in range(4)], reads=[kAT, kkbe], writes=[kp])
                pw_.append((p, kp))
            for g in range(2):
                U, kU = usb.next()
                R.add('act', I('activation', out=U[:], in_=r4(pu_[g][0]), func=AF.Copy), writes=[pu_[g][1], kU])
                grp[g]['U'] = (U, kU)
            for g in range(2):
                WT, kWT = ded[('WT', g)]
                R.add('act', I('activation', out=WT[:], in_=r4(pw_[g][0]), func=AF.Copy), writes=[pw_[g][1], kWT])
                grp[g]['WT'] = (WT, kWT)

            for c in range(2):
                rows = slice(c * 64, (c + 1) * 64)
                ccols = slice(c * 64, (c + 1) * 64)
                ws = []
                for g in range(2):
                    (WT, kWT) = grp[g]['WT']
                    p, kp = bigR.next()
                    R.add('pe', [I('matmul', out=p[:, j * 128:(j + 1) * 128], lhsT=WT[:, j, :], rhs=Sb4[g][:, j, :], start=True, stop=True)
                                 for j in range(4)], reads=[kWT, f'Sb4_{g}'], writes=[kp])
                    ws.append((p, kp))
                for g in range(2):
                    (U, kU) = grp[g]['U']
                    p, kp = ws[g]
                    R.add('dve', I('tensor_tensor', out=vnew[g][rows, :, :], in0=U[rows, :, :],
                                   in1=p[rows, :].rearrange("p (h t) -> p h t", h=4), op=ALU.subtract),
                          reads=[kU], writes=[kp, f'vnew{g}'])
                po_, ps_ = [], []
                for g in range(2):
                    (QK, kQK), (qd, kqd) = grp[g]['QK'], grp[g]['qd']
                    (ks_, kks) = grp[g]['kst'][c]
                    p, kp = bigR.next()
                    mm = []
                    for j in range(4):
                        oc = slice(j * 64, (j + 1) * 64)
                        mm.append(I('matmul', out=p[:, oc], lhsT=Sb4[g][:, j, :], rhs=qd[:, j, ccols], start=True, stop=False))
                        mm.append(I('matmul', out=p[:, oc], lhsT=vnew[g][:, j, :], rhs=QK[:, j, ccols], start=False, stop=True))
                    R.add('pe', mm, reads=[f'Sb4_{g}', kqd, f'vnew{g}', kQK], writes=[kp])
                    po_.append((p, kp))
                    p, kp = bigR.next()
                    R.add('pe', [I('matmul', out=p[:, j * 128:(j + 1) * 128], lhsT=ks_[:, j, :], rhs=vnew[g][:, j, :], start=True, stop=True)
                                 for j in range(4)], reads=[kks, f'vnew{g}'], writes=[kp])
                    ps_.append((p, kp))
                for g in range(2):
                    hv4 = slice(g * 4, (g + 1) * 4)
                    dl = er[:, hv4, c * 64 + 63:c * 64 + 64].to_broadcast([128, 4, 128])
                    R.add('dve', I('tensor_tensor', out=S4[g][:], in0=S4[g][:], in1=dl, op=ALU.mult), reads=[ker], writes=[f'S4_{g}'])
                    R.add('dve', I('tensor_tensor', out=S4[g][:], in0=S4[g][:], in1=r4(ps_[g][0]), op=ALU.add),
                          writes=[ps_[g][1], f'S4_{g}'])
                    R.add('act', I('activation', out=Sb4[g][:], in_=S4[g][:], func=AF.Copy), reads=[f'S4_{g}'], writes=[f'Sb4_{g}'])
                for g in range(2):
                    hv4 = slice(g * 4, (g + 1) * 4)
                    R.add('act', I('activation', out=oblk[:, hv4, tt * 128 + c * 64:tt * 128 + (c + 1) * 64],
                                   in_=po_[g][0][:, 0:256].rearrange("p (h t) -> p h t", h=4), func=AF.Copy), writes=[po_[g][1], 'oblk'])
        for h in range(8):
            sq, ksq = sc.next()
            sqb, ksqb = sqbr.next()
            R.add('act', I('activation', out=sqb[:], in_=oblk[:, h, :], func=AF.Square), reads=['oblk'], writes=[ksqb])
            p, kp = pp.next()
            R.add('pe', I('matmul', out=p[:], lhsT=onesb[:], rhs=sqb[:], start=True, stop=True), reads=[ksqb, 'onesb'], writes=[kp])
            R.add('act', I('activation', out=sq[:], in_=p[:], func=AF.Ln, scale=1.0 / 128, bias=EPS), writes=[kp, ksq])
            R.add('act', I('activation', out=sq[:], in_=sq[:], func=AF.Exp, scale=-0.5), writes=[ksq])
            R.add('dve', I('scalar_tensor_tensor', out=sq[:], in0=oblk[:, h, :], scalar=gng[:, 0:1], in1=sq[:],
                           op0=ALU.mult, op1=ALU.mult), reads=['oblk', 'gng'], writes=[ksq])
            of, kof = ofin.next()
            R.add('dve', I('tensor_tensor', out=of[:], in0=sq[:], in1=zs[:, h, :], op=ALU.mult), reads=[ksq, 'zs'], writes=[kof])
            fin.append(R.add('sp', I('dma_start', out=o_d[h * 128:(h + 1) * 128, bs], in_=of[:]), reads=[kof], dma='oout'))
    return C.finish()


def gdn_inputs(inp, hf, hT_full):
    j = 0
    w = inp['gdn_w_in'][j]
    kq0 = 512 * hf
    v0 = 1024 * hf
    cwf = inp['gdn_conv_w'][j]
    chans = np.concatenate([np.arange(kq0, kq0 + 512), 1024 + np.arange(kq0, kq0 + 512), 2048 + np.arange(v0, v0 + 1024)])
    cw = cwf[:, chans].reshape(4, 16, 128).transpose(2, 1, 0)
    s = np.arange(128)
    same = (s[:, None] // 64 == s[None, :] // 64)
    le = s[:, None] <= s[None, :]
    lt = s[:, None] < s[None, :]
    masks = np.zeros((128, 6, 128), np.float32)
    masks[:, 0, :] = np.where(same & le, 0.0, NEG)
    masks[:, 1, :] = np.where(same & (s[None, :] < s[:, None]), 0.0, -NEG)
    masks[:, 2, :] = np.where(same & lt, -1.0, 0.0)
    masks[:, 3, :] = np.eye(128)
    masks[:, 4, :] = (same & le).astype(np.float32)
    masks[:, 5, :] = same.astype(np.float32)
    return dict(hT=hT_full, w_q=np.ascontiguousarray(w[:, kq0:kq0 + 512]), w_k=np.ascontiguousarray(w[:, 1024 + kq0:1024 + kq0 + 512]),
                w_v=np.ascontiguousarray(w[:, 2048 + v0:2048 + v0 + 1024]), w_z=np.ascontiguousarray(w[:, 4096 + v0:4096 + v0 + 1024]),
                w_ba=np.ascontiguousarray(np.concatenate([w[:, 6144 + 8 * hf:6144 + 8 * hf + 8], w[:, 6160 + 8 * hf:6160 + 8 * hf + 8]], axis=1)),
                convw=np.ascontiguousarray(cw.astype(np.float32)),
                alog=np.ascontiguousarray(np.broadcast_to(inp['gdn_a_log'][j][8 * hf:8 * hf + 8][None, :], (128, 8)).astype(np.float32)),
                dtb=np.ascontiguousarray(np.broadcast_to(inp['gdn_dt_bias'][j][8 * hf:8 * hf + 8][None, :], (128, 8)).astype(np.float32)),
                gng=np.ascontiguousarray(inp['gdn_norm_g'][j].reshape(128, 1).astype(np.float32)),
                masks=masks, identb=np.eye(128, dtype=np.float32).astype(ml_dtypes.bfloat16),
                chm2=(np.arange(128)[:, None] // 64 == np.arange(2)[None, :]).astype(np.float32))


_PROG = {}


def _prog(key, fn):
    if key not in _PROG:
        _PROG[key] = fn()
    return _PROG[key]


def kernel_unfused(**inp):
    inp = {k: np.asarray(v) for k, v in inp.items()}
    x = inp['x']
    ng = inp['norm_g']
    f32 = np.float32

    def ffn_w(i, j, f):
        return {f'wg{f}': inp['ffn_w_gate'][i, j], f'wu{f}': inp['ffn_w_up'][i, j], f'wd{f}': inp['ffn_w_down'][i, j]}

    nc = _prog('tp0', lambda: build_token_phase(dict(mix=None, n_ffn=1, tail='h')))
    maps = []
    for c in range(8):
        b, hf = c // 2, c % 2
        m = dict(xT=np.ascontiguousarray(x[b, hf * TOK:(hf + 1) * TOK].T.astype(f32)), ng0=fm(ng[0, 0]), ngt=fm(ng[0, 1]))
        m.update(ffn_w(0, 0, 0))
        maps.append(m)
    res = run(nc, maps)
    xs = [r['xo'] for r in res]
    hs = [r['ho'] for r in res]

    mixers = [
        ('s5', build_s5, s5_inputs, 'ygT', 'glu', 1024, inp['s5_w_glu'][0]),
        ('hg', lambda: build_hgrn(1), hgrn_inputs, 'oT', 'lin', 1024, inp['hg_w_out'][0]),
        ('gdn', build_gdn, gdn_inputs, 'oT', 'lin', 2048, inp['gdn_w_out'][0]),
        ('moba', build_moba, moba_inputs, 'oT', 'lin', 1024, inp['moba_w_out'][0]),
    ]
    out = None
    for i, (name, bfn, ifn, okey, mixkind, mix_k, w_mix) in enumerate(mixers):
        ncm = _prog(name, bfn)
        maps = []
        for c in range(8):
            b, hf = c // 2, c % 2
            hT_full = h_gather_layout(hs[2 * b], hs[2 * b + 1])
            if name == 's5':
                maps.append(ifn(inp, b, hf, hT_full))
            else:
                maps.append(ifn(inp, hf, hT_full))
        res = run(ncm, maps)
        os_ = [r[okey] for r in res]
        last = (i == 3)
        cfg = dict(mix=mixkind, mix_k=mix_k, n_ffn=(1 if last else 2), tail=('final' if last else 'h'))
        nct = _prog(('tp', mixkind, mix_k, last), lambda: build_token_phase(cfg))
        maps = []
        for c in range(8):
            b, hf = c // 2, c % 2
            oT = o_gather_layout(os_[2 * b], os_[2 * b + 1])
            m = dict(xT=xs[c], oT=oT, hfm=hfm_arr(hf), w_mix=w_mix, ng0=fm(ng[i, 2]))
            m.update(ffn_w(i, 1, 0))
            if last:
                m['ngt'] = fm(inp['final_norm_g'])
            else:
                m['ng1'] = fm(ng[i + 1, 0])
                m['ngt'] = fm(ng[i + 1, 1])
                m.update(ffn_w(i + 1, 0, 1))
            maps.append(m)
        res = run(nct, maps)
        if last:
            out = np.zeros((NB, L, D), f32)
            for c in range(8):
                b, hf = c // 2, c % 2
                out[b, hf * TOK:(hf + 1) * TOK] = res[c]['out'].T
        else:
            xs = [r['xo'] for r in res]
            hs = [r['ho'] for r in res]
    return out


RG = [[0, 1], [2, 3], [4, 5], [6, 7]]
MIXERS = [('s5', 'ygT', 512, 'glu', 1024), ('hg', 'oT', 512, 'lin', 1024), ('gdn', 'oT', 1024, 'lin', 2048), ('moba', 'oT', 512, 'lin', 1024)]


def _mixer_build(name, **kw):
    if name == 's5':
        return build_s5(**kw)
    if name == 'hg':
        return build_hgrn(1, **kw)
    if name == 'gdn':
        return build_gdn(**kw)
    return build_moba(**kw)


def build_fused(stop=99):
    P = Prog()
    nc, R = P.nc, P.rec
    x_in = nc.dram_tensor("xT_in", [D, TOK], F32, kind="ExternalInput").ap()
    out = nc.dram_tensor("out", [D, TOK], F32, kind="ExternalOutput").ap()
    x_scr = nc.dram_tensor("x_scr", [D, TOK], F32)
    h_loc = nc.dram_tensor("h_loc", [D, TOK], BF16)
    h_all = nc.dram_tensor("h_all", [2 * D, TOK], BF16)
    o_loc = {n: nc.dram_tensor(f"o_loc{n}", [n, L], BF16) for n in (512, 1024)}
    o_all = {n: nc.dram_tensor(f"o_all{n}", [n, L], BF16) for n in (1024, 2048)}

    def ag(src, dst):
        rows, cols = src.ap().shape
        pr = (2 * 1024 * 1024) // (cols * 2)
        for p in range(rows // pr):
            R.add('pool', I('collective_compute', kind="AllGather", op=ALU.bypass, replica_groups=RG,
                            ins=[src.ap()[p * pr:(p + 1) * pr, :].opt()], outs=[dst.ap()[2 * p * pr:2 * (p + 1) * pr, :].opt()]),
                  dma='cc', inc=1)
        R.flush()

    build_token_phase(dict(mix=None, n_ffn=1, tail='h'), P, 't0_', {'xT': x_in, 'xo': x_scr.ap(), 'ho': h_loc.ap()})
    step = [0]

    def done():
        step[0] += 1
        return step[0] >= stop

    if done():
        R.flush(final=True)
        return P
    ag(h_loc, h_all)
    if done():
        R.flush(final=True)
        return P
    for i, (name, okey, half, mixkind, mix_k) in enumerate(MIXERS):
        _mixer_build(name, P=P, prefix=f'm{i}_', bind={'hT': h_all.ap(), okey: o_loc[half].ap()})
        if done():
            break
        ag(o_loc[half], o_all[2 * half])
        if done():
            break
        last = (i == 3)
        cfg = dict(mix=mixkind, mix_k=mix_k, n_ffn=(1 if last else 2), tail=('final' if last else 'h'))
        bind = {'xT': x_scr.ap(), 'oT': o_all[2 * half].ap()}
        if last:
            bind['out'] = out
        else:
            bind['xo'] = x_scr.ap()
            bind['ho'] = h_loc.ap()
        build_token_phase(cfg, P, f't{i + 1}_', bind)
        if done():
            break
        if not last:
            ag(h_loc, h_all)
            if done():
                break
    R.flush(final=True)
    return P


def fused_inputs(inp, c):
    b, hf = c // 2, c % 2
    ng = inp['norm_g']
    f32 = np.float32
    m = {'xT_in': np.ascontiguousarray(inp['x'][b, hf * TOK:(hf + 1) * TOK].T.astype(f32))}

    def put(prefix, d):
        for k, v in d.items():
            m[prefix + k] = v

    def ffn_w(i, j, f):
        return {f'wg{f}': inp['ffn_w_gate'][i, j], f'wu{f}': inp['ffn_w_up'][i, j], f'wd{f}': inp['ffn_w_down'][i, j]}

    d = dict(ng0=fm(ng[0, 0]), ngt=fm(ng[0, 1]))
    d.update(ffn_w(0, 0, 0))
    put('t0_', d)
    wmix = [inp['s5_w_glu'][0], inp['hg_w_out'][0], inp['gdn_w_out'][0], inp['moba_w_out'][0]]
    for i, (name, okey, half, mixkind, mix_k) in enumerate(MIXERS):
        if name == 's5':
            d = s5_inputs(inp, b, hf, None)
        elif name == 'hg':
            d = hgrn_inputs(inp, hf, None)
        elif name == 'gdn':
            d = gdn_inputs(inp, hf, None)
        else:
            d = moba_inputs(inp, hf, None)
        d.pop('hT')
        put(f'm{i}_', d)
        last = (i == 3)
        d = dict(hfm=hfm_arr(hf), w_mix=wmix[i], ng0=fm(ng[i, 2]))
        d.update(ffn_w(i, 1, 0))
        if last:
            d['ngt'] = fm(inp['final_norm_g'])
        else:
            d['ng1'] = fm(ng[i + 1, 0])
            d['ngt'] = fm(ng[i + 1, 1])
            d.update(ffn_w(i + 1, 0, 1))
        put(f't{i + 1}_', d)
    return m


def kernel(**inp):
    inp = {k: np.asarray(v) for k, v in inp.items()}
    P = _prog('fused', build_fused)
    maps = []
    for c in range(8):
        m = fused_inputs(inp, c)
        assert set(m.keys()) == set(P.ext.keys()) | {'xT_in'}, (sorted(set(m.keys()) ^ (set(P.ext.keys()) | {'xT_in'})))
        maps.append(m)
    res = run(P.nc, maps)
    out = np.zeros((NB, L, D), np.float32)
    for c in range(8):
        b, hf = c // 2, c % 2
        out[b, hf * TOK:(hf + 1) * TOK] = res[c]['out'].T
    return out
```

```python
import math
from contextlib import ExitStack
import numpy as np
import ml_dtypes
import concourse.bass as bass
import concourse.mybir as mybir
from concourse.bass_utils import run_bass_kernel_spmd

F32 = mybir.dt.float32
BF16 = mybir.dt.bfloat16
AF = mybir.ActivationFunctionType
ALU = mybir.AluOpType
AX = mybir.AxisListType

D = 1024
DFF = 2816
L = 4096
NB = 4
TOK = 2048
EPS = 1e-6
SAME_ENG_SYNC = True


def I(name, **kw):
    return (name, kw)


class Rec:
    def __init__(self, nc):
        self.nc = nc
        self.ops = []
        self.count = {}
        self.last_w = {}
        self.readers = {}
        self.waited = {}

    def _deps(self, eng, reads, writes):
        deps = {}

        def need(tok):
            if tok is None:
                return
            s, v = tok
            if s.startswith('dma_'):
                v = self.count[s]
            if s == eng and (eng == 'pe' or not SAME_ENG_SYNC):
                return
            if deps.get(s, 0) < v:
                deps[s] = v

        for k in reads:
            need(self.last_w.get(k))
        for k in writes:
            need(self.last_w.get(k))
            for t in self.readers.get(k, ()):
                need(t)
        out = []
        for s, v in deps.items():
            if self.waited.get((eng, s), 0) < v:
                self.waited[(eng, s)] = v
                out.append((s, v))
        return out

    def add(self, eng, fns, reads=(), writes=(), dma=None, inc=16):
        if not isinstance(fns, list):
            fns = [fns]
        waits = self._deps(eng, reads, writes)
        if dma is not None:
            sem = 'dma_' + dma
            self.count[sem] = self.count.get(sem, 0) + inc
        else:
            sem = eng
            self.count[sem] = self.count.get(sem, 0) + 1
        tok = (sem, self.count[sem])
        for k in reads:
            self.readers.setdefault(k, []).append(tok)
        for k in writes:
            self.last_w[k] = tok
            self.readers[k] = []
        self.ops.append((eng, list(fns), waits, sem, (inc if dma is not None else 0)))
        return tok

    def flush(self, final=False):
        nc = self.nc
        if not hasattr(self, 'sems'):
            self.sems = {}
            self.semstack = ExitStack()
            self.barrier_tot = {}
        for n in sorted(self.count.keys()):
            if n not in self.sems:
                self.sems[n] = self.semstack.enter_context(nc.semaphore('s_' + n))
        sems = self.sems
        barrier = sorted(self.barrier_tot.items())
        totals = sorted(self.count.items())
        per = {e: [] for e in ('pe', 'dve', 'act', 'pool', 'sp')}
        for op in self.ops:
            per[op[0]].append(op)
        with nc.Block() as block:
            def run(e, ops, fin):
                for (s, v) in barrier:
                    e.wait_ge(sems[s], v)
                for (_, fns, waits, sem, is_dma) in ops:
                    for (s, v) in waits:
                        e.wait_ge(sems[s], v)
                    ins = None
                    for f in fns:
                        ins = getattr(e, f[0])(**f[1]) if isinstance(f, tuple) else f(e)
                    ins.then_inc(sems[sem], is_dma if is_dma else 1)
                if fin:
                    for (s, v) in totals:
                        e.wait_ge(sems[s], v)

            @block.tensor
            def _(e):
                run(e, per['pe'], False)

            @block.vector
            def _(e):
                run(e, per['dve'], False)

            @block.scalar
            def _(e):
                run(e, per['act'], False)

            @block.gpsimd
            def _(e):
                run(e, per['pool'], False)

            @block.sync
            def _(e):
                run(e, per['sp'], final)
        self.barrier_tot = dict(self.count)
        self.ops = []
        self.last_w = {}
        self.readers = {}
        for e in per:
            for sname, v in self.count.items():
                self.waited[(e, sname)] = v

    def emit(self, final_waits):
        self.flush(final=True)


class Prog:
    def __init__(self):
        self.nc = bass.Bass("TRN2", target_bir_lowering=False)
        self.rec = Rec(self.nc)
        self.ext = {}

    def scratch(self, name, shape, dt):
        return self.nc.dram_tensor(name, list(shape), dt)


class Ctx:
    def __init__(self, P=None, prefix='', bind=None):
        self.P = P or Prog()
        self.standalone = P is None
        self.nc = self.P.nc
        self.rec = self.P.rec
        self.prefix = prefix
        self.bind = bind or {}
        self.st = ExitStack()
        self.n = 0

    def sb(self, shape, dt, name=None):
        self.n += 1
        return self.st.enter_context(self.nc.sbuf_tensor(f"sb_{self.prefix}{name or self.n}", list(shape), dt))

    def ps(self, shape, dt=F32, name=None):
        self.n += 1
        return self.st.enter_context(self.nc.psum_tensor(f"ps_{self.prefix}{name or self.n}", list(shape), dt))

    def dram_in(self, name, shape, dt=F32):
        if name in self.bind:
            return self.bind[name]
        self.P.ext[self.prefix + name] = (self.prefix, name)
        return self.nc.dram_tensor(self.prefix + name, list(shape), dt, kind="ExternalInput").ap()

    def dram_out(self, name, shape, dt=F32):
        if name in self.bind:
            return self.bind[name]
        return self.nc.dram_tensor(self.prefix + name, list(shape), dt, kind="ExternalOutput").ap()

    def scratch(self, name, shape, dt):
        return self.nc.dram_tensor(self.prefix + name, list(shape), dt)

    def finish(self):
        self.rec.flush(final=self.standalone)
        self.st.close()
        return self.nc


class Ring:
    def __init__(self, C, n, shape, dt, name, psum=False):
        self.t = [(C.ps(shape, dt, f"{name}{i}") if psum else C.sb(shape, dt, f"{name}{i}")) for i in range(n)]
        self.k = [f"{name}{i}" for i in range(n)]
        self.i = 0

    def next(self):
        j = self.i % len(self.t)
        self.i += 1
        return self.t[j], self.k[j]


def build_token_phase(cfg, P=None, prefix='', bind=None):
    C = Ctx(P, prefix, bind)
    nc, R = C.nc, C.rec
    mix, mix_k, n_ffn, tail = cfg['mix'], cfg.get('mix_k', 0), cfg['n_ffn'], cfg['tail']
    NT = TOK // 512

    x_d = C.dram_in("xT", [D, TOK])
    if mix:
        o_d = C.dram_in("oT", [mix_k, L], BF16)
        hfm_d = C.dram_in("hfm", [128, 2])
        wm_d = C.dram_in("w_mix", [mix_k, 2048 if mix == 'glu' else 1024])
    wg_d = [C.dram_in(f"wg{f}", [D, DFF]) for f in range(n_ffn)]
    wu_d = [C.dram_in(f"wu{f}", [D, DFF]) for f in range(n_ffn)]
    wd_d = [C.dram_in(f"wd{f}", [DFF, D]) for f in range(n_ffn)]
    ng_d = [C.dram_in(f"ng{f}", [128, 8]) for f in range(n_ffn)]
    if tail != 'x':
        ngt_d = C.dram_in("ngt", [128, 8])
    if tail == 'final':
        out_d = C.dram_out("out", [D, TOK])
    else:
        xo_d = C.dram_out("xo", [D, TOK])
        if tail == 'h':
            ho_d = C.dram_out("ho", [D, TOK], BF16)

    xT = C.sb([128, 8, TOK], F32, "xT")
    hT = C.sb([128, 8, TOK], BF16, "hT")
    ones = C.sb([128, 128], BF16, "ones")
    ng_s = [C.sb([128, 8], F32, f"ng_s{f}") for f in range(n_ffn)]
    if tail != 'x':
        ngt_s = C.sb([128, 8], F32, "ngt_s")
    sq = [C.sb([128, 512], BF16, f"sq{i}") for i in range(4)]
    rstd = C.sb([128, 512], F32, "rstd")
    GS = 2
    NG = DFF // 128 // GS
    wgs = [C.sb([128, 8, GS * 128], BF16, f"wgs{i}") for i in range(2)]
    wus = [C.sb([128, 8, GS * 128], BF16, f"wus{i}") for i in range(2)]
    wds = [C.sb([128, GS, D], BF16, f"wds{i}") for i in range(2)]
    sg = [C.sb([128, 512], F32, f"sg{i}") for i in range(2)]
    act = [C.sb([128, 512], BF16, f"act{i}") for i in range(4)]
    pg = [C.ps([128, 512], F32, f"pg{i}") for i in range(2)]
    pu = [C.ps([128, 512], F32, f"pu{i}") for i in range(2)]
    pd = [C.ps([128, 512], F32, f"pd{i}") for i in range(4)]

    R.add('pool', I('memset', ap=ones[:], constant=1.0), writes=['ones'])
    for kc in range(8):
        R.add('sp', I('dma_start', out=xT[:, kc, :], in_=x_d[kc * 128:(kc + 1) * 128, :]),
              writes=[f'x{kc}_{t}' for t in range(NT)], dma=f'x{kc}')
    for f in range(n_ffn):
        R.add('sp', I('dma_start', out=ng_s[f][:], in_=ng_d[f]), writes=[f'ng{f}'], dma='ng')
    if tail != 'x':
        R.add('sp', I('dma_start', out=ngt_s[:], in_=ngt_d), writes=['ngt'], dma='ng')

    cnt = {'sq': 0, 'pd': 0, 'gu': 0, 'act': 0, 'w': 0}

    def rms_to_h(t, g_tile, gkey, out_tile=None):
        ts = slice(t * 512, (t + 1) * 512)
        b = cnt['pd'] % 4
        cnt['pd'] += 1
        for kc in range(8):
            i = cnt['sq'] % 4
            cnt['sq'] += 1
            R.add('act', I('activation', out=sq[i][:], in_=xT[:, kc, ts], func=AF.Square),
                  reads=[f'x{kc}_{t}'], writes=[f'sq{i}'])
            R.add('pe', I('matmul', out=pd[b][:], lhsT=ones[:], rhs=sq[i][:], start=(kc == 0), stop=(kc == 7)),
                  reads=[f'sq{i}', 'ones'], writes=[f'pd{b}'])
        R.add('act', I('activation', out=rstd[:], in_=pd[b][:], func=AF.Ln, scale=1.0 / D, bias=EPS), writes=[f'pd{b}', 'rstd'])
        R.add('act', I('activation', out=rstd[:], in_=rstd[:], func=AF.Exp, scale=-0.5), writes=['rstd'])
        for kc in range(8):
            dst = (hT if out_tile is None else out_tile)[:, kc, ts]
            R.add('dve', I('scalar_tensor_tensor', out=dst, in0=xT[:, kc, ts], scalar=g_tile[:, kc:kc + 1],
                           in1=rstd[:], op0=ALU.mult, op1=ALU.mult),
                  reads=[f'x{kc}_{t}', 'rstd', gkey], writes=[f'h{kc}_{t}' if out_tile is None else f'x{kc}_{t}'])

    if mix:
        nk = mix_k // 128
        ncol = 2048 if mix == 'glu' else 1024
        oTr = Ring(C, 2 if nk <= 8 else 1, [128, nk, 512], BF16, "oTr")
        oTr2 = Ring(C, 1, [128, nk, 512], BF16, "oTrB")
        hfm = C.sb([128, 2], F32, "hfm")
        R.add('sp', I('dma_start', out=hfm[:], in_=hfm_d), writes=['hfm'], dma='ng')
        wm = C.sb([128, nk, ncol], BF16, "wm")
        for k in range(nk):
            R.add('pool', I('dma_start', out=wm[:, k, :], in_=wm_d[k * 128:(k + 1) * 128, :]), writes=[f'wm{k}'], dma='wm')
        wkeys = [f'wm{k}' for k in range(nk)]
        for t in range(NT):
            ts = slice(t * 512, (t + 1) * 512)
            oT, okey = oTr.next()
            oB, okeyB = oTr2.next()
            for k in range(nk):
                rr = orow(k, nk)
                R.add('sp', I('dma_start', out=oT[:, k, :], in_=o_d[rr:rr + 128, ts]), writes=[okey], dma='o' + okey)
                R.add('sp', I('dma_start', out=oB[:, k, :], in_=o_d[rr:rr + 128, TOK + t * 512:TOK + (t + 1) * 512]),
                      writes=[okeyB], dma='o' + okeyB)
            R.add('dve', I('tensor_scalar', out=oT[:], in0=oT[:], scalar1=hfm[:, 0:1], scalar2=None, op0=ALU.mult),
                  reads=['hfm'], writes=[okey])
            R.add('dve', I('scalar_tensor_tensor', out=oT[:], in0=oB[:], scalar=hfm[:, 1:2], in1=oT[:], op0=ALU.mult, op1=ALU.add),
                  reads=['hfm', okeyB], writes=[okey])
            okeys = wkeys + [okey]
            for c in range(8):
                b = cnt['pd'] % 4
                cnt['pd'] += 1
                R.add('pe', [I('matmul', out=pd[b][:], lhsT=wm[:, k, c * 128:(c + 1) * 128], rhs=oT[:, k, :],
                               start=(k == 0), stop=(k == nk - 1)) for k in range(nk)],
                      reads=okeys, writes=[f'pd{b}'])
                if mix == 'lin':
                    R.add('dve', I('tensor_tensor', out=xT[:, c, ts], in0=pd[b][:], in1=xT[:, c, ts], op=ALU.add),
                          reads=[f'x{c}_{t}'], writes=[f'pd{b}', f'x{c}_{t}'])
                else:
                    b2 = cnt['pd'] % 4
                    cnt['pd'] += 1
                    R.add('pe', [I('matmul', out=pd[b2][:], lhsT=wm[:, k, 1024 + c * 128:1024 + (c + 1) * 128], rhs=oT[:, k, :],
                                   start=(k == 0), stop=(k == nk - 1)) for k in range(nk)],
                          reads=okeys, writes=[f'pd{b2}'])
                    i = cnt['gu'] % 2
                    cnt['gu'] += 1
                    R.add('act', I('activation', out=sg[i][:], in_=pd[b2][:], func=AF.Sigmoid),
                          writes=[f'pd{b2}', f'sg{i}'])
                    R.add('dve', I('tensor_tensor', out=sg[i][:], in0=pd[b][:], in1=sg[i][:], op=ALU.mult),
                          writes=[f'pd{b}', f'sg{i}'])
                    R.add('dve', I('tensor_tensor', out=xT[:, c, ts], in0=sg[i][:], in1=xT[:, c, ts], op=ALU.add),
                          reads=[f'sg{i}', f'x{c}_{t}'], writes=[f'x{c}_{t}'])

    for f in range(n_ffn):
        for t in range(NT):
            rms_to_h(t, ng_s[f], f'ng{f}')
        for g in range(NG):
            wi = cnt['w'] % 2
            cnt['w'] += 1
            cs = slice(g * GS * 128, (g + 1) * GS * 128)
            R.add('pool', I('dma_start', out=wgs[wi][:], in_=wg_d[f][:, cs].rearrange("(kc p) c -> p kc c", p=128)),
                  writes=[f'wg{wi}'], dma=f'wg{wi}')
            R.add('pool', I('dma_start', out=wus[wi][:], in_=wu_d[f][:, cs].rearrange("(kc p) c -> p kc c", p=128)),
                  writes=[f'wu{wi}'], dma=f'wu{wi}')
            R.add('pool', I('dma_start', out=wds[wi][:], in_=wd_d[f][cs, :].rearrange("(j p) c -> p j c", p=128)),
                  writes=[f'wd{wi}'], dma=f'wd{wi}')
            for t in range(NT):
                ts = slice(t * 512, (t + 1) * 512)
                hk = [f'h{kc}_{t}' for kc in range(8)]
                acts = []
                for j in range(GS):
                    gi = cnt['gu'] % 2
                    cnt['gu'] += 1
                    R.add('pe', [I('matmul', out=pg[gi][:], lhsT=wgs[wi][:, kc, j * 128:(j + 1) * 128], rhs=hT[:, kc, ts],
                                   start=(kc == 0), stop=(kc == 7)) for kc in range(8)],
                          reads=hk + [f'wg{wi}'], writes=[f'pg{gi}'])
                    R.add('pe', [I('matmul', out=pu[gi][:], lhsT=wus[wi][:, kc, j * 128:(j + 1) * 128], rhs=hT[:, kc, ts],
                                   start=(kc == 0), stop=(kc == 7)) for kc in range(8)],
                          reads=hk + [f'wu{wi}'], writes=[f'pu{gi}'])
                    R.add('act', I('activation', out=sg[gi][:], in_=pg[gi][:], func=AF.Silu),
                          reads=[f'pg{gi}'], writes=[f'sg{gi}'])
                    ai = cnt['act'] % 4
                    cnt['act'] += 1
                    R.add('dve', I('tensor_tensor', out=act[ai][:], in0=pu[gi][:], in1=sg[gi][:], op=ALU.mult),
                          reads=[f'pu{gi}', f'sg{gi}'], writes=[f'act{ai}'])
                    acts.append(ai)
                for c in range(8):
                    b = cnt['pd'] % 4
                    cnt['pd'] += 1
                    R.add('pe', [I('matmul', out=pd[b][:], lhsT=wds[wi][:, j, c * 128:(c + 1) * 128], rhs=act[acts[j]][:],
                                   start=(j == 0), stop=(j == GS - 1)) for j in range(GS)],
                          reads=[f'act{a}' for a in acts] + [f'wd{wi}'], writes=[f'pd{b}'])
                    R.add('dve', I('scalar_tensor_tensor', out=xT[:, c, ts], in0=pd[b][:], scalar=0.5, in1=xT[:, c, ts],
                                   op0=ALU.mult, op1=ALU.add),
                          reads=[f'pd{b}', f'x{c}_{t}'], writes=[f'x{c}_{t}'])

    fin = []
    if tail in ('h', 'x'):
        for kc in range(8):
            tok = R.add('sp', I('dma_start', out=xo_d[kc * 128:(kc + 1) * 128, :], in_=xT[:, kc, :]),
                        reads=[f'x{kc}_{t}' for t in range(NT)], dma='xo')
        fin.append(tok)
    if tail == 'h':
        for t in range(NT):
            rms_to_h(t, ngt_s, 'ngt')
        for kc in range(8):
            tok = R.add('sp', I('dma_start', out=ho_d[kc * 128:(kc + 1) * 128, :], in_=hT[:, kc, :]),
                        reads=[f'h{kc}_{t}' for t in range(NT)], dma='ho')
        fin.append(tok)
    if tail == 'final':
        for t in range(NT):
            rms_to_h(t, ngt_s, 'ngt', out_tile=xT)
        for kc in range(8):
            tok = R.add('sp', I('dma_start', out=out_d[kc * 128:(kc + 1) * 128, :], in_=xT[:, kc, :]),
                        reads=[f'x{kc}_{t}' for t in range(NT)], dma='fo')
        fin.append(tok)
    return C.finish()


def hrow(r, kc):
    return (kc // 4) * 1024 + r * 512 + (kc % 4) * 128


def orow(k, nk):
    half = nk // 2
    r, lk = k // half, k % half
    return (lk // 2) * 512 + r * 256 + (lk % 2) * 128


def h_gather_layout(h0, h1):
    a = np.stack([h0.reshape(2, 512, -1), h1.reshape(2, 512, -1)], axis=1)
    return np.ascontiguousarray(a.reshape(2048, -1))


def o_gather_layout(o0, o1):
    half = o0.shape[0]
    a = np.stack([o0.reshape(half // 256, 256, -1), o1.reshape(half // 256, 256, -1)], axis=1)
    return np.ascontiguousarray(a.reshape(2 * half, -1))


def hfm_arr(hf):
    a = np.zeros((128, 2), np.float32)
    a[:, hf] = 1.0
    return a


def fm(v):
    return np.ascontiguousarray(np.asarray(v, np.float32).reshape(8, 128).T)


def run(nc, in_maps):
    res = run_bass_kernel_spmd(nc, in_maps, core_ids=list(range(8)))
    return res.results


MAGIC = 12582912.0
TWO_PI = 2.0 * math.pi
PI_SAFE = 3.1415925


def emit_sin(R, eng, out, okey, ang, akey, tmp, tkey, shift=0.0):
    R.add(eng, I('tensor_scalar', out=tmp, in0=ang, scalar1=shift, scalar2=1.0 / TWO_PI, op0=ALU.add, op1=ALU.mult),
          reads=[akey], writes=[tkey])
    R.add(eng, I('tensor_scalar', out=tmp, in0=tmp, scalar1=MAGIC, scalar2=-MAGIC, op0=ALU.add, op1=ALU.add),
          reads=[tkey], writes=[tkey])
    R.add(eng, I('scalar_tensor_tensor', out=tmp, in0=tmp, scalar=-TWO_PI, in1=ang, op0=ALU.mult, op1=ALU.add),
          reads=[tkey, akey], writes=[tkey])
    if shift != 0.0:
        R.add(eng, I('tensor_scalar', out=tmp, in0=tmp, scalar1=shift, scalar2=None, op0=ALU.add),
              reads=[tkey], writes=[tkey])
    R.add(eng, I('tensor_scalar', out=tmp, in0=tmp, scalar1=-PI_SAFE, scalar2=PI_SAFE, op0=ALU.max, op1=ALU.min),
          reads=[tkey], writes=[tkey])
    R.add('act', I('activation', out=out, in_=tmp, func=AF.Sin), reads=[tkey], writes=[okey])


def build_s5(P=None, prefix='', bind=None):
    C = Ctx(P, prefix, bind)
    nc, R = C.nc, C.rec
    T = 512
    NCH = L // T
    u_d = C.dram_in("hT", [2 * D, TOK], BF16)
    hfm_d = C.dram_in("hfm", [128, 2])
    lre_d = C.dram_in("lam_re", [128, 16])
    lim_d = C.dram_in("lam_im", [128, 16])
    ldt_d = C.dram_in("log_dt", [128, 16])
    bre_d = C.dram_in("bre", [16, 128, 128])
    bim_d = C.dram_in("bim", [16, 128, 128])
    cre_d = C.dram_in("cre", [16, 128, 128])
    cim_d = C.dram_in("cim", [16, 128, 128])
    d_d = C.dram_in("dsk", [128, 4])
    id_d = C.dram_in("ident", [128, 128])
    y_d = C.dram_out("ygT", [512, L], BF16)

    uT = C.sb([128, 4, L], F32, "uT")
    ident = C.sb([128, 128], F32, "ident")
    dsk = C.sb([128, 4], F32, "dsk")
    P = {}
    for n in ('lre', 'lim', 'ldt', 'dt', 'a', 'th', 'r', 'sn', 'cs', 'lbr', 'lbi', 'den', 'cr', 'ci', 'thT', 'snT', 'csT',
              'tmp', 'tmp2'):
        P[n] = C.sb([128, 16], F32, "p_" + n)
    bre = C.sb([128, 16, 128], F32, "bre")
    bim = C.sb([128, 16, 128], F32, "bim")
    creb = C.sb([128, 16, 128], F32, "creb")
    cimb = C.sb([128, 16, 128], F32, "cimb")
    bbre = C.sb([128, 16, 128], F32, "bbre")
    bbim = C.sb([128, 16, 128], F32, "bbim")
    io = C.sb([128, T], F32, "io")
    cosT = [C.sb([128, T], F32, f"cost{q}") for q in range(4)]
    sinT = [C.sb([128, T], F32, f"sint{q}") for q in range(4)]
    st_re = [C.sb([128, 1], F32, f"stre{q}") for q in range(16)]
    st_im = [C.sb([128, 1], F32, f"stim{q}") for q in range(16)]
    sc = Ring(C, 14, [128, T], F32, "sc")
    gg = Ring(C, 6, [128, T], F32, "gg")
    hb = Ring(C, 4, [128, T], F32, "hb")
    yb = Ring(C, 2, [128, T], F32, "yb")
    yo = Ring(C, 2, [128, T], BF16, "yo")
    c1 = Ring(C, 4, [128, 1], F32, "c1")
    pb = Ring(C, 4, [128, T], F32, "pb", psum=True)
    py = Ring(C, 2, [128, T], F32, "py", psum=True)
    pt = Ring(C, 2, [128, 512], F32, "pt", psum=True)

    hfm = C.sb([128, 2], F32, "hfm")
    uhi = Ring(C, 2, [128, L], BF16, "uhi")
    R.add('sp', I('dma_start', out=hfm[:], in_=hfm_d), writes=['hfm'], dma='c')
    for k in range(4):
        hi, khi = uhi.next()
        for r in range(2):
            R.add('pool', I('dma_start', out=uT[:, k, r * 2048:(r + 1) * 2048], in_=u_d[hrow(r, k):hrow(r, k) + 128, :]),
                  writes=[f'u{k}'], dma=f'u{k}')
            R.add('sp', I('dma_start', out=hi[:, r * 2048:(r + 1) * 2048], in_=u_d[hrow(r, 4 + k):hrow(r, 4 + k) + 128, :]),
                  writes=[khi], dma=f'uh{k}')
        R.add('dve', I('tensor_scalar', out=uT[:, k, :], in0=uT[:, k, :], scalar1=hfm[:, 0:1], scalar2=None, op0=ALU.mult),
              reads=['hfm'], writes=[f'u{k}'])
        R.add('dve', I('scalar_tensor_tensor', out=uT[:, k, :], in0=hi[:], scalar=hfm[:, 1:2], in1=uT[:, k, :], op0=ALU.mult, op1=ALU.add),
              reads=['hfm', khi], writes=[f'u{k}'])
    R.add('sp', I('dma_start', out=ident[:], in_=id_d), writes=['ident'], dma='c')
    R.add('sp', I('dma_start', out=dsk[:], in_=d_d), writes=['dsk'], dma='c')
    R.add('sp', I('dma_start', out=P['lre'][:], in_=lre_d), writes=['lre'], dma='c')
    R.add('sp', I('dma_start', out=P['lim'][:], in_=lim_d), writes=['lim'], dma='c')
    R.add('sp', I('dma_start', out=P['ldt'][:], in_=ldt_d), writes=['ldt'], dma='c')
    R.add('sp', I('dma_start', out=bre[:], in_=bre_d.rearrange("t p c -> p t c")), writes=['bre'], dma='c')
    R.add('sp', I('dma_start', out=bim[:], in_=bim_d.rearrange("t p c -> p t c")), writes=['bim'], dma='c')
    R.add('pool', I('dma_start', out=creb[:], in_=cre_d.rearrange("t p c -> p t c")), writes=['creb'], dma='cb')
    R.add('pool', I('dma_start', out=cimb[:], in_=cim_d.rearrange("t p c -> p t c")), writes=['cimb'], dma='cb')
    R.add('pool', I('iota', out=io[:], pattern=[[1, T]], base=0, channel_multiplier=0, allow_small_or_imprecise_dtypes=True),
          writes=['io'])

    def v(n):
        return P[n][:]

    def op2(out, a, b, op, eng='dve'):
        R.add(eng, I('tensor_tensor', out=v(out), in0=v(a), in1=v(b), op=op), reads=[a, b], writes=[out])

    R.add('act', I('activation', out=v('dt'), in_=v('ldt'), func=AF.Exp), reads=['ldt'], writes=['dt'])
    R.add('dve', I('tensor_scalar', out=v('lre'), in0=v('lre'), scalar1=-1e-4, scalar2=None, op0=ALU.min),
          reads=['lre'], writes=['lre'])
    op2('a', 'lre', 'dt', ALU.mult)
    op2('th', 'lim', 'dt', ALU.mult)
    R.add('act', I('activation', out=v('r'), in_=v('a'), func=AF.Exp), reads=['a'], writes=['r'])
    emit_sin(R, 'dve', v('sn'), 'sn', v('th'), 'th', v('tmp'), 'tmp')
    emit_sin(R, 'dve', v('cs'), 'cs', v('th'), 'th', v('tmp2'), 'tmp2', shift=math.pi / 2)
    op2('lbr', 'r', 'cs', ALU.mult)
    op2('lbi', 'r', 'sn', ALU.mult)
    R.add('dve', I('tensor_scalar', out=v('thT'), in0=v('th'), scalar1=float(T), scalar2=None, op0=ALU.mult),
          reads=['th'], writes=['thT'])
    emit_sin(R, 'dve', v('snT'), 'snT', v('thT'), 'thT', v('tmp'), 'tmp')
    emit_sin(R, 'dve', v('csT'), 'csT', v('thT'), 'thT', v('tmp2'), 'tmp2', shift=math.pi / 2)
    op2('den', 'lre', 'lre', ALU.mult)
    op2('tmp', 'lim', 'lim', ALU.mult)
    op2('den', 'den', 'tmp', ALU.add)
    R.add('dve', I('reciprocal', out=v('den'), in_=v('den')), reads=['den'], writes=['den'])
    R.add('dve', I('tensor_scalar', out=v('tmp'), in0=v('lbr'), scalar1=-1.0, scalar2=None, op0=ALU.add),
          reads=['lbr'], writes=['tmp'])
    op2('cr', 'tmp', 'lre', ALU.mult)
    op2('tmp2', 'lbi', 'lim', ALU.mult)
    op2('cr', 'cr', 'tmp2', ALU.add)
    op2('cr', 'cr', 'den', ALU.mult)
    op2('ci', 'lbi', 'lre', ALU.mult)
    op2('tmp2', 'tmp', 'lim', ALU.mult)
    op2('ci', 'ci', 'tmp2', ALU.subtract)
    op2('ci', 'ci', 'den', ALU.mult)

    for tl in range(16):
        t1, k1 = sc.next()
        t2, k2 = sc.next()
        R.add('dve', I('tensor_scalar', out=t1[:, 0:128], in0=bim[:, tl, :], scalar1=P['ci'][:, tl:tl + 1], scalar2=None,
                       op0=ALU.mult), reads=['bim', 'ci'], writes=[k1])
        R.add('dve', I('scalar_tensor_tensor', out=t1[:, 0:128], in0=bre[:, tl, :], scalar=P['cr'][:, tl:tl + 1],
                       in1=t1[:, 0:128], op0=ALU.mult, op1=ALU.subtract), reads=['bre', 'cr', k1], writes=[k1])
        R.add('dve', I('tensor_scalar', out=t2[:, 0:128], in0=bre[:, tl, :], scalar1=P['ci'][:, tl:tl + 1], scalar2=None,
                       op0=ALU.mult), reads=['bre', 'ci'], writes=[k2])
        R.add('dve', I('scalar_tensor_tensor', out=t2[:, 0:128], in0=bim[:, tl, :], scalar=P['cr'][:, tl:tl + 1],
                       in1=t2[:, 0:128], op0=ALU.mult, op1=ALU.add), reads=['bim', 'cr', k2], writes=[k2])
        for (src, ks, dst, kd) in ((t1, k1, bbre, 'bbre'), (t2, k2, bbim, 'bbim')):
            p, kp = pt.next()
            R.add('pe', I('transpose', out=p[:, 0:128], in_=src[:, 0:128], identity=ident[:]), reads=[ks, 'ident'], writes=[kp])
            R.add('act', I('activation', out=dst[:, tl, :], in_=p[:, 0:128], func=AF.Copy), writes=[kp, f'{kd}{tl}'])

    fin = []
    for cq in range(4):
        for q in range(4):
            tl = cq * 4 + q
            a_t, a_k = sc.next()
            R.add('dve', I('tensor_scalar', out=a_t[:], in0=io[:], scalar1=P['th'][:, tl:tl + 1], scalar2=None, op0=ALU.mult),
                  reads=['io', 'th'], writes=[a_k])
            t_t, t_k = sc.next()
            emit_sin(R, 'dve', sinT[q][:], f'sint{q}', a_t[:], a_k, t_t[:], t_k)
            t_t, t_k = sc.next()
            emit_sin(R, 'dve', cosT[q][:], f'cost{q}', a_t[:], a_k, t_t[:], t_k, shift=math.pi / 2)
            R.add('pool', I('memset', ap=st_re[tl][:], constant=0.0), writes=[f'stre{tl}'])
            R.add('pool', I('memset', ap=st_im[tl][:], constant=0.0), writes=[f'stim{tl}'])
        items = [(c, q) for c in range(NCH) for q in range(4)]
        st8 = {}

        def front_a(i):
            c, q = items[i]
            ts = slice(c * T, (c + 1) * T)
            tl = cq * 4 + q
            cs_, sn_ = cosT[q][:], sinT[q][:]
            ck, sk = f'cost{q}', f'sint{q}'
            p_re, k_pre = pb.next()
            p_im, k_pim = pb.next()
            R.add('pe', I('matmul', out=p_re[:], lhsT=bbre[:, tl, :], rhs=uT[:, cq, ts], start=True, stop=True),
                  reads=[f'bbre{tl}', f'u{cq}'], writes=[k_pre])
            R.add('pe', I('matmul', out=p_im[:], lhsT=bbim[:, tl, :], rhs=uT[:, cq, ts], start=True, stop=True),
                  reads=[f'bbim{tl}', f'u{cq}'], writes=[k_pim])
            b_re, k_bre = sc.next()
            b_im, k_bim = sc.next()
            R.add('act', I('activation', out=b_re[:], in_=p_re[:], func=AF.Copy), writes=[k_pre, k_bre])
            R.add('act', I('activation', out=b_im[:], in_=p_im[:], func=AF.Copy), writes=[k_pim, k_bim])
            t1, k1 = sc.next()
            t2, k2 = sc.next()
            R.add('pool', I('tensor_tensor', out=t1[:], in0=b_re[:], in1=cs_, op=ALU.mult), reads=[k_bre, ck], writes=[k1])
            R.add('pool', I('tensor_tensor', out=t2[:], in0=b_im[:], in1=sn_, op=ALU.mult), reads=[k_bim, sk], writes=[k2])
            st8[i] = (t1, k1, t2, k2, b_re, k_bre, b_im, k_bim)

        def front_b(i):
            c, q = items[i]
            cs_, sn_ = cosT[q][:], sinT[q][:]
            ck, sk = f'cost{q}', f'sint{q}'
            t1, k1, t2, k2, b_re, k_bre, b_im, k_bim = st8[i]
            R.add('dve', I('tensor_tensor', out=t1[:], in0=t1[:], in1=t2[:], op=ALU.add), reads=[k1, k2], writes=[k1])
            R.add('dve', I('tensor_tensor', out=b_im[:], in0=b_im[:], in1=cs_, op=ALU.mult), reads=[ck, k2], writes=[k_bim])
            R.add('dve', I('tensor_tensor', out=b_re[:], in0=b_re[:], in1=sn_, op=ALU.mult), reads=[sk, k1], writes=[k_bre])
            R.add('dve', I('tensor_tensor', out=t2[:], in0=b_im[:], in1=b_re[:], op=ALU.subtract),
                  reads=[k_bim, k_bre, k1], writes=[k2])

        ybank = {}
        st9 = {}

        def back_a(i):
            c, q = items[i]
            tl = cq * 4 + q
            t1, k1, t2, k2 = st8.pop(i)[:4]
            i_re, k_ire = c1.next()
            i_im, k_iim = c1.next()
            tt, k_tt = c1.next()
            csT_, snT_ = P['csT'][:, tl:tl + 1], P['snT'][:, tl:tl + 1]
            R.add('dve', I('tensor_tensor', out=tt[:], in0=st_im[tl][:], in1=snT_, op=ALU.mult),
                  reads=[f'stim{tl}', 'snT'], writes=[k_tt])
            R.add('dve', I('scalar_tensor_tensor', out=i_re[:], in0=st_re[tl][:], scalar=csT_, in1=tt[:],
                           op0=ALU.mult, op1=ALU.subtract), reads=[f'stre{tl}', 'csT', k_tt], writes=[k_ire])
            R.add('dve', I('tensor_tensor', out=tt[:], in0=st_im[tl][:], in1=csT_, op=ALU.mult),
                  reads=[f'stim{tl}', 'csT', k_tt], writes=[k_tt])
            R.add('dve', I('scalar_tensor_tensor', out=i_im[:], in0=st_re[tl][:], scalar=snT_, in1=tt[:],
                           op0=ALU.mult, op1=ALU.add), reads=[f'stre{tl}', 'snT', k_tt], writes=[k_iim])
            g_re, k_gre = gg.next()
            g_im, k_gim = gg.next()
            rb = P['r'][:, tl:tl + 1].to_broadcast([128, T])
            R.add('dve', I('tensor_tensor_scan', out=g_re[:], data0=rb, data1=t1[:], initial=i_re[:, 0:1],
                           op0=ALU.mult, op1=ALU.add), reads=['r', k1, k_ire], writes=[k_gre])
            R.add('dve', I('tensor_tensor_scan', out=g_im[:], data0=rb, data1=t2[:], initial=i_im[:, 0:1],
                           op0=ALU.mult, op1=ALU.add), reads=['r', k2, k_iim], writes=[k_gim])
            R.add('act', I('activation', out=st_re[tl][:], in_=g_re[:, T - 1:T], func=AF.Copy),
                  reads=[k_gre, k_ire, k_iim], writes=[f'stre{tl}'])
            R.add('act', I('activation', out=st_im[tl][:], in_=g_im[:, T - 1:T], func=AF.Copy),
                  reads=[k_gim, k_ire, k_iim], writes=[f'stim{tl}'])
            st9[i] = (t1, k1, t2, k2, g_re, k_gre, g_im, k_gim)

        def back_b(i):
            c, q = items[i]
            ts = slice(c * T, (c + 1) * T)
            tl = cq * 4 + q
            cs_, sn_ = cosT[q][:], sinT[q][:]
            ck, sk = f'cost{q}', f'sint{q}'
            t1, k1, t2, k2, g_re, k_gre, g_im, k_gim = st9.pop(i)
            if q == 0:
                ybank[c] = py.next()
            y_p, y_k = ybank[c]
            h_re, k_hre = hb.next()
            h_imn, k_him = hb.next()
            R.add('pool', I('tensor_tensor', out=t1[:], in0=g_re[:], in1=cs_, op=ALU.mult), reads=[k_gre, ck], writes=[k1])
            R.add('pool', I('tensor_tensor', out=t2[:], in0=g_im[:], in1=sn_, op=ALU.mult), reads=[k_gim, sk], writes=[k2])
            R.add('dve', I('tensor_tensor', out=g_re[:], in0=g_re[:], in1=sn_, op=ALU.mult), reads=[sk, k1], writes=[k_gre])
            R.add('dve', I('tensor_tensor', out=g_im[:], in0=g_im[:], in1=cs_, op=ALU.mult), reads=[ck, k2], writes=[k_gim])
            R.add('dve', I('scalar_tensor_tensor', out=h_imn[:], in0=g_re[:], scalar=-1.0, in1=g_im[:],
                           op0=ALU.mult, op1=ALU.subtract), reads=[k_gre, k_gim], writes=[k_him])
            R.add('dve', I('tensor_tensor', out=h_re[:], in0=t1[:], in1=t2[:], op=ALU.subtract), reads=[k1, k2], writes=[k_hre])
            R.add('pe', [I('matmul', out=y_p[:], lhsT=creb[:, tl, :], rhs=h_re[:], start=(q == 0), stop=False),
                         I('matmul', out=y_p[:], lhsT=cimb[:, tl, :], rhs=h_imn[:], start=False, stop=(q == 3))],
                  reads=['creb', 'cimb', k_hre, k_him], writes=[y_k])
            if q == 3:
                y_s, ys_k = yb.next()
                R.add('dve', I('scalar_tensor_tensor', out=y_s[:], in0=uT[:, cq, ts], scalar=dsk[:, cq:cq + 1], in1=y_p[:],
                               op0=ALU.mult, op1=ALU.add), reads=[f'u{cq}', 'dsk'], writes=[y_k, ys_k])
                y_o, yo_k = yo.next()
                R.add('act', I('activation', out=y_o[:], in_=y_s[:], func=AF.Gelu_apprx_tanh), reads=[ys_k], writes=[yo_k])
                fin.append(R.add('sp', I('dma_start', out=y_d[cq * 128:(cq + 1) * 128, ts], in_=y_o[:]), reads=[yo_k], dma='yout'))

        n_it = len(items)
        front_a(0)
        front_a(1)
        front_b(0)
        for i in range(n_it):
            if i + 2 < n_it:
                front_a(i + 2)
            back_a(i)
            if i + 1 < n_it:
                front_b(i + 1)
            back_b(i)
    return C.finish()


def s5_inputs(inp, b, hf, hT_full):
    G0 = 32 * hf
    j = 0
    lam_re = inp['s5_lam_re'][j][G0:G0 + 32]
    lam_im = inp['s5_lam_im'][j][G0:G0 + 32]
    log_dt = inp['s5_log_dt'][j][G0:G0 + 32]
    bre = inp['s5_b_re'][j][G0:G0 + 32]
    bim = inp['s5_b_im'][j][G0:G0 + 32]
    cre = inp['s5_c_re'][j][G0:G0 + 32]
    cim = inp['s5_c_im'][j][G0:G0 + 32]

    def cols(a):
        return np.ascontiguousarray(a.reshape(16, 128).T.astype(np.float32))

    def padB(a):
        out = np.zeros((16, 128, 128), np.float32)
        for tl in range(16):
            q = tl % 4
            for s in range(2):
                g = 2 * tl + s
                out[tl, 64 * s:64 * s + 64, 32 * q + 16 * s:32 * q + 16 * s + 16] = a[g]
        return out

    def padC(a):
        return padB(np.ascontiguousarray(a.transpose(0, 2, 1)))

    return dict(hT=hT_full, hfm=hfm_arr(hf),
                lam_re=cols(lam_re), lam_im=cols(lam_im), log_dt=cols(np.repeat(log_dt[:, None], 64, axis=1)),
                bre=padB(bre), bim=padB(bim), cre=padC(cre), cim=padC(cim),
                dsk=np.ascontiguousarray(inp['s5_d'][j][512 * hf:512 * hf + 512].reshape(4, 128).T.astype(np.float32)),
                ident=np.eye(128, dtype=np.float32))


def build_hgrn(li, P=None, prefix='', bind=None):
    C = Ctx(P, prefix, bind)
    nc, R = C.nc, C.rec
    NBLK = L // 512
    CH = 32
    h_d = C.dram_in("hT", [2 * D, TOK], BF16)
    w_d = {n: C.dram_in("w_" + n, [D, 512]) for n in ('q', 'f', 'i', 'g')}
    lb_d = C.dram_in("lbraw", [128, 4, 4])
    ng_d = C.dram_in("hng", [128, 1])
    cm_d = C.dram_in("cmask", [128, 512])
    am_d = C.dram_in("amask", [128, 512])
    id_d = C.dram_in("identb", [128, 128], BF16)
    chm_d = C.dram_in("chm", [128, 4])
    o_d = C.dram_out("oT", [512, L], BF16)

    hT = C.sb([128, 8, L], BF16, "hT")
    W = {n: C.sb([128, 8, 512], BF16, "w_" + n) for n in ('q', 'f', 'i', 'g')}
    lbraw = C.sb([128, 4, 4], F32, "lbraw")
    lbs = C.sb([128, 4], F32, "lbs")
    lb = C.sb([128, 4], F32, "lb")
    omlb = C.sb([128, 4], F32, "omlb")
    hng = C.sb([128, 1], F32, "hng")
    cmask = C.sb([128, 512], F32, "cmask")
    amask = C.sb([128, 512], F32, "amask")
    identb = C.sb([128, 128], BF16, "identb")
    chm = C.sb([128, 4], F32, "chm")
    ones = C.sb([128, 128], F32, "ones")
    S = [C.sb([128, 128], F32, f"S{h}") for h in range(4)]
    Sb = [C.sb([128, 128], BF16, f"Sb{h}") for h in range(4)]
    qt = [C.sb([128, 512], BF16, f"qt{h}") for h in range(4)]
    kt = [C.sb([128, 512], BF16, f"kt{h}") for h in range(4)]
    ks = [C.sb([128, 512], BF16, f"ks{h}") for h in range(4)]
    eb = [C.sb([128, 512], F32, f"eb{h}") for h in range(4)]
    gs = [C.sb([128, 512], F32, f"gs{h}") for h in range(4)]
    oblk = C.sb([128, 4, 512], F32, "oblk")
    sc = Ring(C, 6, [128, 512], F32, "sc")
    vtm = Ring(C, 2, [128, 512], BF16, "vtm")
    atm = Ring(C, 2, [128, 512], BF16, "atm")
    ksm = Ring(C, 8, [128, 512], BF16, "ksm")
    ofin = Ring(C, 2, [128, 512], BF16, "ofin")
    sqbr = Ring(C, 2, [128, 512], BF16, "sqbr")
    onesb = C.sb([128, 128], BF16, "onesb")
    pp = Ring(C, 2, [128, 512], F32, "pp", psum=True)
    pk = C.ps([128, 1024], BF16, "pk")
    po = C.ps([128, 512], F32, "po")
    psu = [C.ps([128, 512], F32, f"psu{h}") for h in range(4)]

    for kc in range(8):
        for r in range(2):
            R.add('sp', I('dma_start', out=hT[:, kc, r * 2048:(r + 1) * 2048], in_=h_d[hrow(r, kc):hrow(r, kc) + 128, :]),
                  writes=[f'h{kc}'], dma='h')
    for n in ('q', 'f', 'i', 'g'):
        R.add('pool', I('dma_start', out=W[n][:], in_=w_d[n].rearrange("(kc p) c -> p kc c", p=128)), writes=['w_' + n], dma='w')
    R.add('sp', I('dma_start', out=lbraw[:], in_=lb_d), writes=['lbraw'], dma='c')
    R.add('sp', I('dma_start', out=hng[:], in_=ng_d), writes=['hng'], dma='c')
    R.add('sp', I('dma_start', out=cmask[:], in_=cm_d), writes=['cmask'], dma='c')
    R.add('sp', I('dma_start', out=amask[:], in_=am_d), writes=['amask'], dma='c')
    R.add('sp', I('dma_start', out=identb[:], in_=id_d), writes=['identb'], dma='c')
    R.add('sp', I('dma_start', out=chm[:], in_=chm_d), writes=['chm'], dma='c')
    R.add('pool', I('memset', ap=ones[:], constant=1.0), writes=['ones'])
    R.add('pool', I('memset', ap=onesb[:], constant=1.0), writes=['onesb'])
    for h in range(4):
        R.add('pool', I('memset', ap=S[h][:], constant=0.0), writes=[f'S{h}'])
        R.add('pool', I('memset', ap=Sb[h][:], constant=0.0), writes=[f'Sb{h}'])
    R.add('act', I('activation', out=lbraw[:], in_=lbraw[:], func=AF.Exp), reads=['lbraw'], writes=['lbraw'])
    R.add('dve', I('tensor_tensor', out=lbs[:], in0=lbraw[:, 0, :], in1=lbraw[:, 1, :], op=ALU.add), reads=['lbraw'], writes=['lbs'])
    R.add('dve', I('tensor_tensor', out=lbs[:], in0=lbs[:], in1=lbraw[:, 2, :], op=ALU.add), reads=['lbraw', 'lbs'], writes=['lbs'])
    R.add('dve', I('tensor_tensor', out=lbs[:], in0=lbs[:], in1=lbraw[:, 3, :], op=ALU.add), reads=['lbraw', 'lbs'], writes=['lbs'])
    R.add('dve', I('reciprocal', out=lbs[:], in_=lbs[:]), reads=['lbs'], writes=['lbs'])
    R.add('dve', I('tensor_copy', out=lb[:], in_=lbraw[:, 1, :]), reads=['lbraw'], writes=['lb'])
    for k in range(2, li + 1):
        R.add('dve', I('tensor_tensor', out=lb[:], in0=lb[:], in1=lbraw[:, k, :], op=ALU.add), reads=['lbraw', 'lb'], writes=['lb'])
    R.add('dve', I('tensor_tensor', out=lb[:], in0=lb[:], in1=lbs[:], op=ALU.mult), reads=['lb', 'lbs'], writes=['lb'])
    R.add('dve', I('tensor_scalar', out=omlb[:], in0=lb[:], scalar1=-1.0, scalar2=1.0, op0=ALU.mult, op1=ALU.add),
          reads=['lb'], writes=['omlb'])

    hk = [f'h{kc}' for kc in range(8)]
    fin = []
    for blk in range(NBLK):
        bs = slice(blk * 512, (blk + 1) * 512)
        for h in range(4):
            hs = slice(h * 128, (h + 1) * 128)

            def proj(n):
                p, kp = pp.next()
                R.add('pe', [I('matmul', out=p[:], lhsT=W[n][:, kc, hs], rhs=hT[:, kc, bs], start=(kc == 0), stop=(kc == 7))
                             for kc in range(8)], reads=hk + ['w_' + n], writes=[kp])
                return p, kp

            p, kp = proj('f')
            fg, kfg = sc.next()
            R.add('act', I('activation', out=fg[:], in_=p[:], func=AF.Sigmoid), writes=[kp, kfg])
            R.add('dve', I('tensor_scalar', out=fg[:], in0=fg[:], scalar1=omlb[:, h:h + 1], scalar2=lb[:, h:h + 1],
                           op0=ALU.mult, op1=ALU.add), reads=[kfg, 'omlb', 'lb'], writes=[kfg])
            lf, klf = sc.next()
            R.add('act', I('activation', out=lf[:], in_=fg[:], func=AF.Ln), reads=[kfg], writes=[klf])
            R.add('pool', I('tensor_scalar', out=fg[:], in0=fg[:], scalar1=-1.0, scalar2=1.0, op0=ALU.mult, op1=ALU.add),
                  reads=[kfg, klf], writes=[kfg])
            bb, kbb = sc.next()
            R.add('dve', I('tensor_tensor_scan', out=bb[:], data0=cmask[:], data1=lf[:], initial=0.0,
                           op0=ALU.mult, op1=ALU.add), reads=['cmask', klf], writes=[kbb])
            R.add('act', I('activation', out=eb[h][:], in_=bb[:], func=AF.Exp), reads=[kbb], writes=[f'eb{h}'])
            R.add('act', I('activation', out=lf[:], in_=bb[:], func=AF.Exp, scale=-1.0), reads=[kbb, klf], writes=[klf])
            R.add('pool', I('tensor_tensor', out=kt[h][:], in0=fg[:], in1=lf[:], op=ALU.mult), reads=[kfg, klf], writes=[f'kt{h}'])
            b3 = bb[:].rearrange("p (c t) -> p c t", t=CH)
            R.add('dve', I('tensor_tensor', out=lf[:].rearrange("p (c t) -> p c t", t=CH),
                           in0=b3[:, :, CH - 1:CH].to_broadcast([128, 512 // CH, CH]), in1=b3, op=ALU.subtract),
                  reads=[kbb, klf, f'kt{h}'], writes=[klf])
            R.add('act', I('activation', out=lf[:], in_=lf[:], func=AF.Exp), reads=[klf], writes=[klf])
            R.add('pool', I('tensor_tensor', out=ks[h][:], in0=fg[:], in1=lf[:], op=ALU.mult), reads=[kfg, klf], writes=[f'ks{h}'])
            p, kp = proj('q')
            qs_, kqs = sc.next()
            R.add('act', I('activation', out=qs_[:], in_=p[:], func=AF.Silu), writes=[kp, kqs])
            R.add('dve', I('scalar_tensor_tensor', out=qt[h][:], in0=qs_[:], scalar=128.0 ** -0.5, in1=eb[h][:],
                           op0=ALU.mult, op1=ALU.mult), reads=[kqs, f'eb{h}'], writes=[f'qt{h}'])
            p, kp = proj('g')
            R.add('act', I('activation', out=gs[h][:], in_=p[:], func=AF.Silu), writes=[kp, f'gs{h}'])
        for tt in range(4):
            t0 = blk * 512 + tt * 128
            ls = slice(tt * 128, (tt + 1) * 128)
            gsl = slice(t0, t0 + 128)
            pv, kpv = pp.next()
            R.add('pe', [I('matmul', out=pv[:], lhsT=hT[:, kc, gsl], rhs=W['i'][:, kc, :], start=(kc == 0), stop=(kc == 7))
                         for kc in range(8)], reads=hk + ['w_i'], writes=[kpv])
            v_t, kv = vtm.next()
            R.add('act', I('activation', out=v_t[:], in_=pv[:], func=AF.Copy), writes=[kpv, kv])
            pa, kpa = pp.next()
            R.add('pe', [I('matmul', out=pa[:, h * 128:(h + 1) * 128], lhsT=kt[h][:, ls], rhs=qt[h][:, ls], start=True, stop=True)
                         for h in range(4)], reads=[f'kt{h}' for h in range(4)] + [f'qt{h}' for h in range(4)], writes=[kpa])
            a_t, ka = atm.next()
            R.add('dve', I('tensor_tensor', out=a_t[:], in0=pa[:], in1=amask[:], op=ALU.mult), reads=['amask'], writes=[kpa, ka])
            R.add('pe', [I('transpose', out=pk[:, h * 128:(h + 1) * 128], in_=ks[h][:, ls], identity=identb[:]) for h in range(4)],
                  reads=[f'ks{h}' for h in range(4)] + ['identb'], writes=['pk'])
            km, kkm = [], []
            for c in range(4):
                m_t, km_k = ksm.next()
                R.add('act', I('activation', out=m_t[:], in_=pk[:, 0:512], func=AF.Copy, scale=chm[:, c:c + 1]),
                      reads=['chm'], writes=['pk', km_k])
                km.append(m_t)
                kkm.append(km_k)
            for c in range(4):
                cl = slice(tt * 128 + c * CH, tt * 128 + (c + 1) * CH)
                for h in range(4):
                    hs = slice(h * 128, (h + 1) * 128)
                    oc = slice(h * 128 + c * CH, h * 128 + (c + 1) * CH)
                    R.add('pe', [I('matmul', out=po[:, oc], lhsT=Sb[h][:], rhs=qt[h][:, cl], start=True, stop=False),
                                 I('matmul', out=po[:, oc], lhsT=v_t[:, hs], rhs=a_t[:, oc], start=False, stop=True)],
                          reads=[f'Sb{h}', f'qt{h}', kv, ka], writes=['po'])
                    R.add('pe', I('matmul', out=psu[h][:, 0:128], lhsT=km[c][:, hs], rhs=v_t[:, hs], start=True, stop=True),
                          reads=[kkm[c], kv], writes=[f'psu{h}'])
                    dcol = eb[h][:, tt * 128 + (c + 1) * CH - 1:tt * 128 + (c + 1) * CH]
                    R.add('dve', I('scalar_tensor_tensor', out=S[h][:], in0=S[h][:], scalar=dcol, in1=psu[h][:, 0:128],
                                   op0=ALU.mult, op1=ALU.add), reads=[f'eb{h}'], writes=[f'S{h}', f'psu{h}'])
                    R.add('pool', I('tensor_copy', out=Sb[h][:], in_=S[h][:]), reads=[f'S{h}'], writes=[f'Sb{h}'])
            R.add('act', I('activation', out=oblk[:, :, ls], in_=po[:].rearrange("p (h t) -> p h t", h=4), func=AF.Copy),
                  writes=['po', 'oblk'])
        for h in range(4):
            sq, ksq = sc.next()
            sqb, ksqb = sqbr.next()
            R.add('act', I('activation', out=sqb[:], in_=oblk[:, h, :], func=AF.Square), reads=['oblk'], writes=[ksqb])
            pq, kpq = pp.next()
            R.add('pe', I('matmul', out=pq[:], lhsT=onesb[:], rhs=sqb[:], start=True, stop=True), reads=[ksqb, 'onesb'], writes=[kpq])
            R.add('act', I('activation', out=sq[:], in_=pq[:], func=AF.Ln, scale=1.0 / 128, bias=EPS), writes=[kpq, ksq])
            R.add('act', I('activation', out=sq[:], in_=sq[:], func=AF.Exp, scale=-0.5), writes=[ksq])
            R.add('dve', I('scalar_tensor_tensor', out=sq[:], in0=oblk[:, h, :], scalar=hng[:, 0:1], in1=sq[:],
                           op0=ALU.mult, op1=ALU.mult), reads=['oblk', 'hng', ksq], writes=[ksq])
            of, kof = ofin.next()
            R.add('dve', I('tensor_tensor', out=of[:], in0=sq[:], in1=gs[h][:], op=ALU.mult), reads=[ksq, f'gs{h}'], writes=[kof])
            fin.append(R.add('sp', I('dma_start', out=o_d[h * 128:(h + 1) * 128, bs], in_=of[:]), reads=[kof], dma='oout'))
    return C.finish()


def hgrn_consts():
    t = np.arange(512)
    cmask = np.broadcast_to((t % 32 != 0).astype(np.float32)[None, :], (128, 512)).copy()
    s = np.arange(128)
    amask = ((s[:, None] // 32 == s[None, :] // 32) & (s[:, None] <= s[None, :])).astype(np.float32)
    return cmask, np.ascontiguousarray(np.tile(amask, (1, 4)))


def hgrn_inputs(inp, hf, hT_full):
    j = 0
    w = inp['hg_w_in'][j]
    c0 = 512 * hf
    cmask, amask = hgrn_consts()
    lbr = inp['hg_lower_bounds'][:, c0:c0 + 512].reshape(4, 4, 128).transpose(2, 0, 1)
    return dict(hT=hT_full,
                w_q=np.ascontiguousarray(w[:, c0:c0 + 512]), w_f=np.ascontiguousarray(w[:, 1024 + c0:1024 + c0 + 512]),
                w_i=np.ascontiguousarray(w[:, 2048 + c0:2048 + c0 + 512]), w_g=np.ascontiguousarray(w[:, 3072 + c0:3072 + c0 + 512]),
                lbraw=np.ascontiguousarray(lbr.astype(np.float32)),
                hng=np.ascontiguousarray(inp['hg_norm_g'][j].reshape(128, 1).astype(np.float32)),
                cmask=cmask, amask=amask, identb=np.eye(128, dtype=np.float32).astype(ml_dtypes.bfloat16),
                chm=(np.arange(128)[:, None] // 32 == np.arange(4)[None, :]).astype(np.float32))


MOBA_NR = 4608
MOBA_WH = 4480
NEG = -1.0e30


def build_moba(P=None, prefix='', bind=None):
    C = Ctx(P, prefix, bind)
    nc, R = C.nc, C.rec
    NR, WH = MOBA_NR, MOBA_WH
    h_d = C.dram_in("hT", [2 * D, TOK], BF16)
    w_d = {n: C.dram_in("w_" + n, [D, 512]) for n in ('q', 'k', 'v')}
    tab_d = C.dram_in("tab", [33, 4])
    oh_d = C.dram_in("oh", [33, NR])
    selh_d = C.dram_in("selh", [4, 512])
    J_d = C.dram_in("Jb", [128, 128], BF16)
    E_d = C.dram_in("Eb", [17, 16 * 128], BF16)
    idb_d = C.dram_in("identb", [128, 128], BF16)
    pneg_d = C.dram_in("pneg", [128, 32, 16])
    p01_d = C.dram_in("p01", [128, 32, 16])
    o_d = C.dram_out("oT", [512, L], BF16)
    fs_h = nc.dram_tensor("fscr", [4, NR], F32)
    fs = fs_h.ap()

    hT = C.sb([128, 8, L], BF16, "hT")
    W = {n: C.sb([128, 8, 512], BF16, "w_" + n) for n in ('q', 'k', 'v')}
    tab = C.sb([33, 4], F32, "tab")
    selh = C.sb([4, 512], F32, "selh")
    Jb = C.sb([128, 128], BF16, "Jb")
    Eb = C.sb([17, 16 * 128], BF16, "Eb")
    identb = C.sb([128, 128], BF16, "identb")
    ones = C.sb([128, 128], F32, "ones")
    onesb = C.sb([128, 1], BF16, "onesb")
    fmx = C.sb([4, 16], F32, "fmx")
    bmax = C.sb([128, 4], F32, "bmax")
    vtm = C.sb([128, 32, 4, 129], BF16, "vtm")
    Hk = C.sb([128, WH], BF16, "Hk")
    qT = C.sb([128, L], BF16, "qT")
    kT = C.sb([128, L], BF16, "kT")
    q32 = C.sb([128, L], F32, "q32")
    rowsT = C.sb([17, L], BF16, "rowsT")
    kmean = C.sb([128, 16], F32, "kmean")
    kmx = C.sb([128, 8], F32, "kmx")
    kmax2 = C.sb([128, 1], F32, "kmax2")
    sc = Ring(C, 3, [128, 512], F32, "sc")
    ohr = Ring(C, 1, [33, 512], F32, "ohr")
    fr = Ring(C, 1, [4, 512], F32, "fr")
    gB = [C.sb([128, 16 * 17], F32, f"gB{i}") for i in range(2)]
    gBb = [C.sb([128, 16 * 17], BF16, f"gBb{i}") for i in range(2)]
    gW = Ring(C, 3, [128, 256], F32, "gW")
    gM = Ring(C, 2, [128, 16], F32, "gM")
    pneg = C.sb([128, 32, 16], F32, "pneg")
    p01 = C.sb([128, 32, 16], F32, "p01")
    ptile = Ring(C, 6, [128, 512], BF16, "ptile")
    rl = Ring(C, 2, [128, 1], F32, "rl")
    on = Ring(C, 2, [128, 128], BF16, "on")
    osb = Ring(C, 2, [128, 512], BF16, "osb")
    ps = Ring(C, 2, [128, 512], F32, "ps", psum=True)
    pp = Ring(C, 2, [128, 512], F32, "pp", psum=True)
    po = C.ps([128, 512], F32, "po")
    pl = C.ps([128, 512], F32, "pl")
    pg = C.ps([128, 512], F32, "pg")
    ptb = C.ps([128, 1024], BF16, "ptb")

    class _MixRing:
        def __init__(self, t, k):
            self.t, self.k, self.i = t, k, 0

        def next(self):
            j = self.i % len(self.t)
            self.i += 1
            return self.t[j], self.k[j]

    stR = _MixRing(ps.t + pp.t + [pg], ps.k + pp.k + ['pg'])

    for kc in range(8):
        for r in range(2):
            R.add('sp', I('dma_start', out=hT[:, kc, r * 2048:(r + 1) * 2048], in_=h_d[hrow(r, kc):hrow(r, kc) + 128, :]),
                  writes=[f'h{kc}'], dma='h')
    for n in ('q', 'k', 'v'):
        R.add('pool', I('dma_start', out=W[n][:], in_=w_d[n].rearrange("(kc p) c -> p kc c", p=128)), writes=['w_' + n], dma='w')
    R.add('sp', I('dma_start', out=tab[:], in_=tab_d), writes=['tab'], dma='c')
    R.add('sp', I('dma_start', out=selh[:], in_=selh_d), writes=['selh'], dma='c')
    R.add('sp', I('dma_start', out=Jb[:], in_=J_d), writes=['Jb'], dma='c')
    R.add('sp', I('dma_start', out=Eb[:], in_=E_d), writes=['Eb'], dma='c')
    R.add('sp', I('dma_start', out=identb[:], in_=idb_d), writes=['identb'], dma='c')
    R.add('sp', I('dma_start', out=pneg[:], in_=pneg_d), writes=['pneg'], dma='c')
    R.add('sp', I('dma_start', out=p01[:], in_=p01_d), writes=['p01'], dma='c')
    R.add('pool', I('memset', ap=ones[:], constant=1.0), writes=['ones'])
    R.add('pool', I('memset', ap=onesb[:], constant=1.0), writes=['onesb'])
    R.add('pool', I('memset', ap=vtm[:, :, :, 128:129], constant=1.0), writes=[f'v{t}' for t in range(32)])
    hk = [f'h{kc}' for kc in range(8)]

    def frow(ch):
        o_t, o_k = ohr.next()
        R.add('sp', I('dma_start', out=o_t[:], in_=oh_d[:, ch * 512:(ch + 1) * 512]), writes=[o_k], dma=o_k)
        p, kp = pp.next()
        R.add('pe', I('matmul', out=p[0:4, :], lhsT=tab[:], rhs=o_t[:], start=True, stop=True), reads=['tab', o_k], writes=[kp])
        f_t, f_k = fr.next()
        R.add('dve', I('tensor_copy', out=f_t[:], in_=p[0:4, :]), writes=[kp, f_k])
        R.add('dve', I('tensor_reduce', out=fmx[:, ch:ch + 1], in_=f_t[:], axis=AX.X, op=ALU.max), reads=[f_k], writes=['fmx'])
        R.add('sp', I('dma_start', out=fs[:, ch * 512:(ch + 1) * 512], in_=f_t[:]), reads=[f_k], writes=['fs'], dma='fs')

    def vproj(t):
        p, kp = pp.next()
        R.add('pe', [I('matmul', out=p[:], lhsT=hT[:, kc, t * 128:(t + 1) * 128], rhs=W['v'][:, kc, :], start=(kc == 0), stop=(kc == 7))
                     for kc in range(8)], reads=hk + ['w_v'], writes=[kp])
        R.add('act', I('activation', out=vtm[:, t, :, 0:128], in_=p[:].rearrange("p (h d) -> p h d", h=4), func=AF.Copy), writes=[kp, f'v{t}'])

    for t in range(32):
        vproj(t)
        if t % 3 == 2 and t // 3 < NR // 512:
            frow(t // 3)
    R.add('dve', I('tensor_reduce', out=fmx[:, 15:16], in_=fmx[:, 0:NR // 512], axis=AX.X, op=ALU.max), reads=['fmx'], writes=['fmx'])
    p, kp = pp.next()
    R.add('pe', [I('matmul', out=p[:, h:h + 1], lhsT=selh[:, h * 128:(h + 1) * 128], rhs=fmx[:, 15:16], start=True, stop=True)
                 for h in range(4)], reads=['selh', 'fmx'], writes=[kp])
    R.add('dve', I('tensor_copy', out=bmax[:], in_=p[:, 0:4]), writes=[kp, 'bmax'])

    vkeys = [f'v{t}' for t in range(32)]

    fin = []
    for h in range(4):
        hs = slice(h * 128, (h + 1) * 128)
        R.add('pool', I('dma_start', out=Hk[:], in_=bass.AP(fs_h, h * NR, [[1, 128], [1, WH]])),
              reads=['fs'], writes=['Hk'], dma='hk')
        for blk in range(8):
            bs = slice(blk * 512, (blk + 1) * 512)
            p, kp = pp.next()
            R.add('pe', [I('matmul', out=p[:], lhsT=W['q'][:, kc, hs], rhs=hT[:, kc, bs], start=(kc == 0), stop=(kc == 7))
                         for kc in range(8)], reads=hk + ['w_q'], writes=[kp])
            R.add('act', I('activation', out=q32[:, bs], in_=p[:], func=AF.Copy, scale=128.0 ** -0.5), writes=[kp, f'q32_{blk}'])
            R.add('pool', I('tensor_copy', out=qT[:, bs], in_=q32[:, bs]), reads=[f'q32_{blk}'], writes=[f'qT{blk}'])
        for blk in range(8):
            bs = slice(blk * 512, (blk + 1) * 512)
            p, kp = pp.next()
            R.add('pe', [I('matmul', out=p[:], lhsT=W['k'][:, kc, hs], rhs=hT[:, kc, bs], start=(kc == 0), stop=(kc == 7))
                         for kc in range(8)], reads=hk + ['w_k'], writes=[kp])
            k32, kk32 = sc.next()
            R.add('act', I('activation', out=k32[:], in_=p[:], func=AF.Copy), writes=[kp, kk32])
            R.add('pool', I('tensor_copy', out=kT[:, bs], in_=k32[:]), reads=[kk32], writes=[f'kT{blk}'])
            R.add('dve', I('tensor_reduce', out=kmean[:, 2 * blk:2 * blk + 2], in_=k32[:].rearrange("p (j t) -> p j t", t=256),
                           axis=AX.X, op=ALU.add), reads=[kk32], writes=['kmean'])
            R.add('pool', I('tensor_tensor', out=k32[:], in0=k32[:], in1=k32[:], op=ALU.mult), reads=[kk32, f'kT{blk}', 'kmean'], writes=[kk32])
            p2, kp2 = pp.next()
            R.add('pe', I('matmul', out=p2[:], lhsT=ones[:], rhs=k32[:], start=True, stop=True), reads=['ones', kk32], writes=[kp2])
            R.add('dve', I('tensor_reduce', out=kmx[:, blk:blk + 1], in_=p2[:], axis=AX.X, op=ALU.max), writes=[kp2, 'kmx'])
        R.add('dve', I('tensor_reduce', out=kmax2[:], in_=kmx[:], axis=AX.X, op=ALU.max), reads=['kmx'], writes=['kmax2'])
        R.add('dve', I('tensor_scalar', out=kmean[:], in0=kmean[:], scalar1=1.0 / 256, scalar2=None, op0=ALU.mult),
              reads=['kmean'], writes=['kmean'])
        for half in range(2):
            pgb, kpgb = (pg, 'pg') if half == 0 else pp.next()
            mm = []
            rk = set()
            for tl_ in range(16):
                t = half * 16 + tl_
                tsl = slice(t * 128, (t + 1) * 128)
                blk = t // 4
                q2, kq2 = sc.next() if tl_ % 4 == 0 else (q2, kq2)
                if tl_ % 4 == 0:
                    bsl = slice(blk * 512, (blk + 1) * 512)
                    R.add('pool', I('tensor_tensor', out=q2[:], in0=q32[:, bsl], in1=q32[:, bsl], op=ALU.mult),
                          reads=[f'q32_{blk}'], writes=[kq2])
                    R.add('pe', [I('matmul', out=pgb[:, (tl_ + u) * 17:(tl_ + u) * 17 + 16], lhsT=q32[:, (t + u) * 128:(t + u + 1) * 128],
                                   rhs=kmean[:], start=True, stop=True) for u in range(4)] +
                          [I('matmul', out=pgb[:, (tl_ + u) * 17 + 16:(tl_ + u) * 17 + 17], lhsT=q2[:, u * 128:(u + 1) * 128],
                             rhs=ones[:, 0:1], start=True, stop=True) for u in range(4)],
                          reads=[f'q32_{blk}', 'kmean', kq2, 'ones'], writes=[kpgb])
            G = gB[half]
            kG = f'gB{half}'
            G3 = G[:].rearrange("p (t j) -> p t j", j=17)
            R.add('dve', I('tensor_copy', out=G[:], in_=pgb[:, 0:272]), writes=[kpgb, kG])
            Gg = G3[:, :, 0:16]
            R.add('dve', I('tensor_tensor', out=Gg, in0=Gg, in1=pneg[:, half * 16:(half + 1) * 16, :], op=ALU.add), reads=['pneg'], writes=[kG])
            W1, kW1 = gW.next()
            W13 = W1[:].rearrange("p (t j) -> p t j", j=16)
            mx, kmx_ = gM.next()
            eq, keq = gW.next()
            eq3 = eq[:].rearrange("p (t j) -> p t j", j=16)
            cur = Gg
            kcur = kG
            for it in range(3):
                R.add('dve', I('tensor_reduce', out=mx[:], in_=cur, axis=AX.X, op=ALU.max), reads=[kcur], writes=[kmx_])
                if it == 2:
                    break
                R.add('dve', I('tensor_tensor', out=eq3, in0=cur, in1=mx[:].unsqueeze(2).to_broadcast([128, 16, 16]), op=ALU.is_ge),
                      reads=[kcur, kmx_], writes=[keq])
                R.add('dve', I('scalar_tensor_tensor', out=W13, in0=eq3, scalar=NEG, in1=cur, op0=ALU.mult, op1=ALU.add),
                      reads=[keq, kcur], writes=[kW1])
                cur = W13
                kcur = kW1
            R.add('dve', I('tensor_tensor', out=eq3, in0=Gg, in1=mx[:].unsqueeze(2).to_broadcast([128, 16, 16]), op=ALU.is_ge),
                  reads=[kG, kmx_], writes=[keq])
            R.add('dve', I('tensor_scalar', out=eq3, in0=eq3, scalar1=-NEG, scalar2=NEG, op0=ALU.mult, op1=ALU.add), writes=[keq])
            R.add('dve', I('tensor_tensor', out=Gg, in0=eq3, in1=p01[:, half * 16:(half + 1) * 16, :], op=ALU.mult),
                  reads=[keq, 'p01'], writes=[kG])
            Gb = G3[:, :, 16:17]
            R.add('dve', I('tensor_scalar', out=Gb, in0=Gb, scalar1=kmax2[:, 0:1], scalar2=None, op0=ALU.mult), reads=['kmax2'], writes=[kG])
            R.add('act', I('activation', out=Gb, in_=Gb, func=AF.Sqrt), writes=[kG])
            R.add('dve', I('tensor_scalar', out=Gb, in0=Gb, scalar1=bmax[:, h:h + 1], scalar2=-1.0, op0=ALU.add, op1=ALU.mult),
                  reads=['bmax'], writes=[kG])
            Gh = gBb[half]
            R.add('dve', I('tensor_copy', out=Gh[:], in_=G[:]), reads=[kG], writes=[kG + 'b'])
            for rnd in range(2):
                R.add('pe', [I('transpose', out=ptb[0:17, u * 128:(u + 1) * 128], in_=Gh[:, (rnd * 8 + u) * 17:(rnd * 8 + u + 1) * 17],
                               identity=identb[:]) for u in range(8)], reads=[kG + 'b', 'identb'], writes=['ptb'])
                t0_ = (half * 16 + rnd * 8) * 128
                R.add('act', I('activation', out=rowsT[:, t0_:t0_ + 1024], in_=ptb[0:17, 0:1024], func=AF.Copy),
                      writes=['ptb'] + [f'rows{(t0_ // 512) + u}' for u in range(2)])
        tiles = [(qb, kj) for qb in range(8) for kj in range(4 * qb + 4)]
        pend = {}

        def st_tile(idx):
            qb, kj = tiles[idx]
            qs = slice(qb * 512, (qb + 1) * 512)
            u0 = 512 * qb - 128 * kj + 384
            j = kj // 2
            s_p, s_k = stR.next()
            R.add('pe', [I('matmul', out=s_p[:], lhsT=kT[:, kj * 128:(kj + 1) * 128], rhs=qT[:, qs], start=True, stop=False),
                         I('matmul', out=s_p[:], lhsT=Jb[:], rhs=Hk[:, u0:u0 + 512], start=False, stop=False),
                         I('matmul', out=s_p[:], lhsT=Eb[:, j * 128:(j + 1) * 128], rhs=rowsT[:, qs], start=False, stop=True)],
                  reads=[f'kT{kj // 4}', f'qT{qb}', 'Jb', 'Hk', 'Eb', f'rows{qb}'], writes=[s_k])
            p_t, p_k = ptile.next()
            R.add('act', I('activation', out=p_t[:], in_=s_p[:], func=AF.Exp), writes=[s_k, p_k])
            pend[idx] = (p_t, p_k)

        def pv_tile(idx):
            qb, kj = tiles[idx]
            qs = slice(qb * 512, (qb + 1) * 512)
            p_t, p_k = pend.pop(idx)
            mms0, mms1 = [], []
            for qt in range(4):
                gq = 4 * qb + qt
                if kj > gq:
                    continue
                cs_ = slice(qt * 128, (qt + 1) * 128)
                if qt < 3:
                    mms0.append(I('matmul', out=po[:, qt * 129:(qt + 1) * 129], lhsT=p_t[:, cs_], rhs=vtm[:, kj, h, :], start=(kj == 0 and qt == 0),
                                  stop=(kj == gq), skip_group_check=True))
                else:
                    mms1.append(I('matmul', out=pl[:, 0:129], lhsT=p_t[:, cs_], rhs=vtm[:, kj, h, :], start=(kj == 0), stop=(kj == gq),
                                  skip_group_check=True))
            if mms0:
                R.add('pe', mms0, reads=[p_k, f'v{kj}'], writes=['po'])
            if mms1:
                R.add('pe', mms1, reads=[p_k, f'v{kj}'], writes=['pl'])
            if kj == 4 * qb + 3:
                o_s, o_k = osb.next()
                for qt in range(4):
                    cs_ = slice(qt * 128, (qt + 1) * 128)
                    r_t, r_k = rl.next()
                    bank, bkey, c0_ = (po, 'po', qt * 129) if qt < 3 else (pl, 'pl', 0)
                    R.add('dve', I('reciprocal', out=r_t[:], in_=bank[:, c0_ + 128:c0_ + 129]), writes=[bkey, r_k])
                    n_t, n_k = on.next()
                    R.add('dve', I('tensor_scalar', out=n_t[:], in0=bank[:, c0_:c0_ + 128], scalar1=r_t[:, 0:1], scalar2=None, op0=ALU.mult),
                          reads=[r_k], writes=[bkey, n_k])
                    R.add('pe', I('transpose', out=ptb[:, cs_], in_=n_t[:], identity=identb[:]), reads=[n_k, 'identb'], writes=['ptb'])
                R.add('act', I('activation', out=o_s[:], in_=ptb[:, 0:512], func=AF.Copy), writes=['ptb', o_k])
                fin.append(R.add('sp', I('dma_start', out=o_d[hs, qs], in_=o_s[:]), reads=[o_k], dma='oout'))

        LOOK = 3
        for idx in range(min(LOOK, len(tiles))):
            st_tile(idx)
        for idx in range(len(tiles)):
            if idx + LOOK < len(tiles):
                st_tile(idx + LOOK)
            pv_tile(idx)
    return C.finish()


def t5_bucket_np(rel):
    n = np.maximum(rel, 0)
    max_exact = 16
    nf = np.maximum(n, 1).astype(np.float32)
    large = max_exact + (np.log(nf / np.float32(max_exact)) / np.float32(math.log(2048 / max_exact)) * np.float32(32 - max_exact)).astype(np.int32)
    large = np.minimum(large, 31)
    return np.where(n < max_exact, n, large)


def moba_consts():
    NR = MOBA_NR
    w = np.arange(NR)
    dist = w - 511
    oh = np.zeros((33, NR), np.float32)
    bk = t5_bucket_np(dist)
    valid = dist >= 0
    oh[bk[valid], w[valid]] = 1.0
    oh[32, ~valid] = NEG
    selh = np.zeros((4, 512), np.float32)
    for h in range(4):
        selh[h, h * 128:(h + 1) * 128] = 1.0
    J = np.eye(128, dtype=np.float32)[::-1].copy().astype(ml_dtypes.bfloat16)
    E = np.zeros((17, 16, 128), np.float32)
    for j in range(16):
        E[j, j, :] = 1.0
        E[16, j, :] = 1.0
    return oh, selh, J, E.reshape(17, 16 * 128).astype(ml_dtypes.bfloat16)


def moba_inputs(inp, hf, hT_full):
    j = 0
    w = inp['moba_w_in'][j]
    c0 = 512 * hf
    oh, selh, J, E = moba_consts()
    tab = np.ones((33, 4), np.float32)
    tab[:32] = inp['rel_bias_table'][:, 4 * hf:4 * hf + 4]
    tl = np.arange(32)[:, None] // 2
    jj = np.arange(16)[None, :]
    p01 = np.broadcast_to((jj < tl).astype(np.float32)[None], (128, 32, 16)).copy()
    pneg = np.broadcast_to(np.where(jj >= tl, NEG, 0.0).astype(np.float32)[None], (128, 32, 16)).copy()
    return dict(hT=hT_full, w_q=np.ascontiguousarray(w[:, c0:c0 + 512]), w_k=np.ascontiguousarray(w[:, 1024 + c0:1024 + c0 + 512]),
                w_v=np.ascontiguousarray(w[:, 2048 + c0:2048 + c0 + 512]), tab=tab, oh=oh, selh=selh, Jb=J, Eb=E, pneg=pneg, p01=p01,
                identb=np.eye(128, dtype=np.float32).astype(ml_dtypes.bfloat16))


def build_gdn(P=None, prefix='', bind=None):
    C = Ctx(P, prefix, bind)
    nc, R = C.nc, C.rec
    h_d = C.dram_in("hT", [2 * D, TOK], BF16)
    wq_d = C.dram_in("w_q", [D, 512])
    wk_d = C.dram_in("w_k", [D, 512])
    wv_d = C.dram_in("w_v", [D, 1024])
    wz_d = C.dram_in("w_z", [D, 1024])
    wba_d = C.dram_in("w_ba", [D, 16])
    cw_d = C.dram_in("convw", [128, 16, 4])
    na_d = C.dram_in("alog", [128, 8])
    dtb_d = C.dram_in("dtb", [128, 8])
    ng_d = C.dram_in("gng", [128, 1])
    msk_d = C.dram_in("masks", [128, 6, 128])
    idb_d = C.dram_in("identb", [128, 128], BF16)
    chm_d = C.dram_in("chm2", [128, 2])
    o_d = C.dram_out("oT", [1024, L], BF16)

    W = {'q': C.sb([128, 8, 512], BF16, "w_q"), 'k': C.sb([128, 8, 512], BF16, "w_k"),
         'v': C.sb([128, 8, 1024], BF16, "w_v"), 'z': C.sb([128, 8, 1024], BF16, "w_z"),
         'ba': C.sb([128, 8, 16], BF16, "w_ba")}
    hTb = Ring(C, 2, [128, 8, 512], BF16, "hTb")
    cw = C.sb([128, 16, 4], F32, "cw")
    alog = C.sb([128, 8], F32, "alog")
    dtb = C.sb([128, 8], F32, "dtb")
    gng = C.sb([128, 1], F32, "gng")
    msk = C.sb([128, 6, 128], F32, "msk")
    identb = C.sb([128, 128], BF16, "identb")
    chm = C.sb([128, 2], F32, "chm")
    ones = C.sb([128, 128], F32, "ones")
    halo = C.sb([128, 16, 4], F32, "halo")
    xs = Ring(C, 4, [128, 516], F32, "xs")
    cv = Ring(C, 4, [128, 512], F32, "cv")
    sc = Ring(C, 2, [128, 512], F32, "sc")
    qT = C.sb([128, 4, 512], BF16, "qT")
    kT = C.sb([128, 4, 512], BF16, "kT")
    vT = C.sb([128, 8, 512], BF16, "vT")
    zs = C.sb([128, 8, 512], BF16, "zs")
    oblk = C.sb([128, 8, 512], F32, "oblk")
    S4 = [C.sb([128, 4, 128], F32, f"S4_{g}") for g in range(2)]
    Sb4 = [C.sb([128, 4, 128], BF16, f"Sb4_{g}") for g in range(2)]
    vnew = [C.sb([128, 4, 128], BF16, f"vnew{g}") for g in range(2)]
    cols = {n: Ring(C, 2, [128, 8], F32, "c_" + n) for n in
            ('ba', 'beta', 'nbeta', 'g', 'ax', 'b', 'bl', 'eb', 'bexp', 'eblb', 'e0', 'e1')}
    ba16 = Ring(C, 2, [128, 16], F32, "ba16")
    brep = Ring(C, 1, [128, 8, 128], F32, "brep")
    ebrep = Ring(C, 2, [128, 8, 128], F32, "ebrep")
    bns = Ring(C, 1, [128, 8, 128], F32, "bns")
    dg = Ring(C, 2, [128, 512], F32, "dg")
    f4 = Ring(C, 5, [128, 4, 128], F32, "f4")
    b4 = Ring(C, 16, [128, 4, 128], BF16, "b4")
    ded = {(n, g): (C.sb([128, 4, 128], BF16, f"d_{n}{g}"), f"d_{n}{g}") for n in ("QK", "vb", "kbe", "kst0", "kst1", "qd", "WT") for g in range(2)}
    usb = Ring(C, 2, [128, 4, 128], F32, "usb")
    ofin = Ring(C, 2, [128, 512], BF16, "ofin")
    sqbr = Ring(C, 2, [128, 512], BF16, "sqbr")
    onesb = C.sb([128, 128], BF16, "onesb")
    pp = Ring(C, 2, [128, 512], F32, "pp", psum=True)
    pC = C.ps([128, 512], F32, "pC")
    pD = C.ps([128, 1024], BF16, "pD")
    pE = Ring(C, 2, [128, 512], F32, "pE", psum=True)
    pG = C.ps([128, 512], F32, "pG")
    pH = C.ps([128, 512], F32, "pH")

    class _MixRing:
        def __init__(self, t, k):
            self.t, self.k, self.i = t, k, 0

        def next(self):
            j = self.i % len(self.t)
            self.i += 1
            return self.t[j], self.k[j]

    bigR = _MixRing(pE.t + [pG, pH, pC] + pp.t, pE.k + ['pG', 'pH', 'pC'] + pp.k)

    for n, d_, in (('q', wq_d), ('k', wk_d), ('v', wv_d), ('z', wz_d), ('ba', wba_d)):
        R.add('pool', I('dma_start', out=W[n][:], in_=d_.rearrange("(kc p) c -> p kc c", p=128)), writes=['w_' + n], dma='w')
    for t_, d_, k_ in ((cw, cw_d, 'cw'), (alog, na_d, 'alog'), (dtb, dtb_d, 'dtb'), (gng, ng_d, 'gng'), (msk, msk_d, 'msk'),
                       (identb, idb_d, 'identb'), (chm, chm_d, 'chm')):
        R.add('sp', I('dma_start', out=t_[:], in_=d_), writes=[k_], dma='c')
    R.add('pool', I('memset', ap=ones[:], constant=1.0), writes=['ones'])
    R.add('pool', I('memset', ap=onesb[:], constant=1.0), writes=['onesb'])
    R.add('pool', I('memset', ap=halo[:], constant=0.0), writes=['halo'])
    for g in range(2):
        R.add('pool', I('memset', ap=S4[g][:], constant=0.0), writes=[f'S4_{g}'])
        R.add('pool', I('memset', ap=Sb4[g][:], constant=0.0), writes=[f'Sb4_{g}'])
        R.add('pool', I('memset', ap=vnew[g][:], constant=0.0), writes=[f'vnew{g}'])
    R.add('act', I('activation', out=alog[:], in_=alog[:], func=AF.Exp), reads=['alog'], writes=['alog'])
    R.add('dve', I('tensor_scalar', out=alog[:], in0=alog[:], scalar1=-1.0, scalar2=None, op0=ALU.mult), reads=['alog'], writes=['alog'])
    M_NEGT, M_POS, M_SNEG, M_ID, M_TRI, M_CH = range(6)

    def bc4(ap2d):
        return ap2d.unsqueeze(1).to_broadcast([128, 4, 128])

    def colbc(col4):
        return col4.unsqueeze(2).to_broadcast([128, 4, 128])

    fin = []
    eng_rr = [0]

    def evac(out, in_, pkey, okey, extra_reads=()):
        e = 'act'
        eng_rr[0] += 1
        if e == 'act':
            R.add('act', I('activation', out=out, in_=in_, func=AF.Copy), reads=list(extra_reads), writes=[pkey, okey])
        else:
            R.add('dve', I('tensor_copy', out=out, in_=in_), reads=list(extra_reads), writes=[pkey, okey])

    for sb_ in range(L // 512):
        bs = slice(sb_ * 512, (sb_ + 1) * 512)
        hT, hkey = hTb.next()
        for kc in range(8):
            r_ = sb_ // 4
            R.add('sp', I('dma_start', out=hT[:, kc, :], in_=h_d[hrow(r_, kc):hrow(r_, kc) + 128, (sb_ % 4) * 512:(sb_ % 4 + 1) * 512]),
                  writes=[hkey], dma='h' + hkey)

        jobs = ([('q', h, h) for h in range(4)] + [('k', h, 4 + h) for h in range(4)] + [('v', h, 8 + h) for h in range(8)]
                + [('z', h, None) for h in range(8)])
        stg = {}

        def stage_a(n):
            wname, h, ci = jobs[n]
            p, kp = pp.next()
            R.add('pe', [I('matmul', out=p[:], lhsT=W[wname][:, kc, h * 128:(h + 1) * 128], rhs=hT[:, kc, :], start=(kc == 0), stop=(kc == 7))
                         for kc in range(8)], reads=[hkey, 'w_' + wname], writes=[kp])
            if wname == 'z':
                R.add('act', I('activation', out=zs[:, h, :], in_=p[:], func=AF.Silu), writes=[kp, 'zs'])
                return
            x, kx = xs.next()
            R.add('pool', I('tensor_copy', out=x[:, 0:3], in_=halo[:, ci, 0:3]), reads=['halo'], writes=[kx])
            R.add('act', I('activation', out=x[:, 3:515], in_=p[:], func=AF.Copy), writes=[kp, kx])
            R.add('pool', I('tensor_copy', out=halo[:, ci, 0:3], in_=x[:, 512:515]), reads=[kx], writes=['halo'])
            stg[n] = (x, kx)

        def stage_b(n):
            wname, h, ci = jobs[n]
            if wname == 'z':
                return
            x, kx = stg[n]
            y, ky = cv.next()
            R.add('dve', I('tensor_scalar', out=y[:], in0=x[:, 3:515], scalar1=cw[:, ci, 3:4], scalar2=None, op0=ALU.mult),
                  reads=[kx, 'cw'], writes=[ky])
            for j in (2, 1, 0):
                R.add('dve', I('scalar_tensor_tensor', out=y[:], in0=x[:, j:j + 512], scalar=cw[:, ci, j:j + 1], in1=y[:],
                               op0=ALU.mult, op1=ALU.add), reads=[kx, 'cw'], writes=[ky])
            stg[n] = (y, ky)

        def stage_c(n):
            wname, h, ci = jobs[n]
            if wname == 'z':
                return
            y, ky = stg.pop(n)
            if wname == 'v':
                R.add('act', I('activation', out=vT[:, h, :], in_=y[:], func=AF.Silu), reads=[ky], writes=['vT'])
                return
            R.add('act', I('activation', out=y[:], in_=y[:], func=AF.Silu), writes=[ky])
            sq, ksq = sc.next()
            sqb, ksqb = sqbr.next()
            R.add('act', I('activation', out=sqb[:], in_=y[:], func=AF.Square), reads=[ky], writes=[ksqb])
            p, kp = pp.next()
            R.add('pe', I('matmul', out=p[:], lhsT=onesb[:], rhs=sqb[:], start=True, stop=True), reads=['onesb', ksqb], writes=[kp])
            R.add('act', I('activation', out=sq[:], in_=p[:], func=AF.Ln, bias=1e-6), writes=[kp, ksq])
            R.add('act', I('activation', out=sq[:], in_=sq[:], func=AF.Exp, scale=-0.5), writes=[ksq])
            dst, dkey, scale = (qT[:, h, :], 'qT', 128.0 ** -0.5) if wname == 'q' else (kT[:, h, :], 'kT', 1.0)
            R.add('dve', I('scalar_tensor_tensor', out=dst, in0=y[:], scalar=scale, in1=sq[:], op0=ALU.mult, op1=ALU.mult),
                  reads=[ky, ksq], writes=[dkey])

        nj = len(jobs)
        for n in range(nj + 2):
            if n < nj:
                stage_a(n)
            if 0 <= n - 1 < nj:
                stage_b(n - 1)
            if 0 <= n - 2 < nj:
                stage_c(n - 2)

        def prep(tt):
            ls = slice(tt * 128, (tt + 1) * 128)
            p, kp = pp.next()
            R.add('pe', [I('matmul', out=p[:, 0:16], lhsT=hT[:, kc, ls], rhs=W['ba'][:, kc, :], start=(kc == 0), stop=(kc == 7))
                         for kc in range(8)], reads=[hkey, 'w_ba'], writes=[kp])
            ba, kba = ba16.next()
            R.add('dve', I('tensor_copy', out=ba[:], in_=p[:, 0:16]), writes=[kp, kba])
            cc = {n: cols[n].next() for n in cols}
            beta, kbeta = cc['beta']
            R.add('act', I('activation', out=beta[:], in_=ba[:, 0:8], func=AF.Exp, scale=-1.0), reads=[kba], writes=[kbeta])
            R.add('dve', I('tensor_scalar', out=beta[:], in0=beta[:], scalar1=1.0, scalar2=None, op0=ALU.add), writes=[kbeta])
            R.add('dve', I('reciprocal', out=beta[:], in_=beta[:]), writes=[kbeta])
            nbeta, knbeta = cc['nbeta']
            R.add('dve', I('tensor_scalar', out=nbeta[:], in0=beta[:], scalar1=-1.0, scalar2=None, op0=ALU.mult), reads=[kbeta], writes=[knbeta])
            ax, kax = cc['ax']
            gg, kgg = cc['g']
            R.add('dve', I('tensor_tensor', out=ax[:], in0=ba[:, 8:16], in1=dtb[:], op=ALU.add), reads=[kba, 'dtb'], writes=[kax])
            R.add('act', I('activation', out=gg[:], in_=ax[:], func=AF.Abs), reads=[kax], writes=[kgg])
            R.add('act', I('activation', out=gg[:], in_=gg[:], func=AF.Exp, scale=-1.0), writes=[kgg])
            R.add('act', I('activation', out=gg[:], in_=gg[:], func=AF.Ln, bias=1.0), writes=[kgg])
            R.add('dve', I('scalar_tensor_tensor', out=gg[:], in0=ax[:], scalar=0.0, in1=gg[:], op0=ALU.max, op1=ALU.add),
                  reads=[kax], writes=[kgg])
            R.add('dve', I('tensor_tensor', out=gg[:], in0=gg[:], in1=alog[:], op=ALU.mult), reads=['alog'], writes=[kgg])
            p, kp = pp.next()
            R.add('pe', [I('matmul', out=p[:, 0:8], lhsT=msk[:, M_TRI, :], rhs=gg[:], start=True, stop=True),
                         I('matmul', out=p[:, 8:16], lhsT=msk[:, M_CH, :], rhs=gg[:], start=True, stop=True)],
                  reads=['msk', kgg], writes=[kp])
            bcol, kb = cc['b']
            blc, kbl = cc['bl']
            R.add('dve', I('tensor_copy', out=bcol[:], in_=p[:, 0:8]), writes=[kp, kb])
            R.add('dve', I('tensor_tensor', out=blc[:], in0=p[:, 8:16], in1=bcol[:], op=ALU.subtract), reads=[kb], writes=[kp, kbl])
            ebc, keb = cc['eb']
            R.add('act', I('activation', out=ebc[:], in_=bcol[:], func=AF.Exp), reads=[kb], writes=[keb])
            bexp, kbexp = cc['bexp']
            R.add('dve', I('tensor_tensor', out=bexp[:], in0=ebc[:], in1=beta[:], op=ALU.mult), reads=[keb, kbeta], writes=[kbexp])
            eblb, keblb = cc['eblb']
            R.add('act', I('activation', out=eblb[:], in_=blc[:], func=AF.Exp), reads=[kbl], writes=[keblb])
            e01 = []
            for c in range(2):
                e_, ke_ = cc[f'e{c}']
                R.add('dve', I('tensor_scalar', out=e_[:], in0=eblb[:], scalar1=chm[:, c:c + 1], scalar2=None, op0=ALU.mult),
                      reads=[keblb, 'chm'], writes=[ke_])
                e01.append((e_, ke_))
            br, kbr = brep.next()
            bn, kbn = bns.next()
            for (src, ksrc, dst, kdst) in ((bcol, kb, br, kbr), (beta, kbeta, bn, kbn)):
                for g in range(2):
                    d_, kd_ = dg.next()
                    R.add('dve', I('tensor_tensor', out=d_[:].rearrange("p (h t) -> p h t", h=4), in0=bc4(msk[:, M_ID, :]),
                                   in1=colbc(src[:, g * 4:(g + 1) * 4]), op=ALU.mult), reads=['msk', ksrc], writes=[kd_])
                    p, kp = pp.next()
                    R.add('pe', I('matmul', out=p[:], lhsT=ones[:], rhs=d_[:], start=True, stop=True), reads=['ones', kd_], writes=[kp])
                    evac(dst[:, g * 4:(g + 1) * 4, :], p[:].rearrange("p (h t) -> p h t", h=4), kp, kdst)
            er, ker = ebrep.next()
            R.add('act', I('activation', out=er[:], in_=br[:], func=AF.Exp), reads=[kbr], writes=[ker])
            for g in range(2):
                R.add('pool', I('tensor_tensor', out=bn[:, g * 4:(g + 1) * 4, :], in0=bn[:, g * 4:(g + 1) * 4, :],
                                in1=bc4(msk[:, M_SNEG, :]), op=ALU.mult), reads=['msk'], writes=[kbn])
            return dict(ls=ls, bcol=bcol, kb=kb, beta=beta, kbeta=kbeta, nbeta=nbeta, knbeta=knbeta, bexp=bexp, kbexp=kbexp,
                        e01=e01, br=br, kbr=kbr, bn=bn, kbn=kbn, er=er, ker=ker)

        for tt in range(4):
            cur_ = prep(tt)
            ls, bcol, kb, beta, kbeta, nbeta, knbeta = (cur_[k_] for k_ in ('ls', 'bcol', 'kb', 'beta', 'kbeta', 'nbeta', 'knbeta'))
            bexp, kbexp, e01, br, kbr, bn, kbn, er, ker = (cur_[k_] for k_ in ('bexp', 'kbexp', 'e01', 'br', 'kbr', 'bn', 'kbn', 'er', 'ker'))
            grp = []
            for g in range(2):
                hv4 = slice(g * 4, (g + 1) * 4)
                mm = []
                for kk_ in range(2):
                    kh = 2 * g + kk_
                    mm.append(I('matmul', out=pC[:, kk_ * 128:(kk_ + 1) * 128], lhsT=kT[:, kh, ls], rhs=kT[:, kh, ls], start=True, stop=True))
                    mm.append(I('matmul', out=pC[:, 256 + kk_ * 128:256 + (kk_ + 1) * 128], lhsT=kT[:, kh, ls], rhs=qT[:, kh, ls],
                                start=True, stop=True))
                R.add('pe', mm, reads=['kT', 'qT'], writes=['pC'])
                kk_bc = pC[:, 0:256].rearrange("p (k t) -> p k t", k=2).unsqueeze(2).to_broadcast([128, 2, 2, 128])
                qk_bc = pC[:, 256:512].rearrange("p (k t) -> p k t", k=2).unsqueeze(2).to_broadcast([128, 2, 2, 128])

                def v4(t):
                    return t.rearrange("p (k j) t -> p k j t", k=2)

                Dm, kDm = f4.next()
                R.add('dve', I('tensor_tensor', out=Dm[:], in0=br[:, hv4, :], in1=colbc(bcol[:, hv4]), op=ALU.subtract),
                      reads=[kbr, kb], writes=[kDm])
                E1, kE1 = f4.next()
                R.add('dve', I('tensor_tensor', out=E1[:], in0=Dm[:], in1=bc4(msk[:, M_NEGT, :]), op=ALU.add), reads=[kDm, 'msk'], writes=[kE1])
                R.add('act', I('activation', out=E1[:], in_=E1[:], func=AF.Exp), writes=[kE1])
                E2, kE2 = f4.next()
                R.add('pool', I('tensor_tensor', out=E2[:], in0=Dm[:], in1=bc4(msk[:, M_POS, :]), op=ALU.add), reads=[kDm, 'msk'], writes=[kE2])
                R.add('act', I('activation', out=E2[:], in_=E2[:], func=AF.Exp, scale=-1.0), writes=[kE2])
                QK, kQK = ded[('QK', g)]
                R.add('dve', I('tensor_tensor', out=v4(QK[:]), in0=qk_bc, in1=v4(E1[:]), op=ALU.mult), reads=[kE1], writes=['pC', kQK])
                tmp, ktmp = f4.next()
                R.add('dve', I('tensor_tensor', out=v4(tmp[:]), in0=kk_bc, in1=v4(E1[:]), op=ALU.mult), reads=[kE1], writes=['pC', ktmp])
                NT, kNT = b4.next()
                R.add('pool', I('tensor_tensor', out=NT[:], in0=tmp[:], in1=bn[:, hv4, :], op=ALU.mult), reads=[ktmp, kbn], writes=[kNT])
                AT, kAT = b4.next()
                R.add('pool', I('tensor_tensor', out=AT[:], in0=NT[:], in1=bc4(msk[:, M_ID, :]), op=ALU.add), reads=[kNT, 'msk'], writes=[kAT])
                t2, kt2 = f4.next()
                R.add('dve', I('tensor_tensor', out=v4(t2[:]), in0=kk_bc, in1=v4(colbc(nbeta[:, hv4])), op=ALU.mult),
                      reads=[knbeta], writes=['pC', kt2])
                N_, kN = b4.next()
                R.add('pool', I('tensor_tensor', out=N_[:], in0=t2[:], in1=E2[:], op=ALU.mult), reads=[kt2, kE2], writes=[kN])
                mm = [I('transpose', out=pD[:, kk_ * 128:(kk_ + 1) * 128], in_=kT[:, 2 * g + kk_, ls], identity=identb[:]) for kk_ in range(2)]
                mm += [I('transpose', out=pD[:, 256 + j * 128:256 + (j + 1) * 128], in_=vT[:, g * 4 + j, ls], identity=identb[:]) for j in range(4)]
                R.add('pe', mm, reads=['kT', 'vT', 'identb'], writes=['pD'])
                ktm_bc = pD[:, 0:256].rearrange("p (k t) -> p k t", k=2).unsqueeze(2).to_broadcast([128, 2, 2, 128])
                vb, kvb = ded[('vb', g)]
                R.add('dve', I('tensor_tensor', out=vb[:], in0=pD[:, 256:768].rearrange("p (h t) -> p h t", h=4),
                               in1=colbc(beta[:, hv4]), op=ALU.mult), reads=[kbeta], writes=['pD', kvb])
                kbe, kkbe = ded[('kbe', g)]
                R.add('dve', I('tensor_tensor', out=v4(kbe[:]), in0=ktm_bc, in1=v4(colbc(bexp[:, hv4])), op=ALU.mult),
                      reads=[kbexp], writes=['pD', kkbe])
                kst = []
                for c in range(2):
                    k_, kk2 = ded[(f'kst{c}', g)]
                    R.add('dve', I('tensor_tensor', out=v4(k_[:]), in0=ktm_bc, in1=v4(colbc(e01[c][0][:, hv4])), op=ALU.mult),
                          reads=[e01[c][1]], writes=['pD', kk2])
                    kst.append((k_, kk2))
                qd, kqd = ded[('qd', g)]
                R.add('pool', I('tensor_tensor', out=v4(qd[:]), in0=qT[:, 2 * g:2 * g + 2, ls].unsqueeze(2).to_broadcast([128, 2, 2, 128]),
                                in1=v4(er[:, hv4, :]), op=ALU.mult), reads=['qT', ker], writes=[kqd])
                grp.append(dict(N=(N_, kN), NT=(NT, kNT), AT=(AT, kAT), QK=(QK, kQK), vb=(vb, kvb), kbe=(kbe, kkbe), kst=kst, qd=(qd, kqd)))

            def r4(t):
                return t[:].rearrange("p (h t) -> p h t", h=4)

            for lev in range(5):
                st_ = []
                for g in range(2):
                    G_ = grp[g]
                    (N_, kN), (NT, kNT) = G_['N'], G_['NT']
                    p1, kp1 = bigR.next()
                    R.add('pe', [I('matmul', out=p1[:, j * 128:(j + 1) * 128], lhsT=NT[:, j, :], rhs=N_[:, j, :], start=True, stop=True)
                                 for j in range(4)], reads=[kN, kNT], writes=[kp1])
                    p2 = kp2 = None
                    if lev < 4:
                        p2, kp2 = bigR.next()
                        R.add('pe', [I('matmul', out=p2[:, j * 128:(j + 1) * 128], lhsT=N_[:, j, :], rhs=NT[:, j, :], start=True, stop=True)
                                     for j in range(4)], reads=[kN, kNT], writes=[kp2])
                    st_.append((p1, kp1, p2, kp2))
                n2s = []
                for g in range(2):
                    p1, kp1, p2, kp2 = st_[g]
                    N2, kN2 = b4.next()
                    R.add('act', I('activation', out=N2[:], in_=r4(p1), func=AF.Copy), writes=[kp1, kN2])
                    n2s.append((N2, kN2))
                for g in range(2):
                    p1, kp1, p2, kp2 = st_[g]
                    if p2 is not None:
                        NT2, kNT2 = b4.next()
                        R.add('act', I('activation', out=NT2[:], in_=r4(p2), func=AF.Copy), writes=[kp2, kNT2])
                        grp[g]['NT'] = (NT2, kNT2)
                pas = []
                for g in range(2):
                    (AT, kAT) = grp[g]['AT']
                    N2, kN2 = n2s[g]
                    p3, kp3 = bigR.next()
                    R.add('pe', [I('matmul', out=p3[:, j * 128:(j + 1) * 128], lhsT=N2[:, j, :], rhs=AT[:, j, :], start=True, stop=True)
                                 for j in range(4)], reads=[kN2, kAT], writes=[kp3])
                    pas.append((p3, kp3))
                for g in range(2):
                    (AT, kAT) = grp[g]['AT']
                    p3, kp3 = pas[g]
                    AT2, kAT2 = b4.next()
                    R.add('dve', I('tensor_tensor', out=AT2[:], in0=r4(p3), in1=AT[:], op=ALU.add), reads=[kAT], writes=[kp3, kAT2])
                    grp[g]['N'] = n2s[g]
                    grp[g]['AT'] = (AT2, kAT2)

            pu_ = []
            for g in range(2):
                (AT, kAT), (vb, kvb) = grp[g]['AT'], grp[g]['vb']
                p, kp = bigR.next()
                R.add('pe', [I('matmul', out=p[:, j * 128:(j + 1) * 128], lhsT=AT[:, j, :], rhs=vb[:, j, :], start=True, stop=True)
                             for j in range(4)], reads=[kAT, kvb], writes=[kp])
                pu_.append((p, kp))
            pw_ = []
            for g in range(2):
                (AT, kAT), (kbe, kkbe) = grp[g]['AT'], grp[g]['kbe']
                p, kp = bigR.next()
                R.add('pe', [I('matmul', out=p[:, j * 128:(j + 1) * 128], lhsT=kbe[:, j, :], rhs=AT[:, j, :], start=True, stop=True)
                             for j in range(4)], reads=[kAT, kkbe], writes=[kp])
                pw_.append((p, kp))
            for g in range(2):
                U, kU = usb.next()
                R.add('act', I('activation', out=U[:], in_=r4(pu_[g][0]), func=AF.Copy), writes=[pu_[g][1], kU])
                grp[g]['U'] = (U, kU)
            for g in range(2):
                WT, kWT = ded[('WT', g)]
                R.add('act', I('activation', out=WT[:], in_=r4(pw_[g][0]), func=AF.Copy), writes=[pw_[g][1], kWT])
                grp[g]['WT'] = (WT, kWT)

            for c in range(2):
                rows = slice(c * 64, (c + 1) * 64)
                ccols = slice(c * 64, (c + 1) * 64)
                ws = []
                for g in range(2):
                    (WT, kWT) = grp[g]['WT']
                    p, kp = bigR.next()
                    R.add('pe', [I('matmul', out=p[:, j * 128:(j + 1) * 128], lhsT=WT[:, j, :], rhs=Sb4[g][:, j, :], start=True, stop=True)
                                 for j in range(4)], reads=[kWT, f'Sb4_{g}'], writes=[kp])
                    ws.append((p, kp))
                for g in range(2):
                    (U, kU) = grp[g]['U']
                    p, kp = ws[g]
                    R.add('dve', I('tensor_tensor', out=vnew[g][rows, :, :], in0=U[rows, :, :],
                                   in1=p[rows, :].rearrange("p (h t) -> p h t", h=4), op=ALU.subtract),
                          reads=[kU], writes=[kp, f'vnew{g}'])
                po_, ps_ = [], []
                for g in range(2):
                    (QK, kQK), (qd, kqd) = grp[g]['QK'], grp[g]['qd']
                    (ks_, kks) = grp[g]['kst'][c]
                    p, kp = bigR.next()
                    mm = []
                    for j in range(4):
                        oc = slice(j * 64, (j + 1) * 64)
                        mm.append(I('matmul', out=p[:, oc], lhsT=Sb4[g][:, j, :], rhs=qd[:, j, ccols], start=True, stop=False))
                        mm.append(I('matmul', out=p[:, oc], lhsT=vnew[g][:, j, :], rhs=QK[:, j, ccols], start=False, stop=True))
                    R.add('pe', mm, reads=[f'Sb4_{g}', kqd, f'vnew{g}', kQK], writes=[kp])
                    po_.append((p, kp))
                    p, kp = bigR.next()
                    R.add('pe', [I('matmul', out=p[:, j * 128:(j + 1) * 128], lhsT=ks_[:, j, :], rhs=vnew[g][:, j, :], start=True, stop=True)
                                 for j in range(4)], reads=[kks, f'vnew{g}'], writes=[kp])
                    ps_.append((p, kp))
                for g in range(2):
                    hv4 = slice(g * 4, (g + 1) * 4)
                    dl = er[:, hv4, c * 64 + 63:c * 64 + 64].to_broadcast([128, 4, 128])
                    R.add('dve', I('tensor_tensor', out=S4[g][:], in0=S4[g][:], in1=dl, op=ALU.mult), reads=[ker], writes=[f'S4_{g}'])
                    R.add('dve', I('tensor_tensor', out=S4[g][:], in0=S4[g][:], in1=r4(ps_[g][0]), op=ALU.add),
                          writes=[ps_[g][1], f'S4_{g}'])
                    R.add('act', I('activation', out=Sb4[g][:], in_=S4[g][:], func=AF.Copy), reads=[f'S4_{g}'], writes=[f'Sb4_{g}'])
                for g in range(2):
                    hv4 = slice(g * 4, (g + 1) * 4)
                    R.add('act', I('activation', out=oblk[:, hv4, tt * 128 + c * 64:tt * 128 + (c + 1) * 64],
                                   in_=po_[g][0][:, 0:256].rearrange("p (h t) -> p h t", h=4), func=AF.Copy), writes=[po_[g][1], 'oblk'])
        for h in range(8):
            sq, ksq = sc.next()
            sqb, ksqb = sqbr.next()
            R.add('act', I('activation', out=sqb[:], in_=oblk[:, h, :], func=AF.Square), reads=['oblk'], writes=[ksqb])
            p, kp = pp.next()
            R.add('pe', I('matmul', out=p[:], lhsT=onesb[:], rhs=sqb[:], start=True, stop=True), reads=[ksqb, 'onesb'], writes=[kp])
            R.add('act', I('activation', out=sq[:], in_=p[:], func=AF.Ln, scale=1.0 / 128, bias=EPS), writes=[kp, ksq])
            R.add('act', I('activation', out=sq[:], in_=sq[:], func=AF.Exp, scale=-0.5), writes=[ksq])
            R.add('dve', I('scalar_tensor_tensor', out=sq[:], in0=oblk[:, h, :], scalar=gng[:, 0:1], in1=sq[:],
                           op0=ALU.mult, op1=ALU.mult), reads=['oblk', 'gng'], writes=[ksq])
            of, kof = ofin.next()
            R.add('dve', I('tensor_tensor', out=of[:], in0=sq[:], in1=zs[:, h, :], op=ALU.mult), reads=[ksq, 'zs'], writes=[kof])
            fin.append(R.add('sp', I('dma_start', out=o_d[h * 128:(h + 1) * 128, bs], in_=of[:]), reads=[kof], dma='oout'))
    return C.finish()


def gdn_inputs(inp, hf, hT_full):
    j = 0
    w = inp['gdn_w_in'][j]
    kq0 = 512 * hf
    v0 = 1024 * hf
    cwf = inp['gdn_conv_w'][j]
    chans = np.concatenate([np.arange(kq0, kq0 + 512), 1024 + np.arange(kq0, kq0 + 512), 2048 + np.arange(v0, v0 + 1024)])
    cw = cwf[:, chans].reshape(4, 16, 128).transpose(2, 1, 0)
    s = np.arange(128)
    same = (s[:, None] // 64 == s[None, :] // 64)
    le = s[:, None] <= s[None, :]
    lt = s[:, None] < s[None, :]
    masks = np.zeros((128, 6, 128), np.float32)
    masks[:, 0, :] = np.where(same & le, 0.0, NEG)
    masks[:, 1, :] = np.where(same & (s[None, :] < s[:, None]), 0.0, -NEG)
    masks[:, 2, :] = np.where(same & lt, -1.0, 0.0)
    masks[:, 3, :] = np.eye(128)
    masks[:, 4, :] = (same & le).astype(np.float32)
    masks[:, 5, :] = same.astype(np.float32)
    return dict(hT=hT_full, w_q=np.ascontiguousarray(w[:, kq0:kq0 + 512]), w_k=np.ascontiguousarray(w[:, 1024 + kq0:1024 + kq0 + 512]),
                w_v=np.ascontiguousarray(w[:, 2048 + v0:2048 + v0 + 1024]), w_z=np.ascontiguousarray(w[:, 4096 + v0:4096 + v0 + 1024]),
                w_ba=np.ascontiguousarray(np.concatenate([w[:, 6144 + 8 * hf:6144 + 8 * hf + 8], w[:, 6160 + 8 * hf:6160 + 8 * hf + 8]], axis=1)),
                convw=np.ascontiguousarray(cw.astype(np.float32)),
                alog=np.ascontiguousarray(np.broadcast_to(inp['gdn_a_log'][j][8 * hf:8 * hf + 8][None, :], (128, 8)).astype(np.float32)),
                dtb=np.ascontiguousarray(np.broadcast_to(inp['gdn_dt_bias'][j][8 * hf:8 * hf + 8][None, :], (128, 8)).astype(np.float32)),
                gng=np.ascontiguousarray(inp['gdn_norm_g'][j].reshape(128, 1).astype(np.float32)),
                masks=masks, identb=np.eye(128, dtype=np.float32).astype(ml_dtypes.bfloat16),
                chm2=(np.arange(128)[:, None] // 64 == np.arange(2)[None, :]).astype(np.float32))


_PROG = {}


def _prog(key, fn):
    if key not in _PROG:
        _PROG[key] = fn()
    return _PROG[key]


def kernel_unfused(**inp):
    inp = {k: np.asarray(v) for k, v in inp.items()}
    x = inp['x']
    ng = inp['norm_g']
    f32 = np.float32

    def ffn_w(i, j, f):
        return {f'wg{f}': inp['ffn_w_gate'][i, j], f'wu{f}': inp['ffn_w_up'][i, j], f'wd{f}': inp['ffn_w_down'][i, j]}

    nc = _prog('tp0', lambda: build_token_phase(dict(mix=None, n_ffn=1, tail='h')))
    maps = []
    for c in range(8):
        b, hf = c // 2, c % 2
        m = dict(xT=np.ascontiguousarray(x[b, hf * TOK:(hf + 1) * TOK].T.astype(f32)), ng0=fm(ng[0, 0]), ngt=fm(ng[0, 1]))
        m.update(ffn_w(0, 0, 0))
        maps.append(m)
    res = run(nc, maps)
    xs = [r['xo'] for r in res]
    hs = [r['ho'] for r in res]

    mixers = [
        ('s5', build_s5, s5_inputs, 'ygT', 'glu', 1024, inp['s5_w_glu'][0]),
        ('hg', lambda: build_hgrn(1), hgrn_inputs, 'oT', 'lin', 1024, inp['hg_w_out'][0]),
        ('gdn', build_gdn, gdn_inputs, 'oT', 'lin', 2048, inp['gdn_w_out'][0]),
        ('moba', build_moba, moba_inputs, 'oT', 'lin', 1024, inp['moba_w_out'][0]),
    ]
    out = None
    for i, (name, bfn, ifn, okey, mixkind, mix_k, w_mix) in enumerate(mixers):
        ncm = _prog(name, bfn)
        maps = []
        for c in range(8):
            b, hf = c // 2, c % 2
            hT_full = h_gather_layout(hs[2 * b], hs[2 * b + 1])
            if name == 's5':
                maps.append(ifn(inp, b, hf, hT_full))
            else:
                maps.append(ifn(inp, hf, hT_full))
        res = run(ncm, maps)
        os_ = [r[okey] for r in res]
        last = (i == 3)
        cfg = dict(mix=mixkind, mix_k=mix_k, n_ffn=(1 if last else 2), tail=('final' if last else 'h'))
        nct = _prog(('tp', mixkind, mix_k, last), lambda: build_token_phase(cfg))
        maps = []
        for c in range(8):
            b, hf = c // 2, c % 2
            oT = o_gather_layout(os_[2 * b], os_[2 * b + 1])
            m = dict(xT=xs[c], oT=oT, hfm=hfm_arr(hf), w_mix=w_mix, ng0=fm(ng[i, 2]))
            m.update(ffn_w(i, 1, 0))
            if last:
                m['ngt'] = fm(inp['final_norm_g'])
            else:
                m['ng1'] = fm(ng[i + 1, 0])
                m['ngt'] = fm(ng[i + 1, 1])
                m.update(ffn_w(i + 1, 0, 1))
            maps.append(m)
        res = run(nct, maps)
        if last:
            out = np.zeros((NB, L, D), f32)
            for c in range(8):
                b, hf = c // 2, c % 2
                out[b, hf * TOK:(hf + 1) * TOK] = res[c]['out'].T
        else:
            xs = [r['xo'] for r in res]
            hs = [r['ho'] for r in res]
    return out


RG = [[0, 1], [2, 3], [4, 5], [6, 7]]
MIXERS = [('s5', 'ygT', 512, 'glu', 1024), ('hg', 'oT', 512, 'lin', 1024), ('gdn', 'oT', 1024, 'lin', 2048), ('moba', 'oT', 512, 'lin', 1024)]


def _mixer_build(name, **kw):
    if name == 's5':
        return build_s5(**kw)
    if name == 'hg':
        return build_hgrn(1, **kw)
    if name == 'gdn':
        return build_gdn(**kw)
    return build_moba(**kw)


def build_fused(stop=99):
    P = Prog()
    nc, R = P.nc, P.rec
    x_in = nc.dram_tensor("xT_in", [D, TOK], F32, kind="ExternalInput").ap()
    out = nc.dram_tensor("out", [D, TOK], F32, kind="ExternalOutput").ap()
    x_scr = nc.dram_tensor("x_scr", [D, TOK], F32)
    h_loc = nc.dram_tensor("h_loc", [D, TOK], BF16)
    h_all = nc.dram_tensor("h_all", [2 * D, TOK], BF16)
    o_loc = {n: nc.dram_tensor(f"o_loc{n}", [n, L], BF16) for n in (512, 1024)}
    o_all = {n: nc.dram_tensor(f"o_all{n}", [n, L], BF16) for n in (1024, 2048)}

    def ag(src, dst):
        rows, cols = src.ap().shape
        pr = (2 * 1024 * 1024) // (cols * 2)
        for p in range(rows // pr):
            R.add('pool', I('collective_compute', kind="AllGather", op=ALU.bypass, replica_groups=RG,
                            ins=[src.ap()[p * pr:(p + 1) * pr, :].opt()], outs=[dst.ap()[2 * p * pr:2 * (p + 1) * pr, :].opt()]),
                  dma='cc', inc=1)
        R.flush()

    build_token_phase(dict(mix=None, n_ffn=1, tail='h'), P, 't0_', {'xT': x_in, 'xo': x_scr.ap(), 'ho': h_loc.ap()})
    step = [0]

    def done():
        step[0] += 1
        return step[0] >= stop

    if done():
        R.flush(final=True)
        return P
    ag(h_loc, h_all)
    if done():
        R.flush(final=True)
        return P
    for i, (name, okey, half, mixkind, mix_k) in enumerate(MIXERS):
        _mixer_build(name, P=P, prefix=f'm{i}_', bind={'hT': h_all.ap(), okey: o_loc[half].ap()})
        if done():
            break
        ag(o_loc[half], o_all[2 * half])
        if done():
            break
        last = (i == 3)
        cfg = dict(mix=mixkind, mix_k=mix_k, n_ffn=(1 if last else 2), tail=('final' if last else 'h'))
        bind = {'xT': x_scr.ap(), 'oT': o_all[2 * half].ap()}
        if last:
            bind['out'] = out
        else:
            bind['xo'] = x_scr.ap()
            bind['ho'] = h_loc.ap()
        build_token_phase(cfg, P, f't{i + 1}_', bind)
        if done():
            break
        if not last:
            ag(h_loc, h_all)
            if done():
                break
    R.flush(final=True)
    return P


def fused_inputs(inp, c):
    b, hf = c // 2, c % 2
    ng = inp['norm_g']
    f32 = np.float32
    m = {'xT_in': np.ascontiguousarray(inp['x'][b, hf * TOK:(hf + 1) * TOK].T.astype(f32))}

    def put(prefix, d):
        for k, v in d.items():
            m[prefix + k] = v

    def ffn_w(i, j, f):
        return {f'wg{f}': inp['ffn_w_gate'][i, j], f'wu{f}': inp['ffn_w_up'][i, j], f'wd{f}': inp['ffn_w_down'][i, j]}

    d = dict(ng0=fm(ng[0, 0]), ngt=fm(ng[0, 1]))
    d.update(ffn_w(0, 0, 0))
    put('t0_', d)
    wmix = [inp['s5_w_glu'][0], inp['hg_w_out'][0], inp['gdn_w_out'][0], inp['moba_w_out'][0]]
    for i, (name, okey, half, mixkind, mix_k) in enumerate(MIXERS):
        if name == 's5':
            d = s5_inputs(inp, b, hf, None)
        elif name == 'hg':
            d = hgrn_inputs(inp, hf, None)
        elif name == 'gdn':
            d = gdn_inputs(inp, hf, None)
        else:
            d = moba_inputs(inp, hf, None)
        d.pop('hT')
        put(f'm{i}_', d)
        last = (i == 3)
        d = dict(hfm=hfm_arr(hf), w_mix=wmix[i], ng0=fm(ng[i, 2]))
        d.update(ffn_w(i, 1, 0))
        if last:
            d['ngt'] = fm(inp['final_norm_g'])
        else:
            d['ng1'] = fm(ng[i + 1, 0])
            d['ngt'] = fm(ng[i + 1, 1])
            d.update(ffn_w(i + 1, 0, 1))
        put(f't{i + 1}_', d)
    return m


def kernel(**inp):
    inp = {k: np.asarray(v) for k, v in inp.items()}
    P = _prog('fused', build_fused)
    maps = []
    for c in range(8):
        m = fused_inputs(inp, c)
        assert set(m.keys()) == set(P.ext.keys()) | {'xT_in'}, (sorted(set(m.keys()) ^ (set(P.ext.keys()) | {'xT_in'})))
        maps.append(m)
    res = run(P.nc, maps)
    out = np.zeros((NB, L, D), np.float32)
    for c in range(8):
        b, hf = c // 2, c % 2
        out[b, hf * TOK:(hf + 1) * TOK] = res[c]['out'].T
    return out
```

```python
import math
from contextlib import ExitStack
import numpy as np
import ml_dtypes
import concourse.bass as bass
import concourse.mybir as mybir
from concourse.bass_utils import run_bass_kernel_spmd

F32 = mybir.dt.float32
BF16 = mybir.dt.bfloat16
AF = mybir.ActivationFunctionType
ALU = mybir.AluOpType
AX = mybir.AxisListType

D = 1024
DFF = 2816
L = 4096
NB = 4
TOK = 2048
EPS = 1e-6
SAME_ENG_SYNC = True


def I(name, **kw):
    return (name, kw)


class Rec:
    def __init__(self, nc):
        self.nc = nc
        self.ops = []
        self.count = {}
        self.last_w = {}
        self.readers = {}
        self.waited = {}

    def _deps(self, eng, reads, writes):
        deps = {}

        def need(tok):
            if tok is None:
                return
            s, v = tok
            if s.startswith('dma_'):
                v = self.count[s]
            if s == eng and (eng == 'pe' or not SAME_ENG_SYNC):
                return
            if deps.get(s, 0) < v:
                deps[s] = v

        for k in reads:
            need(self.last_w.get(k))
        for k in writes:
            need(self.last_w.get(k))
            for t in self.readers.get(k, ()):
                need(t)
        out = []
        for s, v in deps.items():
            if self.waited.get((eng, s), 0) < v:
                self.waited[(eng, s)] = v
                out.append((s, v))
        return out

    def add(self, eng, fns, reads=(), writes=(), dma=None, inc=16):
        if not isinstance(fns, list):
            fns = [fns]
        waits = self._deps(eng, reads, writes)
        if dma is not None:
            sem = 'dma_' + dma
            self.count[sem] = self.count.get(sem, 0) + inc
        else:
            sem = eng
            self.count[sem] = self.count.get(sem, 0) + 1
        tok = (sem, self.count[sem])
        for k in reads:
            self.readers.setdefault(k, []).append(tok)
        for k in writes:
            self.last_w[k] = tok
            self.readers[k] = []
        self.ops.append((eng, list(fns), waits, sem, (inc if dma is not None else 0)))
        return tok

    def flush(self, final=False):
        nc = self.nc
        if not hasattr(self, 'sems'):
            self.sems = {}
            self.semstack = ExitStack()
            self.barrier_tot = {}
        for n in sorted(self.count.keys()):
            if n not in self.sems:
                self.sems[n] = self.semstack.enter_context(nc.semaphore('s_' + n))
        sems = self.sems
        barrier = sorted(self.barrier_tot.items())
        totals = sorted(self.count.items())
        per = {e: [] for e in ('pe', 'dve', 'act', 'pool', 'sp')}
        for op in self.ops:
            per[op[0]].append(op)
        with nc.Block() as block:
            def run(e, ops, fin):
                for (s, v) in barrier:
                    e.wait_ge(sems[s], v)
                for (_, fns, waits, sem, is_dma) in ops:
                    for (s, v) in waits:
                        e.wait_ge(sems[s], v)
                    ins = None
                    for f in fns:
                        ins = getattr(e, f[0])(**f[1]) if isinstance(f, tuple) else f(e)
                    ins.then_inc(sems[sem], is_dma if is_dma else 1)
                if fin:
                    for (s, v) in totals:
                        e.wait_ge(sems[s], v)

            @block.tensor
            def _(e):
                run(e, per['pe'], False)

            @block.vector
            def _(e):
                run(e, per['dve'], False)

            @block.scalar
            def _(e):
                run(e, per['act'], False)

            @block.gpsimd
            def _(e):
                run(e, per['pool'], False)

            @block.sync
            def _(e):
                run(e, per['sp'], final)
        self.barrier_tot = dict(self.count)
        self.ops = []
        self.last_w = {}
        self.readers = {}
        for e in per:
            for sname, v in self.count.items():
                self.waited[(e, sname)] = v

    def emit(self, final_waits):
        self.flush(final=True)


class Prog:
    def __init__(self):
        self.nc = bass.Bass("TRN2", target_bir_lowering=False)
        self.rec = Rec(self.nc)
        self.ext = {}

    def scratch(self, name, shape, dt):
        return self.nc.dram_tensor(name, list(shape), dt)


class Ctx:
    def __init__(self, P=None, prefix='', bind=None):
        self.P = P or Prog()
        self.standalone = P is None
        self.nc = self.P.nc
        self.rec = self.P.rec
        self.prefix = prefix
        self.bind = bind or {}
        self.st = ExitStack()
        self.n = 0

    def sb(self, shape, dt, name=None):
        self.n += 1
        return self.st.enter_context(self.nc.sbuf_tensor(f"sb_{self.prefix}{name or self.n}", list(shape), dt))

    def ps(self, shape, dt=F32, name=None):
        self.n += 1
        return self.st.enter_context(self.nc.psum_tensor(f"ps_{self.prefix}{name or self.n}", list(shape), dt))

    def dram_in(self, name, shape, dt=F32):
        if name in self.bind:
            return self.bind[name]
        self.P.ext[self.prefix + name] = (self.prefix, name)
        return self.nc.dram_tensor(self.prefix + name, list(shape), dt, kind="ExternalInput").ap()

    def dram_out(self, name, shape, dt=F32):
        if name in self.bind:
            return self.bind[name]
        return self.nc.dram_tensor(self.prefix + name, list(shape), dt, kind="ExternalOutput").ap()

    def scratch(self, name, shape, dt):
        return self.nc.dram_tensor(self.prefix + name, list(shape), dt)

    def finish(self):
        self.rec.flush(final=self.standalone)
        self.st.close()
        return self.nc


class Ring:
    def __init__(self, C, n, shape, dt, name, psum=False):
        self.t = [(C.ps(shape, dt, f"{name}{i}") if psum else C.sb(shape, dt, f"{name}{i}")) for i in range(n)]
        self.k = [f"{name}{i}" for i in range(n)]
        self.i = 0

    def next(self):
        j = self.i % len(self.t)
        self.i += 1
        return self.t[j], self.k[j]


def build_token_phase(cfg, P=None, prefix='', bind=None):
    C = Ctx(P, prefix, bind)
    nc, R = C.nc, C.rec
    mix, mix_k, n_ffn, tail = cfg['mix'], cfg.get('mix_k', 0), cfg['n_ffn'], cfg['tail']
    NT = TOK // 512

    x_d = C.dram_in("xT", [D, TOK])
    if mix:
        o_d = C.dram_in("oT", [mix_k, L], BF16)
        hfm_d = C.dram_in("hfm", [128, 2])
        wm_d = C.dram_in("w_mix", [mix_k, 2048 if mix == 'glu' else 1024])
    wg_d = [C.dram_in(f"wg{f}", [D, DFF]) for f in range(n_ffn)]
    wu_d = [C.dram_in(f"wu{f}", [D, DFF]) for f in range(n_ffn)]
    wd_d = [C.dram_in(f"wd{f}", [DFF, D]) for f in range(n_ffn)]
    ng_d = [C.dram_in(f"ng{f}", [128, 8]) for f in range(n_ffn)]
    if tail != 'x':
        ngt_d = C.dram_in("ngt", [128, 8])
    if tail == 'final':
        out_d = C.dram_out("out", [D, TOK])
    else:
        xo_d = C.dram_out("xo", [D, TOK])
        if tail == 'h':
            ho_d = C.dram_out("ho", [D, TOK], BF16)

    xT = C.sb([128, 8, TOK], F32, "xT")
    hT = C.sb([128, 8, TOK], BF16, "hT")
    ones = C.sb([128, 128], BF16, "ones")
    ng_s = [C.sb([128, 8], F32, f"ng_s{f}") for f in range(n_ffn)]
    if tail != 'x':
        ngt_s = C.sb([128, 8], F32, "ngt_s")
    sq = [C.sb([128, 512], BF16, f"sq{i}") for i in range(4)]
    rstd = C.sb([128, 512], F32, "rstd")
    GS = 2
    NG = DFF // 128 // GS
    wgs = [C.sb([128, 8, GS * 128], BF16, f"wgs{i}") for i in range(2)]
    wus = [C.sb([128, 8, GS * 128], BF16, f"wus{i}") for i in range(2)]
    wds = [C.sb([128, GS, D], BF16, f"wds{i}") for i in range(2)]
    sg = [C.sb([128, 512], F32, f"sg{i}") for i in range(2)]
    act = [C.sb([128, 512], BF16, f"act{i}") for i in range(4)]
    pg = [C.ps([128, 512], F32, f"pg{i}") for i in range(2)]
    pu = [C.ps([128, 512], F32, f"pu{i}") for i in range(2)]
    pd = [C.ps([128, 512], F32, f"pd{i}") for i in range(4)]

    R.add('pool', I('memset', ap=ones[:], constant=1.0), writes=['ones'])
    for kc in range(8):
        R.add('sp', I('dma_start', out=xT[:, kc, :], in_=x_d[kc * 128:(kc + 1) * 128, :]),
              writes=[f'x{kc}_{t}' for t in range(NT)], dma=f'x{kc}')
    for f in range(n_ffn):
        R.add('sp', I('dma_start', out=ng_s[f][:], in_=ng_d[f]), writes=[f'ng{f}'], dma='ng')
    if tail != 'x':
        R.add('sp', I('dma_start', out=ngt_s[:], in_=ngt_d), writes=['ngt'], dma='ng')

    cnt = {'sq': 0, 'pd': 0, 'gu': 0, 'act': 0, 'w': 0}

    def rms_to_h(t, g_tile, gkey, out_tile=None):
        ts = slice(t * 512, (t + 1) * 512)
        b = cnt['pd'] % 4
        cnt['pd'] += 1
        for kc in range(8):
            i = cnt['sq'] % 4
            cnt['sq'] += 1
            R.add('act', I('activation', out=sq[i][:], in_=xT[:, kc, ts], func=AF.Square),
                  reads=[f'x{kc}_{t}'], writes=[f'sq{i}'])
            R.add('pe', I('matmul', out=pd[b][:], lhsT=ones[:], rhs=sq[i][:], start=(kc == 0), stop=(kc == 7)),
                  reads=[f'sq{i}', 'ones'], writes=[f'pd{b}'])
        R.add('act', I('activation', out=rstd[:], in_=pd[b][:], func=AF.Ln, scale=1.0 / D, bias=EPS), writes=[f'pd{b}', 'rstd'])
        R.add('act', I('activation', out=rstd[:], in_=rstd[:], func=AF.Exp, scale=-0.5), writes=['rstd'])
        for kc in range(8):
            dst = (hT if out_tile is None else out_tile)[:, kc, ts]
            R.add('dve', I('scalar_tensor_tensor', out=dst, in0=xT[:, kc, ts], scalar=g_tile[:, kc:kc + 1],
                           in1=rstd[:], op0=ALU.mult, op1=ALU.mult),
                  reads=[f'x{kc}_{t}', 'rstd', gkey], writes=[f'h{kc}_{t}' if out_tile is None else f'x{kc}_{t}'])

    if mix:
        nk = mix_k // 128
        ncol = 2048 if mix == 'glu' else 1024
        oTr = Ring(C, 2 if nk <= 8 else 1, [128, nk, 512], BF16, "oTr")
        oTr2 = Ring(C, 1, [128, nk, 512], BF16, "oTrB")
        hfm = C.sb([128, 2], F32, "hfm")
        R.add('sp', I('dma_start', out=hfm[:], in_=hfm_d), writes=['hfm'], dma='ng')
        wm = C.sb([128, nk, ncol], BF16, "wm")
        for k in range(nk):
            R.add('pool', I('dma_start', out=wm[:, k, :], in_=wm_d[k * 128:(k + 1) * 128, :]), writes=[f'wm{k}'], dma='wm')
        wkeys = [f'wm{k}' for k in range(nk)]
        for t in range(NT):
            ts = slice(t * 512, (t + 1) * 512)
            oT, okey = oTr.next()
            oB, okeyB = oTr2.next()
            for k in range(nk):
                rr = orow(k, nk)
                R.add('sp', I('dma_start', out=oT[:, k, :], in_=o_d[rr:rr + 128, ts]), writes=[okey], dma='o' + okey)
                R.add('sp', I('dma_start', out=oB[:, k, :], in_=o_d[rr:rr + 128, TOK + t * 512:TOK + (t + 1) * 512]),
                      writes=[okeyB], dma='o' + okeyB)
            R.add('dve', I('tensor_scalar', out=oT[:], in0=oT[:], scalar1=hfm[:, 0:1], scalar2=None, op0=ALU.mult),
                  reads=['hfm'], writes=[okey])
            R.add('dve', I('scalar_tensor_tensor', out=oT[:], in0=oB[:], scalar=hfm[:, 1:2], in1=oT[:], op0=ALU.mult, op1=ALU.add),
                  reads=['hfm', okeyB], writes=[okey])
            okeys = wkeys + [okey]
            for c in range(8):
                b = cnt['pd'] % 4
                cnt['pd'] += 1
                R.add('pe', [I('matmul', out=pd[b][:], lhsT=wm[:, k, c * 128:(c + 1) * 128], rhs=oT[:, k, :],
                               start=(k == 0), stop=(k == nk - 1)) for k in range(nk)],
                      reads=okeys, writes=[f'pd{b}'])
                if mix == 'lin':
                    R.add('dve', I('tensor_tensor', out=xT[:, c, ts], in0=pd[b][:], in1=xT[:, c, ts], op=ALU.add),
                          reads=[f'x{c}_{t}'], writes=[f'pd{b}', f'x{c}_{t}'])
                else:
                    b2 = cnt['pd'] % 4
                    cnt['pd'] += 1
                    R.add('pe', [I('matmul', out=pd[b2][:], lhsT=wm[:, k, 1024 + c * 128:1024 + (c + 1) * 128], rhs=oT[:, k, :],
                                   start=(k == 0), stop=(k == nk - 1)) for k in range(nk)],
                          reads=okeys, writes=[f'pd{b2}'])
                    i = cnt['gu'] % 2
                    cnt['gu'] += 1
                    R.add('act', I('activation', out=sg[i][:], in_=pd[b2][:], func=AF.Sigmoid),
                          writes=[f'pd{b2}', f'sg{i}'])
                    R.add('dve', I('tensor_tensor', out=sg[i][:], in0=pd[b][:], in1=sg[i][:], op=ALU.mult),
                          writes=[f'pd{b}', f'sg{i}'])
                    R.add('dve', I('tensor_tensor', out=xT[:, c, ts], in0=sg[i][:], in1=xT[:, c, ts], op=ALU.add),
                          reads=[f'sg{i}', f'x{c}_{t}'], writes=[f'x{c}_{t}'])

    for f in range(n_ffn):
        for t in range(NT):
            rms_to_h(t, ng_s[f], f'ng{f}')
        for g in range(NG):
            wi = cnt['w'] % 2
            cnt['w'] += 1
            cs = slice(g * GS * 128, (g + 1) * GS * 128)
            R.add('pool', I('dma_start', out=wgs[wi][:], in_=wg_d[f][:, cs].rearrange("(kc p) c -> p kc c", p=128)),
                  writes=[f'wg{wi}'], dma=f'wg{wi}')
            R.add('pool', I('dma_start', out=wus[wi][:], in_=wu_d[f][:, cs].rearrange("(kc p) c -> p kc c", p=128)),
                  writes=[f'wu{wi}'], dma=f'wu{wi}')
            R.add('pool', I('dma_start', out=wds[wi][:], in_=wd_d[f][cs, :].rearrange("(j p) c -> p j c", p=128)),
                  writes=[f'wd{wi}'], dma=f'wd{wi}')
            for t in range(NT):
                ts = slice(t * 512, (t + 1) * 512)
                hk = [f'h{kc}_{t}' for kc in range(8)]
                acts = []
                for j in range(GS):
                    gi = cnt['gu'] % 2
                    cnt['gu'] += 1
                    R.add('pe', [I('matmul', out=pg[gi][:], lhsT=wgs[wi][:, kc, j * 128:(j + 1) * 128], rhs=hT[:, kc, ts],
                                   start=(kc == 0), stop=(kc == 7)) for kc in range(8)],
                          reads=hk + [f'wg{wi}'], writes=[f'pg{gi}'])
                    R.add('pe', [I('matmul', out=pu[gi][:], lhsT=wus[wi][:, kc, j * 128:(j + 1) * 128], rhs=hT[:, kc, ts],
                                   start=(kc == 0), stop=(kc == 7)) for kc in range(8)],
                          reads=hk + [f'wu{wi}'], writes=[f'pu{gi}'])
                    R.add('act', I('activation', out=sg[gi][:], in_=pg[gi][:], func=AF.Silu),
                          reads=[f'pg{gi}'], writes=[f'sg{gi}'])
                    ai = cnt['act'] % 4
                    cnt['act'] += 1
                    R.add('dve', I('tensor_tensor', out=act[ai][:], in0=pu[gi][:], in1=sg[gi][:], op=ALU.mult),
                          reads=[f'pu{gi}', f'sg{gi}'], writes=[f'act{ai}'])
                    acts.append(ai)
                for c in range(8):
                    b = cnt['pd'] % 4
                    cnt['pd'] += 1
                    R.add('pe', [I('matmul', out=pd[b][:], lhsT=wds[wi][:, j, c * 128:(c + 1) * 128], rhs=act[acts[j]][:],
                                   start=(j == 0), stop=(j == GS - 1)) for j in range(GS)],
                          reads=[f'act{a}' for a in acts] + [f'wd{wi}'], writes=[f'pd{b}'])
                    R.add('dve', I('scalar_tensor_tensor', out=xT[:, c, ts], in0=pd[b][:], scalar=0.5, in1=xT[:, c, ts],
                                   op0=ALU.mult, op1=ALU.add),
                          reads=[f'pd{b}', f'x{c}_{t}'], writes=[f'x{c}_{t}'])

    fin = []
    if tail in ('h', 'x'):
        for kc in range(8):
            tok = R.add('sp', I('dma_start', out=xo_d[kc * 128:(kc + 1) * 128, :], in_=xT[:, kc, :]),
                        reads=[f'x{kc}_{t}' for t in range(NT)], dma='xo')
        fin.append(tok)
    if tail == 'h':
        for t in range(NT):
            rms_to_h(t, ngt_s, 'ngt')
        for kc in range(8):
            tok = R.add('sp', I('dma_start', out=ho_d[kc * 128:(kc + 1) * 128, :], in_=hT[:, kc, :]),
                        reads=[f'h{kc}_{t}' for t in range(NT)], dma='ho')
        fin.append(tok)
    if tail == 'final':
        for t in range(NT):
            rms_to_h(t, ngt_s, 'ngt', out_tile=xT)
        for kc in range(8):
            tok = R.add('sp', I('dma_start', out=out_d[kc * 128:(kc + 1) * 128, :], in_=xT[:, kc, :]),
                        reads=[f'x{kc}_{t}' for t in range(NT)], dma='fo')
        fin.append(tok)
    return C.finish()


def hrow(r, kc):
    return (kc // 4) * 1024 + r * 512 + (kc % 4) * 128


def orow(k, nk):
    half = nk // 2
    r, lk = k // half, k % half
    return (lk // 2) * 512 + r * 256 + (lk % 2) * 128


def h_gather_layout(h0, h1):
    a = np.stack([h0.reshape(2, 512, -1), h1.reshape(2, 512, -1)], axis=1)
    return np.ascontiguousarray(a.reshape(2048, -1))


def o_gather_layout(o0, o1):
    half = o0.shape[0]
    a = np.stack([o0.reshape(half // 256, 256, -1), o1.reshape(half // 256, 256, -1)], axis=1)
    return np.ascontiguousarray(a.reshape(2 * half, -1))


def hfm_arr(hf):
    a = np.zeros((128, 2), np.float32)
    a[:, hf] = 1.0
    return a


def fm(v):
    return np.ascontiguousarray(np.asarray(v, np.float32).reshape(8, 128).T)


def run(nc, in_maps):
    res = run_bass_kernel_spmd(nc, in_maps, core_ids=list(range(8)))
    return res.results


MAGIC = 12582912.0
TWO_PI = 2.0 * math.pi
PI_SAFE = 3.1415925


def emit_sin(R, eng, out, okey, ang, akey, tmp, tkey, shift=0.0):
    R.add(eng, I('tensor_scalar', out=tmp, in0=ang, scalar1=shift, scalar2=1.0 / TWO_PI, op0=ALU.add, op1=ALU.mult),
          reads=[akey], writes=[tkey])
    R.add(eng, I('tensor_scalar', out=tmp, in0=tmp, scalar1=MAGIC, scalar2=-MAGIC, op0=ALU.add, op1=ALU.add),
          reads=[tkey], writes=[tkey])
    R.add(eng, I('scalar_tensor_tensor', out=tmp, in0=tmp, scalar=-TWO_PI, in1=ang, op0=ALU.mult, op1=ALU.add),
          reads=[tkey, akey], writes=[tkey])
    if shift != 0.0:
        R.add(eng, I('tensor_scalar', out=tmp, in0=tmp, scalar1=shift, scalar2=None, op0=ALU.add),
              reads=[tkey], writes=[tkey])
    R.add(eng, I('tensor_scalar', out=tmp, in0=tmp, scalar1=-PI_SAFE, scalar2=PI_SAFE, op0=ALU.max, op1=ALU.min),
          reads=[tkey], writes=[tkey])
    R.add('act', I('activation', out=out, in_=tmp, func=AF.Sin), reads=[tkey], writes=[okey])


def build_s5(P=None, prefix='', bind=None):
    C = Ctx(P, prefix, bind)
    nc, R = C.nc, C.rec
    T = 512
    NCH = L // T
    u_d = C.dram_in("hT", [2 * D, TOK], BF16)
    hfm_d = C.dram_in("hfm", [128, 2])
    lre_d = C.dram_in("lam_re", [128, 16])
    lim_d = C.dram_in("lam_im", [128, 16])
    ldt_d = C.dram_in("log_dt", [128, 16])
    bre_d = C.dram_in("bre", [16, 128, 128])
    bim_d = C.dram_in("bim", [16, 128, 128])
    cre_d = C.dram_in("cre", [16, 128, 128])
    cim_d = C.dram_in("cim", [16, 128, 128])
    d_d = C.dram_in("dsk", [128, 4])
    id_d = C.dram_in("ident", [128, 128])
    y_d = C.dram_out("ygT", [512, L], BF16)

    uT = C.sb([128, 4, L], F32, "uT")
    ident = C.sb([128, 128], F32, "ident")
    dsk = C.sb([128, 4], F32, "dsk")
    P = {}
    for n in ('lre', 'lim', 'ldt', 'dt', 'a', 'th', 'r', 'sn', 'cs', 'lbr', 'lbi', 'den', 'cr', 'ci', 'thT', 'snT', 'csT',
              'tmp', 'tmp2'):
        P[n] = C.sb([128, 16], F32, "p_" + n)
    bre = C.sb([128, 16, 128], F32, "bre")
    bim = C.sb([128, 16, 128], F32, "bim")
    creb = C.sb([128, 16, 128], F32, "creb")
    cimb = C.sb([128, 16, 128], F32, "cimb")
    bbre = C.sb([128, 16, 128], F32, "bbre")
    bbim = C.sb([128, 16, 128], F32, "bbim")
    io = C.sb([128, T], F32, "io")
    cosT = [C.sb([128, T], F32, f"cost{q}") for q in range(4)]
    sinT = [C.sb([128, T], F32, f"sint{q}") for q in range(4)]
    st_re = [C.sb([128, 1], F32, f"stre{q}") for q in range(16)]
    st_im = [C.sb([128, 1], F32, f"stim{q}") for q in range(16)]
    sc = Ring(C, 14, [128, T], F32, "sc")
    gg = Ring(C, 6, [128, T], F32, "gg")
    hb = Ring(C, 4, [128, T], F32, "hb")
    yb = Ring(C, 2, [128, T], F32, "yb")
    yo = Ring(C, 2, [128, T], BF16, "yo")
    c1 = Ring(C, 4, [128, 1], F32, "c1")
    pb = Ring(C, 4, [128, T], F32, "pb", psum=True)
    py = Ring(C, 2, [128, T], F32, "py", psum=True)
    pt = Ring(C, 2, [128, 512], F32, "pt", psum=True)

    hfm = C.sb([128, 2], F32, "hfm")
    uhi = Ring(C, 2, [128, L], BF16, "uhi")
    R.add('sp', I('dma_start', out=hfm[:], in_=hfm_d), writes=['hfm'], dma='c')
    for k in range(4):
        hi, khi = uhi.next()
        for r in range(2):
            R.add('pool', I('dma_start', out=uT[:, k, r * 2048:(r + 1) * 2048], in_=u_d[hrow(r, k):hrow(r, k) + 128, :]),
                  writes=[f'u{k}'], dma=f'u{k}')
            R.add('sp', I('dma_start', out=hi[:, r * 2048:(r + 1) * 2048], in_=u_d[hrow(r, 4 + k):hrow(r, 4 + k) + 128, :]),
                  writes=[khi], dma=f'uh{k}')
        R.add('dve', I('tensor_scalar', out=uT[:, k, :], in0=uT[:, k, :], scalar1=hfm[:, 0:1], scalar2=None, op0=ALU.mult),
              reads=['hfm'], writes=[f'u{k}'])
        R.add('dve', I('scalar_tensor_tensor', out=uT[:, k, :], in0=hi[:], scalar=hfm[:, 1:2], in1=uT[:, k, :], op0=ALU.mult, op1=ALU.add),
              reads=['hfm', khi], writes=[f'u{k}'])
    R.add('sp', I('dma_start', out=ident[:], in_=id_d), writes=['ident'], dma='c')
    R.add('sp', I('dma_start', out=dsk[:], in_=d_d), writes=['dsk'], dma='c')
    R.add('sp', I('dma_start', out=P['lre'][:], in_=lre_d), writes=['lre'], dma='c')
    R.add('sp', I('dma_start', out=P['lim'][:], in_=lim_d), writes=['lim'], dma='c')
    R.add('sp', I('dma_start', out=P['ldt'][:], in_=ldt_d), writes=['ldt'], dma='c')
    R.add('sp', I('dma_start', out=bre[:], in_=bre_d.rearrange("t p c -> p t c")), writes=['bre'], dma='c')
    R.add('sp', I('dma_start', out=bim[:], in_=bim_d.rearrange("t p c -> p t c")), writes=['bim'], dma='c')
    R.add('pool', I('dma_start', out=creb[:], in_=cre_d.rearrange("t p c -> p t c")), writes=['creb'], dma='cb')
    R.add('pool', I('dma_start', out=cimb[:], in_=cim_d.rearrange("t p c -> p t c")), writes=['cimb'], dma='cb')
    R.add('pool', I('iota', out=io[:], pattern=[[1, T]], base=0, channel_multiplier=0, allow_small_or_imprecise_dtypes=True),
          writes=['io'])

    def v(n):
        return P[n][:]

    def op2(out, a, b, op, eng='dve'):
        R.add(eng, I('tensor_tensor', out=v(out), in0=v(a), in1=v(b), op=op), reads=[a, b], writes=[out])

    R.add('act', I('activation', out=v('dt'), in_=v('ldt'), func=AF.Exp), reads=['ldt'], writes=['dt'])
    R.add('dve', I('tensor_scalar', out=v('lre'), in0=v('lre'), scalar1=-1e-4, scalar2=None, op0=ALU.min),
          reads=['lre'], writes=['lre'])
    op2('a', 'lre', 'dt', ALU.mult)
    op2('th', 'lim', 'dt', ALU.mult)
    R.add('act', I('activation', out=v('r'), in_=v('a'), func=AF.Exp), reads=['a'], writes=['r'])
    emit_sin(R, 'dve', v('sn'), 'sn', v('th'), 'th', v('tmp'), 'tmp')
    emit_sin(R, 'dve', v('cs'), 'cs', v('th'), 'th', v('tmp2'), 'tmp2', shift=math.pi / 2)
    op2('lbr', 'r', 'cs', ALU.mult)
    op2('lbi', 'r', 'sn', ALU.mult)
    R.add('dve', I('tensor_scalar', out=v('thT'), in0=v('th'), scalar1=float(T), scalar2=None, op0=ALU.mult),
          reads=['th'], writes=['thT'])
    emit_sin(R, 'dve', v('snT'), 'snT', v('thT'), 'thT', v('tmp'), 'tmp')
    emit_sin(R, 'dve', v('csT'), 'csT', v('thT'), 'thT', v('tmp2'), 'tmp2', shift=math.pi / 2)
    op2('den', 'lre', 'lre', ALU.mult)
    op2('tmp', 'lim', 'lim', ALU.mult)
    op2('den', 'den', 'tmp', ALU.add)
    R.add('dve', I('reciprocal', out=v('den'), in_=v('den')), reads=['den'], writes=['den'])
    R.add('dve', I('tensor_scalar', out=v('tmp'), in0=v('lbr'), scalar1=-1.0, scalar2=None, op0=ALU.add),
          reads=['lbr'], writes=['tmp'])
    op2('cr', 'tmp', 'lre', ALU.mult)
    op2('tmp2', 'lbi', 'lim', ALU.mult)
    op2('cr', 'cr', 'tmp2', ALU.add)
    op2('cr', 'cr', 'den', ALU.mult)
    op2('ci', 'lbi', 'lre', ALU.mult)
    op2('tmp2', 'tmp', 'lim', ALU.mult)
    op2('ci', 'ci', 'tmp2', ALU.subtract)
    op2('ci', 'ci', 'den', ALU.mult)

    for tl in range(16):
        t1, k1 = sc.next()
        t2, k2 = sc.next()
        R.add('dve', I('tensor_scalar', out=t1[:, 0:128], in0=bim[:, tl, :], scalar1=P['ci'][:, tl:tl + 1], scalar2=None,
                       op0=ALU.mult), reads=['bim', 'ci'], writes=[k1])
        R.add('dve', I('scalar_tensor_tensor', out=t1[:, 0:128], in0=bre[:, tl, :], scalar=P['cr'][:, tl:tl + 1],
                       in1=t1[:, 0:128], op0=ALU.mult, op1=ALU.subtract), reads=['bre', 'cr', k1], writes=[k1])
        R.add('dve', I('tensor_scalar', out=t2[:, 0:128], in0=bre[:, tl, :], scalar1=P['ci'][:, tl:tl + 1], scalar2=None,
                       op0=ALU.mult), reads=['bre', 'ci'], writes=[k2])
        R.add('dve', I('scalar_tensor_tensor', out=t2[:, 0:128], in0=bim[:, tl, :], scalar=P['cr'][:, tl:tl + 1],
                       in1=t2[:, 0:128], op0=ALU.mult, op1=ALU.add), reads=['bim', 'cr', k2], writes=[k2])
        for (src, ks, dst, kd) in ((t1, k1, bbre, 'bbre'), (t2, k2, bbim, 'bbim')):
            p, kp = pt.next()
            R.add('pe', I('transpose', out=p[:, 0:128], in_=src[:, 0:128], identity=ident[:]), reads=[ks, 'ident'], writes=[kp])
            R.add('act', I('activation', out=dst[:, tl, :], in_=p[:, 0:128], func=AF.Copy), writes=[kp, f'{kd}{tl}'])

    fin = []
    for cq in range(4):
        for q in range(4):
            tl = cq * 4 + q
            a_t, a_k = sc.next()
            R.add('dve', I('tensor_scalar', out=a_t[:], in0=io[:], scalar1=P['th'][:, tl:tl + 1], scalar2=None, op0=ALU.mult),
                  reads=['io', 'th'], writes=[a_k])
            t_t, t_k = sc.next()
            emit_sin(R, 'dve', sinT[q][:], f'sint{q}', a_t[:], a_k, t_t[:], t_k)
            t_t, t_k = sc.next()
            emit_sin(R, 'dve', cosT[q][:], f'cost{q}', a_t[:], a_k, t_t[:], t_k, shift=math.pi / 2)
            R.add('pool', I('memset', ap=st_re[tl][:], constant=0.0), writes=[f'stre{tl}'])
            R.add('pool', I('memset', ap=st_im[tl][:], constant=0.0), writes=[f'stim{tl}'])
        items = [(c, q) for c in range(NCH) for q in range(4)]
        st8 = {}

        def front_a(i):
            c, q = items[i]
            ts = slice(c * T, (c + 1) * T)
            tl = cq * 4 + q
            cs_, sn_ = cosT[q][:], sinT[q][:]
            ck, sk = f'cost{q}', f'sint{q}'
            p_re, k_pre = pb.next()
            p_im, k_pim = pb.next()
            R.add('pe', I('matmul', out=p_re[:], lhsT=bbre[:, tl, :], rhs=uT[:, cq, ts], start=True, stop=True),
                  reads=[f'bbre{tl}', f'u{cq}'], writes=[k_pre])
            R.add('pe', I('matmul', out=p_im[:], lhsT=bbim[:, tl, :], rhs=uT[:, cq, ts], start=True, stop=True),
                  reads=[f'bbim{tl}', f'u{cq}'], writes=[k_pim])
            b_re, k_bre = sc.next()
            b_im, k_bim = sc.next()
            R.add('act', I('activation', out=b_re[:], in_=p_re[:], func=AF.Copy), writes=[k_pre, k_bre])
            R.add('act', I('activation', out=b_im[:], in_=p_im[:], func=AF.Copy), writes=[k_pim, k_bim])
            t1, k1 = sc.next()
            t2, k2 = sc.next()
            R.add('pool', I('tensor_tensor', out=t1[:], in0=b_re[:], in1=cs_, op=ALU.mult), reads=[k_bre, ck], writes=[k1])
            R.add('pool', I('tensor_tensor', out=t2[:], in0=b_im[:], in1=sn_, op=ALU.mult), reads=[k_bim, sk], writes=[k2])
            st8[i] = (t1, k1, t2, k2, b_re, k_bre, b_im, k_bim)

        def front_b(i):
            c, q = items[i]
            cs_, sn_ = cosT[q][:], sinT[q][:]
            ck, sk = f'cost{q}', f'sint{q}'
            t1, k1, t2, k2, b_re, k_bre, b_im, k_bim = st8[i]
            R.add('dve', I('tensor_tensor', out=t1[:], in0=t1[:], in1=t2[:], op=ALU.add), reads=[k1, k2], writes=[k1])
            R.add('dve', I('tensor_tensor', out=b_im[:], in0=b_im[:], in1=cs_, op=ALU.mult), reads=[ck, k2], writes=[k_bim])
            R.add('dve', I('tensor_tensor', out=b_re[:], in0=b_re[:], in1=sn_, op=ALU.mult), reads=[sk, k1], writes=[k_bre])
            R.add('dve', I('tensor_tensor', out=t2[:], in0=b_im[:], in1=b_re[:], op=ALU.subtract),
                  reads=[k_bim, k_bre, k1], writes=[k2])

        ybank = {}
        st9 = {}

        def back_a(i):
            c, q = items[i]
            tl = cq * 4 + q
            t1, k1, t2, k2 = st8.pop(i)[:4]
            i_re, k_ire = c1.next()
            i_im, k_iim = c1.next()
            tt, k_tt = c1.next()
            csT_, snT_ = P['csT'][:, tl:tl + 1], P['snT'][:, tl:tl + 1]
            R.add('dve', I('tensor_tensor', out=tt[:], in0=st_im[tl][:], in1=snT_, op=ALU.mult),
                  reads=[f'stim{tl}', 'snT'], writes=[k_tt])
            R.add('dve', I('scalar_tensor_tensor', out=i_re[:], in0=st_re[tl][:], scalar=csT_, in1=tt[:],
                           op0=ALU.mult, op1=ALU.subtract), reads=[f'stre{tl}', 'csT', k_tt], writes=[k_ire])
            R.add('dve', I('tensor_tensor', out=tt[:], in0=st_im[tl][:], in1=csT_, op=ALU.mult),
                  reads=[f'stim{tl}', 'csT', k_tt], writes=[k_tt])
            R.add('dve', I('scalar_tensor_tensor', out=i_im[:], in0=st_re[tl][:], scalar=snT_, in1=tt[:],
                           op0=ALU.mult, op1=ALU.add), reads=[f'stre{tl}', 'snT', k_tt], writes=[k_iim])
            g_re, k_gre = gg.next()
            g_im, k_gim = gg.next()
            rb = P['r'][:, tl:tl + 1].to_broadcast([128, T])
            R.add('dve', I('tensor_tensor_scan', out=g_re[:], data0=rb, data1=t1[:], initial=i_re[:, 0:1],
                           op0=ALU.mult, op1=ALU.add), reads=['r', k1, k_ire], writes=[k_gre])
            R.add('dve', I('tensor_tensor_scan', out=g_im[:], data0=rb, data1=t2[:], initial=i_im[:, 0:1],
                           op0=ALU.mult, op1=ALU.add), reads=['r', k2, k_iim], writes=[k_gim])
            R.add('act', I('activation', out=st_re[tl][:], in_=g_re[:, T - 1:T], func=AF.Copy),
                  reads=[k_gre, k_ire, k_iim], writes=[f'stre{tl}'])
            R.add('act', I('activation', out=st_im[tl][:], in_=g_im[:, T - 1:T], func=AF.Copy),
                  reads=[k_gim, k_ire, k_iim], writes=[f'stim{tl}'])
            st9[i] = (t1, k1, t2, k2, g_re, k_gre, g_im, k_gim)

        def back_b(i):
            c, q = items[i]
            ts = slice(c * T, (c + 1) * T)
            tl = cq * 4 + q
            cs_, sn_ = cosT[q][:], sinT[q][:]
            ck, sk = f'cost{q}', f'sint{q}'
            t1, k1, t2, k2, g_re, k_gre, g_im, k_gim = st9.pop(i)
            if q == 0:
                ybank[c] = py.next()
            y_p, y_k = ybank[c]
            h_re, k_hre = hb.next()
            h_imn, k_him = hb.next()
            R.add('pool', I('tensor_tensor', out=t1[:], in0=g_re[:], in1=cs_, op=ALU.mult), reads=[k_gre, ck], writes=[k1])
            R.add('pool', I('tensor_tensor', out=t2[:], in0=g_im[:], in1=sn_, op=ALU.mult), reads=[k_gim, sk], writes=[k2])
            R.add('dve', I('tensor_tensor', out=g_re[:], in0=g_re[:], in1=sn_, op=ALU.mult), reads=[sk, k1], writes=[k_gre])
            R.add('dve', I('tensor_tensor', out=g_im[:], in0=g_im[:], in1=cs_, op=ALU.mult), reads=[ck, k2], writes=[k_gim])
            R.add('dve', I('scalar_tensor_tensor', out=h_imn[:], in0=g_re[:], scalar=-1.0, in1=g_im[:],
                           op0=ALU.mult, op1=ALU.subtract), reads=[k_gre, k_gim], writes=[k_him])
            R.add('dve', I('tensor_tensor', out=h_re[:], in0=t1[:], in1=t2[:], op=ALU.subtract), reads=[k1, k2], writes=[k_hre])
            R.add('pe', [I('matmul', out=y_p[:], lhsT=creb[:, tl, :], rhs=h_re[:], start=(q == 0), stop=False),
                         I('matmul', out=y_p[:], lhsT=cimb[:, tl, :], rhs=h_imn[:], start=False, stop=(q == 3))],
                  reads=['creb', 'cimb', k_hre, k_him], writes=[y_k])
            if q == 3:
                y_s, ys_k = yb.next()
                R.add('dve', I('scalar_tensor_tensor', out=y_s[:], in0=uT[:, cq, ts], scalar=dsk[:, cq:cq + 1], in1=y_p[:],
                               op0=ALU.mult, op1=ALU.add), reads=[f'u{cq}', 'dsk'], writes=[y_k, ys_k])
                y_o, yo_k = yo.next()
                R.add('act', I('activation', out=y_o[:], in_=y_s[:], func=AF.Gelu_apprx_tanh), reads=[ys_k], writes=[yo_k])
                fin.append(R.add('sp', I('dma_start', out=y_d[cq * 128:(cq + 1) * 128, ts], in_=y_o[:]), reads=[yo_k], dma='yout'))

        n_it = len(items)
        front_a(0)
        front_a(1)
        front_b(0)
        for i in range(n_it):
            if i + 2 < n_it:
                front_a(i + 2)
            back_a(i)
            if i + 1 < n_it:
                front_b(i + 1)
            back_b(i)
    return C.finish()


def s5_inputs(inp, b, hf, hT_full):
    G0 = 32 * hf
    j = 0
    lam_re = inp['s5_lam_re'][j][G0:G0 + 32]
    lam_im = inp['s5_lam_im'][j][G0:G0 + 32]
    log_dt = inp['s5_log_dt'][j][G0:G0 + 32]
    bre = inp['s5_b_re'][j][G0:G0 + 32]
    bim = inp['s5_b_im'][j][G0:G0 + 32]
    cre = inp['s5_c_re'][j][G0:G0 + 32]
    cim = inp['s5_c_im'][j][G0:G0 + 32]

    def cols(a):
        return np.ascontiguousarray(a.reshape(16, 128).T.astype(np.float32))

    def padB(a):
        out = np.zeros((16, 128, 128), np.float32)
        for tl in range(16):
            q = tl % 4
            for s in range(2):
                g = 2 * tl + s
                out[tl, 64 * s:64 * s + 64, 32 * q + 16 * s:32 * q + 16 * s + 16] = a[g]
        return out

    def padC(a):
        return padB(np.ascontiguousarray(a.transpose(0, 2, 1)))

    return dict(hT=hT_full, hfm=hfm_arr(hf),
                lam_re=cols(lam_re), lam_im=cols(lam_im), log_dt=cols(np.repeat(log_dt[:, None], 64, axis=1)),
                bre=padB(bre), bim=padB(bim), cre=padC(cre), cim=padC(cim),
                dsk=np.ascontiguousarray(inp['s5_d'][j][512 * hf:512 * hf + 512].reshape(4, 128).T.astype(np.float32)),
                ident=np.eye(128, dtype=np.float32))


def build_hgrn(li, P=None, prefix='', bind=None):
    C = Ctx(P, prefix, bind)
    nc, R = C.nc, C.rec
    NBLK = L // 512
    CH = 32
    h_d = C.dram_in("hT", [2 * D, TOK], BF16)
    w_d = {n: C.dram_in("w_" + n, [D, 512]) for n in ('q', 'f', 'i', 'g')}
    lb_d = C.dram_in("lbraw", [128, 4, 4])
    ng_d = C.dram_in("hng", [128, 1])
    cm_d = C.dram_in("cmask", [128, 512])
    am_d = C.dram_in("amask", [128, 512])
    id_d = C.dram_in("identb", [128, 128], BF16)
    chm_d = C.dram_in("chm", [128, 4])
    o_d = C.dram_out("oT", [512, L], BF16)

    hT = C.sb([128, 8, L], BF16, "hT")
    W = {n: C.sb([128, 8, 512], BF16, "w_" + n) for n in ('q', 'f', 'i', 'g')}
    lbraw = C.sb([128, 4, 4], F32, "lbraw")
    lbs = C.sb([128, 4], F32, "lbs")
    lb = C.sb([128, 4], F32, "lb")
    omlb = C.sb([128, 4], F32, "omlb")
    hng = C.sb([128, 1], F32, "hng")
    cmask = C.sb([128, 512], F32, "cmask")
    amask = C.sb([128, 512], F32, "amask")
    identb = C.sb([128, 128], BF16, "identb")
    chm = C.sb([128, 4], F32, "chm")
    ones = C.sb([128, 128], F32, "ones")
    S = [C.sb([128, 128], F32, f"S{h}") for h in range(4)]
    Sb = [C.sb([128, 128], BF16, f"Sb{h}") for h in range(4)]
    qt = [C.sb([128, 512], BF16, f"qt{h}") for h in range(4)]
    kt = [C.sb([128, 512], BF16, f"kt{h}") for h in range(4)]
    ks = [C.sb([128, 512], BF16, f"ks{h}") for h in range(4)]
    eb = [C.sb([128, 512], F32, f"eb{h}") for h in range(4)]
    gs = [C.sb([128, 512], F32, f"gs{h}") for h in range(4)]
    oblk = C.sb([128, 4, 512], F32, "oblk")
    sc = Ring(C, 6, [128, 512], F32, "sc")
    vtm = Ring(C, 2, [128, 512], BF16, "vtm")
    atm = Ring(C, 2, [128, 512], BF16, "atm")
    ksm = Ring(C, 8, [128, 512], BF16, "ksm")
    ofin = Ring(C, 2, [128, 512], BF16, "ofin")
    sqbr = Ring(C, 2, [128, 512], BF16, "sqbr")
    onesb = C.sb([128, 128], BF16, "onesb")
    pp = Ring(C, 2, [128, 512], F32, "pp", psum=True)
    pk = C.ps([128, 1024], BF16, "pk")
    po = C.ps([128, 512], F32, "po")
    psu = [C.ps([128, 512], F32, f"psu{h}") for h in range(4)]

    for r in range(2):
        for kc in range(8):
            R.add('sp', I('dma_start', out=hT[:, kc, r * 2048:(r + 1) * 2048], in_=h_d[hrow(r, kc):hrow(r, kc) + 128, :]),
                  writes=[f'h{kc}_{r}'], dma=f'h{r}')
    for n in ('f', 'q', 'g', 'i'):
        R.add('pool', I('dma_start', out=W[n][:], in_=w_d[n].rearrange("(kc p) c -> p kc c", p=128)), writes=['w_' + n], dma='w_' + n)
    R.add('sp', I('dma_start', out=lbraw[:], in_=lb_d), writes=['lbraw'], dma='c')
    R.add('sp', I('dma_start', out=hng[:], in_=ng_d), writes=['hng'], dma='c')
    R.add('sp', I('dma_start', out=cmask[:], in_=cm_d), writes=['cmask'], dma='c')
    R.add('sp', I('dma_start', out=amask[:], in_=am_d), writes=['amask'], dma='c')
    R.add('sp', I('dma_start', out=identb[:], in_=id_d), writes=['identb'], dma='c')
    R.add('sp', I('dma_start', out=chm[:], in_=chm_d), writes=['chm'], dma='c')
    R.add('pool', I('memset', ap=ones[:], constant=1.0), writes=['ones'])
    R.add('pool', I('memset', ap=onesb[:], constant=1.0), writes=['onesb'])
    for h in range(4):
        R.add('pool', I('memset', ap=S[h][:], constant=0.0), writes=[f'S{h}'])
        R.add('pool', I('memset', ap=Sb[h][:], constant=0.0), writes=[f'Sb{h}'])
    R.add('act', I('activation', out=lbraw[:], in_=lbraw[:], func=AF.Exp), reads=['lbraw'], writes=['lbraw'])
    R.add('dve', I('tensor_tensor', out=lbs[:], in0=lbraw[:, 0, :], in1=lbraw[:, 1, :], op=ALU.add), reads=['lbraw'], writes=['lbs'])
    R.add('dve', I('tensor_tensor', out=lbs[:], in0=lbs[:], in1=lbraw[:, 2, :], op=ALU.add), reads=['lbraw', 'lbs'], writes=['lbs'])
    R.add('dve', I('tensor_tensor', out=lbs[:], in0=lbs[:], in1=lbraw[:, 3, :], op=ALU.add), reads=['lbraw', 'lbs'], writes=['lbs'])
    R.add('dve', I('reciprocal', out=lbs[:], in_=lbs[:]), reads=['lbs'], writes=['lbs'])
    R.add('dve', I('tensor_copy', out=lb[:], in_=lbraw[:, 1, :]), reads=['lbraw'], writes=['lb'])
    for k in range(2, li + 1):
        R.add('dve', I('tensor_tensor', out=lb[:], in0=lb[:], in1=lbraw[:, k, :], op=ALU.add), reads=['lbraw', 'lb'], writes=['lb'])
    R.add('dve', I('tensor_tensor', out=lb[:], in0=lb[:], in1=lbs[:], op=ALU.mult), reads=['lb', 'lbs'], writes=['lb'])
    R.add('dve', I('tensor_scalar', out=omlb[:], in0=lb[:], scalar1=-1.0, scalar2=1.0, op0=ALU.mult, op1=ALU.add),
          reads=['lb'], writes=['omlb'])

    fin = []
    for blk in range(NBLK):
        bs = slice(blk * 512, (blk + 1) * 512)
        hk = [f'h{kc}_{blk // 4}' for kc in range(8)]
        for h in range(4):
            hs = slice(h * 128, (h + 1) * 128)

            def proj(n):
                p, kp = pp.next()
                R.add('pe', [I('matmul', out=p[:], lhsT=W[n][:, kc, hs], rhs=hT[:, kc, bs], start=(kc == 0), stop=(kc == 7))
                             for kc in range(8)], reads=hk + ['w_' + n], writes=[kp])
                return p, kp

            p, kp = proj('f')
            fg, kfg = sc.next()
            R.add('act', I('activation', out=fg[:], in_=p[:], func=AF.Sigmoid), writes=[kp, kfg])
            R.add('dve', I('tensor_scalar', out=fg[:], in0=fg[:], scalar1=omlb[:, h:h + 1], scalar2=lb[:, h:h + 1],
                           op0=ALU.mult, op1=ALU.add), reads=[kfg, 'omlb', 'lb'], writes=[kfg])
            lf, klf = sc.next()
            R.add('act', I('activation', out=lf[:], in_=fg[:], func=AF.Ln), reads=[kfg], writes=[klf])
            R.add('pool', I('tensor_scalar', out=fg[:], in0=fg[:], scalar1=-1.0, scalar2=1.0, op0=ALU.mult, op1=ALU.add),
                  reads=[kfg, klf], writes=[kfg])
            bb, kbb = sc.next()
            R.add('dve', I('tensor_tensor_scan', out=bb[:], data0=cmask[:], data1=lf[:], initial=0.0,
                           op0=ALU.mult, op1=ALU.add), reads=['cmask', klf], writes=[kbb])
            R.add('act', I('activation', out=eb[h][:], in_=bb[:], func=AF.Exp), reads=[kbb], writes=[f'eb{h}'])
            R.add('act', I('activation', out=lf[:], in_=bb[:], func=AF.Exp, scale=-1.0), reads=[kbb, klf], writes=[klf])
            R.add('pool', I('tensor_tensor', out=kt[h][:], in0=fg[:], in1=lf[:], op=ALU.mult), reads=[kfg, klf], writes=[f'kt{h}'])
            b3 = bb[:].rearrange("p (c t) -> p c t", t=CH)
            R.add('dve', I('tensor_tensor', out=lf[:].rearrange("p (c t) -> p c t", t=CH),
                           in0=b3[:, :, CH - 1:CH].to_broadcast([128, 512 // CH, CH]), in1=b3, op=ALU.subtract),
                  reads=[kbb, klf, f'kt{h}'], writes=[klf])
            R.add('act', I('activation', out=lf[:], in_=lf[:], func=AF.Exp), reads=[klf], writes=[klf])
            R.add('pool', I('tensor_tensor', out=ks[h][:], in0=fg[:], in1=lf[:], op=ALU.mult), reads=[kfg, klf], writes=[f'ks{h}'])
            p, kp = proj('q')
            qs_, kqs = sc.next()
            R.add('act', I('activation', out=qs_[:], in_=p[:], func=AF.Silu), writes=[kp, kqs])
            R.add('dve', I('scalar_tensor_tensor', out=qt[h][:], in0=qs_[:], scalar=128.0 ** -0.5, in1=eb[h][:],
                           op0=ALU.mult, op1=ALU.mult), reads=[kqs, f'eb{h}'], writes=[f'qt{h}'])
            p, kp = proj('g')
            R.add('act', I('activation', out=gs[h][:], in_=p[:], func=AF.Silu), writes=[kp, f'gs{h}'])
        for tt in range(4):
            t0 = blk * 512 + tt * 128
            ls = slice(tt * 128, (tt + 1) * 128)
            gsl = slice(t0, t0 + 128)
            pv, kpv = pp.next()
            R.add('pe', [I('matmul', out=pv[:], lhsT=hT[:, kc, gsl], rhs=W['i'][:, kc, :], start=(kc == 0), stop=(kc == 7))
                         for kc in range(8)], reads=hk + ['w_i'], writes=[kpv])
            v_t, kv = vtm.next()
            R.add('act', I('activation', out=v_t[:], in_=pv[:], func=AF.Copy), writes=[kpv, kv])
            pa, kpa = pp.next()
            R.add('pe', [I('matmul', out=pa[:, h * 128:(h + 1) * 128], lhsT=kt[h][:, ls], rhs=qt[h][:, ls], start=True, stop=True)
                         for h in range(4)], reads=[f'kt{h}' for h in range(4)] + [f'qt{h}' for h in range(4)], writes=[kpa])
            a_t, ka = atm.next()
            R.add('dve', I('tensor_tensor', out=a_t[:], in0=pa[:], in1=amask[:], op=ALU.mult), reads=['amask'], writes=[kpa, ka])
            R.add('pe', [I('transpose', out=pk[:, h * 128:(h + 1) * 128], in_=ks[h][:, ls], identity=identb[:]) for h in range(4)],
                  reads=[f'ks{h}' for h in range(4)] + ['identb'], writes=['pk'])
            km, kkm = [], []
            for c in range(4):
                m_t, km_k = ksm.next()
                R.add('act', I('activation', out=m_t[:], in_=pk[:, 0:512], func=AF.Copy, scale=chm[:, c:c + 1]),
                      reads=['chm'], writes=['pk', km_k])
                km.append(m_t)
                kkm.append(km_k)
            for c in range(4):
                cl = slice(tt * 128 + c * CH, tt * 128 + (c + 1) * CH)
                for h in range(4):
                    hs = slice(h * 128, (h + 1) * 128)
                    oc = slice(h * 128 + c * CH, h * 128 + (c + 1) * CH)
                    R.add('pe', [I('matmul', out=po[:, oc], lhsT=Sb[h][:], rhs=qt[h][:, cl], start=True, stop=False),
                                 I('matmul', out=po[:, oc], lhsT=v_t[:, hs], rhs=a_t[:, oc], start=False, stop=True)],
                          reads=[f'Sb{h}', f'qt{h}', kv, ka], writes=['po'])
                    R.add('pe', I('matmul', out=psu[h][:, 0:128], lhsT=km[c][:, hs], rhs=v_t[:, hs], start=True, stop=True),
                          reads=[kkm[c], kv], writes=[f'psu{h}'])
                    dcol = eb[h][:, tt * 128 + (c + 1) * CH - 1:tt * 128 + (c + 1) * CH]
                    R.add('dve', I('scalar_tensor_tensor', out=S[h][:], in0=S[h][:], scalar=dcol, in1=psu[h][:, 0:128],
                                   op0=ALU.mult, op1=ALU.add), reads=[f'eb{h}'], writes=[f'S{h}', f'psu{h}'])
                    R.add('pool', I('tensor_copy', out=Sb[h][:], in_=S[h][:]), reads=[f'S{h}'], writes=[f'Sb{h}'])
            R.add('act', I('activation', out=oblk[:, :, ls], in_=po[:].rearrange("p (h t) -> p h t", h=4), func=AF.Copy),
                  writes=['po', 'oblk'])
        for h in range(4):
            sq, ksq = sc.next()
            sqb, ksqb = sqbr.next()
            R.add('act', I('activation', out=sqb[:], in_=oblk[:, h, :], func=AF.Square), reads=['oblk'], writes=[ksqb])
            pq, kpq = pp.next()
            R.add('pe', I('matmul', out=pq[:], lhsT=onesb[:], rhs=sqb[:], start=True, stop=True), reads=[ksqb, 'onesb'], writes=[kpq])
            R.add('act', I('activation', out=sq[:], in_=pq[:], func=AF.Ln, scale=1.0 / 128, bias=EPS), writes=[kpq, ksq])
            R.add('act', I('activation', out=sq[:], in_=sq[:], func=AF.Exp, scale=-0.5), writes=[ksq])
            R.add('dve', I('scalar_tensor_tensor', out=sq[:], in0=oblk[:, h, :], scalar=hng[:, 0:1], in1=sq[:],
                           op0=ALU.mult, op1=ALU.mult), reads=['oblk', 'hng', ksq], writes=[ksq])
            of, kof = ofin.next()
            R.add('dve', I('tensor_tensor', out=of[:], in0=sq[:], in1=gs[h][:], op=ALU.mult), reads=[ksq, f'gs{h}'], writes=[kof])
            fin.append(R.add('sp', I('dma_start', out=o_d[h * 128:(h + 1) * 128, bs], in_=of[:]), reads=[kof], dma='oout'))
    return C.finish()


def hgrn_consts():
    t = np.arange(512)
    cmask = np.broadcast_to((t % 32 != 0).astype(np.float32)[None, :], (128, 512)).copy()
    s = np.arange(128)
    amask = ((s[:, None] // 32 == s[None, :] // 32) & (s[:, None] <= s[None, :])).astype(np.float32)
    return cmask, np.ascontiguousarray(np.tile(amask, (1, 4)))


def hgrn_inputs(inp, hf, hT_full):
    j = 0
    w = inp['hg_w_in'][j]
    c0 = 512 * hf
    cmask, amask = hgrn_consts()
    lbr = inp['hg_lower_bounds'][:, c0:c0 + 512].reshape(4, 4, 128).transpose(2, 0, 1)
    return dict(hT=hT_full,
                w_q=np.ascontiguousarray(w[:, c0:c0 + 512]), w_f=np.ascontiguousarray(w[:, 1024 + c0:1024 + c0 + 512]),
                w_i=np.ascontiguousarray(w[:, 2048 + c0:2048 + c0 + 512]), w_g=np.ascontiguousarray(w[:, 3072 + c0:3072 + c0 + 512]),
                lbraw=np.ascontiguousarray(lbr.astype(np.float32)),
                hng=np.ascontiguousarray(inp['hg_norm_g'][j].reshape(128, 1).astype(np.float32)),
                cmask=cmask, amask=amask, identb=np.eye(128, dtype=np.float32).astype(ml_dtypes.bfloat16),
                chm=(np.arange(128)[:, None] // 32 == np.arange(4)[None, :]).astype(np.float32))


MOBA_NR = 4608
MOBA_WH = 4480
NEG = -1.0e30


def build_moba(P=None, prefix='', bind=None):
    C = Ctx(P, prefix, bind)
    nc, R = C.nc, C.rec
    NR, WH = MOBA_NR, MOBA_WH
    h_d = C.dram_in("hT", [2 * D, TOK], BF16)
    w_d = {n: C.dram_in("w_" + n, [D, 512]) for n in ('q', 'k', 'v')}
    tab_d = C.dram_in("tab", [33, 4])
    oh_d = C.dram_in("oh", [33, NR])
    selh_d = C.dram_in("selh", [4, 512])
    J_d = C.dram_in("Jb", [128, 128], BF16)
    E_d = C.dram_in("Eb", [17, 16 * 128], BF16)
    idb_d = C.dram_in("identb", [128, 128], BF16)
    pneg_d = C.dram_in("pneg", [128, 32, 16])
    p01_d = C.dram_in("p01", [128, 32, 16])
    o_d = C.dram_out("oT", [512, L], BF16)
    fs_h = nc.dram_tensor("fscr", [4, NR], F32)
    fs = fs_h.ap()

    hT = C.sb([128, 8, L], BF16, "hT")
    W = {n: C.sb([128, 8, 512], BF16, "w_" + n) for n in ('q', 'k', 'v')}
    tab = C.sb([33, 4], F32, "tab")
    selh = C.sb([4, 512], F32, "selh")
    Jb = C.sb([128, 128], BF16, "Jb")
    Eb = C.sb([17, 16 * 128], BF16, "Eb")
    identb = C.sb([128, 128], BF16, "identb")
    ones = C.sb([128, 128], F32, "ones")
    onesb = C.sb([128, 1], BF16, "onesb")
    fmx = C.sb([4, 16], F32, "fmx")
    bmax = C.sb([128, 4], F32, "bmax")
    vtm = C.sb([128, 32, 4, 129], BF16, "vtm")
    Hk = C.sb([128, WH], BF16, "Hk")
    qT = C.sb([128, L], BF16, "qT")
    kT = C.sb([128, L], BF16, "kT")
    q32 = C.sb([128, L], F32, "q32")
    rowsT = C.sb([17, L], BF16, "rowsT")
    kmean = C.sb([128, 16], F32, "kmean")
    kmx = C.sb([128, 8], F32, "kmx")
    kmax2 = C.sb([128, 1], F32, "kmax2")
    sc = Ring(C, 3, [128, 512], F32, "sc")
    ohr = Ring(C, 1, [33, 512], F32, "ohr")
    fr = Ring(C, 1, [4, 512], F32, "fr")
    gB = [C.sb([128, 16 * 17], F32, f"gB{i}") for i in range(2)]
    gBb = [C.sb([128, 16 * 17], BF16, f"gBb{i}") for i in range(2)]
    gW = Ring(C, 3, [128, 256], F32, "gW")
    gM = Ring(C, 2, [128, 16], F32, "gM")
    pneg = C.sb([128, 32, 16], F32, "pneg")
    p01 = C.sb([128, 32, 16], F32, "p01")
    ptile = Ring(C, 6, [128, 512], BF16, "ptile")
    rl = Ring(C, 2, [128, 1], F32, "rl")
    on = Ring(C, 2, [128, 128], BF16, "on")
    osb = Ring(C, 2, [128, 512], BF16, "osb")
    ps = Ring(C, 2, [128, 512], F32, "ps", psum=True)
    pp = Ring(C, 2, [128, 512], F32, "pp", psum=True)
    po = C.ps([128, 512], F32, "po")
    pl = C.ps([128, 512], F32, "pl")
    pg = C.ps([128, 512], F32, "pg")
    ptb = C.ps([128, 1024], BF16, "ptb")

    class _MixRing:
        def __init__(self, t, k):
            self.t, self.k, self.i = t, k, 0

        def next(self):
            j = self.i % len(self.t)
            self.i += 1
            return self.t[j], self.k[j]

    stR = _MixRing(ps.t + pp.t + [pg], ps.k + pp.k + ['pg'])

    for kc in range(8):
        for r in range(2):
            R.add('sp', I('dma_start', out=hT[:, kc, r * 2048:(r + 1) * 2048], in_=h_d[hrow(r, kc):hrow(r, kc) + 128, :]),
                  writes=[f'h{kc}'], dma='h')
    for n in ('q', 'k', 'v'):
        R.add('pool', I('dma_start', out=W[n][:], in_=w_d[n].rearrange("(kc p) c -> p kc c", p=128)), writes=['w_' + n], dma='w')
    R.add('sp', I('dma_start', out=tab[:], in_=tab_d), writes=['tab'], dma='c')
    R.add('sp', I('dma_start', out=selh[:], in_=selh_d), writes=['selh'], dma='c')
    R.add('sp', I('dma_start', out=Jb[:], in_=J_d), writes=['Jb'], dma='c')
    R.add('sp', I('dma_start', out=Eb[:], in_=E_d), writes=['Eb'], dma='c')
    R.add('sp', I('dma_start', out=identb[:], in_=idb_d), writes=['identb'], dma='c')
    R.add('sp', I('dma_start', out=pneg[:], in_=pneg_d), writes=['pneg'], dma='c')
    R.add('sp', I('dma_start', out=p01[:], in_=p01_d), writes=['p01'], dma='c')
    R.add('pool', I('memset', ap=ones[:], constant=1.0), writes=['ones'])
    R.add('pool', I('memset', ap=onesb[:], constant=1.0), writes=['onesb'])
    R.add('pool', I('memset', ap=vtm[:, :, :, 128:129], constant=1.0), writes=[f'v{t}' for t in range(32)])
    hk = [f'h{kc}' for kc in range(8)]

    def frow(ch):
        o_t, o_k = ohr.next()
        R.add('sp', I('dma_start', out=o_t[:], in_=oh_d[:, ch * 512:(ch + 1) * 512]), writes=[o_k], dma=o_k)
        p, kp = pp.next()
        R.add('pe', I('matmul', out=p[0:4, :], lhsT=tab[:], rhs=o_t[:], start=True, stop=True), reads=['tab', o_k], writes=[kp])
        f_t, f_k = fr.next()
        R.add('dve', I('tensor_copy', out=f_t[:], in_=p[0:4, :]), writes=[kp, f_k])
        R.add('dve', I('tensor_reduce', out=fmx[:, ch:ch + 1], in_=f_t[:], axis=AX.X, op=ALU.max), reads=[f_k], writes=['fmx'])
        R.add('sp', I('dma_start', out=fs[:, ch * 512:(ch + 1) * 512], in_=f_t[:]), reads=[f_k], writes=['fs'], dma='fs')

    def vproj(t):
        p, kp = pp.next()
        R.add('pe', [I('matmul', out=p[:], lhsT=hT[:, kc, t * 128:(t + 1) * 128], rhs=W['v'][:, kc, :], start=(kc == 0), stop=(kc == 7))
                     for kc in range(8)], reads=hk + ['w_v'], writes=[kp])
        R.add('act', I('activation', out=vtm[:, t, :, 0:128], in_=p[:].rearrange("p (h d) -> p h d", h=4), func=AF.Copy), writes=[kp, f'v{t}'])

    for t in range(32):
        vproj(t)
        if t % 3 == 2 and t // 3 < NR // 512:
            frow(t // 3)
    R.add('dve', I('tensor_reduce', out=fmx[:, 15:16], in_=fmx[:, 0:NR // 512], axis=AX.X, op=ALU.max), reads=['fmx'], writes=['fmx'])
    p, kp = pp.next()
    R.add('pe', [I('matmul', out=p[:, h:h + 1], lhsT=selh[:, h * 128:(h + 1) * 128], rhs=fmx[:, 15:16], start=True, stop=True)
                 for h in range(4)], reads=['selh', 'fmx'], writes=[kp])
    R.add('dve', I('tensor_copy', out=bmax[:], in_=p[:, 0:4]), writes=[kp, 'bmax'])

    vkeys = [f'v{t}' for t in range(32)]

    fin = []
    for h in range(4):
        hs = slice(h * 128, (h + 1) * 128)
        R.add('pool', I('dma_start', out=Hk[:], in_=bass.AP(fs_h, h * NR, [[1, 128], [1, WH]])),
              reads=['fs'], writes=['Hk'], dma='hk')
        for blk in range(8):
            bs = slice(blk * 512, (blk + 1) * 512)
            p, kp = pp.next()
            R.add('pe', [I('matmul', out=p[:], lhsT=W['q'][:, kc, hs], rhs=hT[:, kc, bs], start=(kc == 0), stop=(kc == 7))
                         for kc in range(8)], reads=hk + ['w_q'], writes=[kp])
            R.add('act', I('activation', out=q32[:, bs], in_=p[:], func=AF.Copy, scale=128.0 ** -0.5), writes=[kp, f'q32_{blk}'])
            R.add('pool', I('tensor_copy', out=qT[:, bs], in_=q32[:, bs]), reads=[f'q32_{blk}'], writes=[f'qT{blk}'])
        for blk in range(8):
            bs = slice(blk * 512, (blk + 1) * 512)
            p, kp = pp.next()
            R.add('pe', [I('matmul', out=p[:], lhsT=W['k'][:, kc, hs], rhs=hT[:, kc, bs], start=(kc == 0), stop=(kc == 7))
                         for kc in range(8)], reads=hk + ['w_k'], writes=[kp])
            k32, kk32 = sc.next()
            R.add('act', I('activation', out=k32[:], in_=p[:], func=AF.Copy), writes=[kp, kk32])
            R.add('pool', I('tensor_copy', out=kT[:, bs], in_=k32[:]), reads=[kk32], writes=[f'kT{blk}'])
            R.add('dve', I('tensor_reduce', out=kmean[:, 2 * blk:2 * blk + 2], in_=k32[:].rearrange("p (j t) -> p j t", t=256),
                           axis=AX.X, op=ALU.add), reads=[kk32], writes=['kmean'])
            R.add('pool', I('tensor_tensor', out=k32[:], in0=k32[:], in1=k32[:], op=ALU.mult), reads=[kk32, f'kT{blk}', 'kmean'], writes=[kk32])
            p2, kp2 = pp.next()
            R.add('pe', I('matmul', out=p2[:], lhsT=ones[:], rhs=k32[:], start=True, stop=True), reads=['ones', kk32], writes=[kp2])
            R.add('dve', I('tensor_reduce', out=kmx[:, blk:blk + 1], in_=p2[:], axis=AX.X, op=ALU.max), writes=[kp2, 'kmx'])
        R.add('dve', I('tensor_reduce', out=kmax2[:], in_=kmx[:], axis=AX.X, op=ALU.max), reads=['kmx'], writes=['kmax2'])
        R.add('dve', I('tensor_scalar', out=kmean[:], in0=kmean[:], scalar1=1.0 / 256, scalar2=None, op0=ALU.mult),
              reads=['kmean'], writes=['kmean'])
        for half in range(2):
            pgb, kpgb = (pg, 'pg') if half == 0 else pp.next()
            mm = []
            rk = set()
            for tl_ in range(16):
                t = half * 16 + tl_
                tsl = slice(t * 128, (t + 1) * 128)
                blk = t // 4
                q2, kq2 = sc.next() if tl_ % 4 == 0 else (q2, kq2)
                if tl_ % 4 == 0:
                    bsl = slice(blk * 512, (blk + 1) * 512)
                    R.add('pool', I('tensor_tensor', out=q2[:], in0=q32[:, bsl], in1=q32[:, bsl], op=ALU.mult),
                          reads=[f'q32_{blk}'], writes=[kq2])
                    R.add('pe', [I('matmul', out=pgb[:, (tl_ + u) * 17:(tl_ + u) * 17 + 16], lhsT=q32[:, (t + u) * 128:(t + u + 1) * 128],
                                   rhs=kmean[:], start=True, stop=True) for u in range(4)] +
                          [I('matmul', out=pgb[:, (tl_ + u) * 17 + 16:(tl_ + u) * 17 + 17], lhsT=q2[:, u * 128:(u + 1) * 128],
                             rhs=ones[:, 0:1], start=True, stop=True) for u in range(4)],
                          reads=[f'q32_{blk}', 'kmean', kq2, 'ones'], writes=[kpgb])
            G = gB[half]
            kG = f'gB{half}'
            G3 = G[:].rearrange("p (t j) -> p t j", j=17)
            R.add('dve', I('tensor_copy', out=G[:], in_=pgb[:, 0:272]), writes=[kpgb, kG])
            Gg = G3[:, :, 0:16]
            R.add('dve', I('tensor_tensor', out=Gg, in0=Gg, in1=pneg[:, half * 16:(half + 1) * 16, :], op=ALU.add), reads=['pneg'], writes=[kG])
            W1, kW1 = gW.next()
            W13 = W1[:].rearrange("p (t j) -> p t j", j=16)
            mx, kmx_ = gM.next()
            eq, keq = gW.next()
            eq3 = eq[:].rearrange("p (t j) -> p t j", j=16)
            cur = Gg
            kcur = kG
            for it in range(3):
                R.add('dve', I('tensor_reduce', out=mx[:], in_=cur, axis=AX.X, op=ALU.max), reads=[kcur], writes=[kmx_])
                if it == 2:
                    break
                R.add('dve', I('tensor_tensor', out=eq3, in0=cur, in1=mx[:].unsqueeze(2).to_broadcast([128, 16, 16]), op=ALU.is_ge),
                      reads=[kcur, kmx_], writes=[keq])
                R.add('dve', I('scalar_tensor_tensor', out=W13, in0=eq3, scalar=NEG, in1=cur, op0=ALU.mult, op1=ALU.add),
                      reads=[keq, kcur], writes=[kW1])
                cur = W13
                kcur = kW1
            R.add('dve', I('tensor_tensor', out=eq3, in0=Gg, in1=mx[:].unsqueeze(2).to_broadcast([128, 16, 16]), op=ALU.is_ge),
                  reads=[kG, kmx_], writes=[keq])
            R.add('dve', I('tensor_scalar', out=eq3, in0=eq3, scalar1=-NEG, scalar2=NEG, op0=ALU.mult, op1=ALU.add), writes=[keq])
            R.add('dve', I('tensor_tensor', out=Gg, in0=eq3, in1=p01[:, half * 16:(half + 1) * 16, :], op=ALU.mult),
                  reads=[keq, 'p01'], writes=[kG])
            Gb = G3[:, :, 16:17]
            R.add('dve', I('tensor_scalar', out=Gb, in0=Gb, scalar1=kmax2[:, 0:1], scalar2=None, op0=ALU.mult), reads=['kmax2'], writes=[kG])
            R.add('act', I('activation', out=Gb, in_=Gb, func=AF.Sqrt), writes=[kG])
            R.add('dve', I('tensor_scalar', out=Gb, in0=Gb, scalar1=bmax[:, h:h + 1], scalar2=-1.0, op0=ALU.add, op1=ALU.mult),
                  reads=['bmax'], writes=[kG])
            Gh = gBb[half]
            R.add('dve', I('tensor_copy', out=Gh[:], in_=G[:]), reads=[kG], writes=[kG + 'b'])
            for rnd in range(2):
                R.add('pe', [I('transpose', out=ptb[0:17, u * 128:(u + 1) * 128], in_=Gh[:, (rnd * 8 + u) * 17:(rnd * 8 + u + 1) * 17],
                               identity=identb[:]) for u in range(8)], reads=[kG + 'b', 'identb'], writes=['ptb'])
                t0_ = (half * 16 + rnd * 8) * 128
                R.add('act', I('activation', out=rowsT[:, t0_:t0_ + 1024], in_=ptb[0:17, 0:1024], func=AF.Copy),
                      writes=['ptb'] + [f'rows{(t0_ // 512) + u}' for u in range(2)])
        tiles = [(qb, kj) for qb in range(8) for kj in range(4 * qb + 4)]
        pend = {}

        def st_tile(idx):
            qb, kj = tiles[idx]
            qs = slice(qb * 512, (qb + 1) * 512)
            u0 = 512 * qb - 128 * kj + 384
            j = kj // 2
            s_p, s_k = stR.next()
            R.add('pe', [I('matmul', out=s_p[:], lhsT=kT[:, kj * 128:(kj + 1) * 128], rhs=qT[:, qs], start=True, stop=False),
                         I('matmul', out=s_p[:], lhsT=Jb[:], rhs=Hk[:, u0:u0 + 512], start=False, stop=False),
                         I('matmul', out=s_p[:], lhsT=Eb[:, j * 128:(j + 1) * 128], rhs=rowsT[:, qs], start=False, stop=True)],
                  reads=[f'kT{kj // 4}', f'qT{qb}', 'Jb', 'Hk', 'Eb', f'rows{qb}'], writes=[s_k])
            p_t, p_k = ptile.next()
            R.add('act', I('activation', out=p_t[:], in_=s_p[:], func=AF.Exp), writes=[s_k, p_k])
            pend[idx] = (p_t, p_k)

        def pv_tile(idx):
            qb, kj = tiles[idx]
            qs = slice(qb * 512, (qb + 1) * 512)
            p_t, p_k = pend.pop(idx)
            mms0, mms1 = [], []
            for qt in range(4):
                gq = 4 * qb + qt
                if kj > gq:
                    continue
                cs_ = slice(qt * 128, (qt + 1) * 128)
                if qt < 3:
                    mms0.append(I('matmul', out=po[:, qt * 129:(qt + 1) * 129], lhsT=p_t[:, cs_], rhs=vtm[:, kj, h, :], start=(kj == 0 and qt == 0),
                                  stop=(kj == gq), skip_group_check=True))
                else:
                    mms1.append(I('matmul', out=pl[:, 0:129], lhsT=p_t[:, cs_], rhs=vtm[:, kj, h, :], start=(kj == 0), stop=(kj == gq),
                                  skip_group_check=True))
            if mms0:
                R.add('pe', mms0, reads=[p_k, f'v{kj}'], writes=['po'])
            if mms1:
                R.add('pe', mms1, reads=[p_k, f'v{kj}'], writes=['pl'])
            if kj == 4 * qb + 3:
                o_s, o_k = osb.next()
                for qt in range(4):
                    cs_ = slice(qt * 128, (qt + 1) * 128)
                    r_t, r_k = rl.next()
                    bank, bkey, c0_ = (po, 'po', qt * 129) if qt < 3 else (pl, 'pl', 0)
                    R.add('dve', I('reciprocal', out=r_t[:], in_=bank[:, c0_ + 128:c0_ + 129]), writes=[bkey, r_k])
                    n_t, n_k = on.next()
                    R.add('dve', I('tensor_scalar', out=n_t[:], in0=bank[:, c0_:c0_ + 128], scalar1=r_t[:, 0:1], scalar2=None, op0=ALU.mult),
                          reads=[r_k], writes=[bkey, n_k])
                    R.add('pe', I('transpose', out=ptb[:, cs_], in_=n_t[:], identity=identb[:]), reads=[n_k, 'identb'], writes=['ptb'])
                R.add('act', I('activation', out=o_s[:], in_=ptb[:, 0:512], func=AF.Copy), writes=['ptb', o_k])
                fin.append(R.add('sp', I('dma_start', out=o_d[hs, qs], in_=o_s[:]), reads=[o_k], dma='oout'))

        LOOK = 3
        for idx in range(min(LOOK, len(tiles))):
            st_tile(idx)
        for idx in range(len(tiles)):
            if idx + LOOK < len(tiles):
                st_tile(idx + LOOK)
            pv_tile(idx)
    return C.finish()


def t5_bucket_np(rel):
    n = np.maximum(rel, 0)
    max_exact = 16
    nf = np.maximum(n, 1).astype(np.float32)
    large = max_exact + (np.log(nf / np.float32(max_exact)) / np.float32(math.log(2048 / max_exact)) * np.float32(32 - max_exact)).astype(np.int32)
    large = np.minimum(large, 31)
    return np.where(n < max_exact, n, large)


def moba_consts():
    NR = MOBA_NR
    w = np.arange(NR)
    dist = w - 511
    oh = np.zeros((33, NR), np.float32)
    bk = t5_bucket_np(dist)
    valid = dist >= 0
    oh[bk[valid], w[valid]] = 1.0
    oh[32, ~valid] = NEG
    selh = np.zeros((4, 512), np.float32)
    for h in range(4):
        selh[h, h * 128:(h + 1) * 128] = 1.0
    J = np.eye(128, dtype=np.float32)[::-1].copy().astype(ml_dtypes.bfloat16)
    E = np.zeros((17, 16, 128), np.float32)
    for j in range(16):
        E[j, j, :] = 1.0
        E[16, j, :] = 1.0
    return oh, selh, J, E.reshape(17, 16 * 128).astype(ml_dtypes.bfloat16)


def moba_inputs(inp, hf, hT_full):
    j = 0
    w = inp['moba_w_in'][j]
    c0 = 512 * hf
    oh, selh, J, E = moba_consts()
    tab = np.ones((33, 4), np.float32)
    tab[:32] = inp['rel_bias_table'][:, 4 * hf:4 * hf + 4]
    tl = np.arange(32)[:, None] // 2
    jj = np.arange(16)[None, :]
    p01 = np.broadcast_to((jj < tl).astype(np.float32)[None], (128, 32, 16)).copy()
    pneg = np.broadcast_to(np.where(jj >= tl, NEG, 0.0).astype(np.float32)[None], (128, 32, 16)).copy()
    return dict(hT=hT_full, w_q=np.ascontiguousarray(w[:, c0:c0 + 512]), w_k=np.ascontiguousarray(w[:, 1024 + c0:1024 + c0 + 512]),
                w_v=np.ascontiguousarray(w[:, 2048 + c0:2048 + c0 + 512]), tab=tab, oh=oh, selh=selh, Jb=J, Eb=E, pneg=pneg, p01=p01,
                identb=np.eye(128, dtype=np.float32).astype(ml_dtypes.bfloat16))


def build_gdn(P=None, prefix='', bind=None):
    C = Ctx(P, prefix, bind)
    nc, R = C.nc, C.rec
    h_d = C.dram_in("hT", [2 * D, TOK], BF16)
    wq_d = C.dram_in("w_q", [D, 512])
    wk_d = C.dram_in("w_k", [D, 512])
    wv_d = C.dram_in("w_v", [D, 1024])
    wz_d = C.dram_in("w_z", [D, 1024])
    wba_d = C.dram_in("w_ba", [D, 16])
    cw_d = C.dram_in("convw", [128, 16, 4])
    na_d = C.dram_in("alog", [128, 8])
    dtb_d = C.dram_in("dtb", [128, 8])
    ng_d = C.dram_in("gng", [128, 1])
    msk_d = C.dram_in("masks", [128, 6, 128])
    idb_d = C.dram_in("identb", [128, 128], BF16)
    chm_d = C.dram_in("chm2", [128, 2])
    o_d = C.dram_out("oT", [1024, L], BF16)

    W = {'q': C.sb([128, 8, 512], BF16, "w_q"), 'k': C.sb([128, 8, 512], BF16, "w_k"),
         'v': C.sb([128, 8, 1024], BF16, "w_v"), 'z': C.sb([128, 8, 1024], BF16, "w_z"),
         'ba': C.sb([128, 8, 16], BF16, "w_ba")}
    hTb = Ring(C, 2, [128, 8, 512], BF16, "hTb")
    cw = C.sb([128, 16, 4], F32, "cw")
    alog = C.sb([128, 8], F32, "alog")
    dtb = C.sb([128, 8], F32, "dtb")
    gng = C.sb([128, 1], F32, "gng")
    msk = C.sb([128, 6, 128], F32, "msk")
    identb = C.sb([128, 128], BF16, "identb")
    chm = C.sb([128, 2], F32, "chm")
    ones = C.sb([128, 128], F32, "ones")
    halo = C.sb([128, 16, 4], F32, "halo")
    xs = Ring(C, 4, [128, 516], F32, "xs")
    cv = Ring(C, 4, [128, 512], F32, "cv")
    sc = Ring(C, 2, [128, 512], F32, "sc")
    qT = C.sb([128, 4, 512], BF16, "qT")
    kT = C.sb([128, 4, 512], BF16, "kT")
    vT = C.sb([128, 8, 512], BF16, "vT")
    zs = C.sb([128, 8, 512], BF16, "zs")
    oblk = C.sb([128, 8, 512], F32, "oblk")
    S4 = [C.sb([128, 4, 128], F32, f"S4_{g}") for g in range(2)]
    Sb4 = [C.sb([128, 4, 128], BF16, f"Sb4_{g}") for g in range(2)]
    vnew = [C.sb([128, 4, 128], BF16, f"vnew{g}") for g in range(2)]
    cols = {n: Ring(C, 2, [128, 8], F32, "c_" + n) for n in
            ('ba', 'beta', 'nbeta', 'g', 'ax', 'b', 'bl', 'eb', 'bexp', 'eblb', 'e0', 'e1')}
    ba16 = Ring(C, 2, [128, 16], F32, "ba16")
    brep = Ring(C, 1, [128, 8, 128], F32, "brep")
    ebrep = Ring(C, 2, [128, 8, 128], F32, "ebrep")
    bns = Ring(C, 1, [128, 8, 128], F32, "bns")
    dg = Ring(C, 2, [128, 512], F32, "dg")
    f4 = Ring(C, 5, [128, 4, 128], F32, "f4")
    b4 = Ring(C, 16, [128, 4, 128], BF16, "b4")
    ded = {(n, g): (C.sb([128, 4, 128], BF16, f"d_{n}{g}"), f"d_{n}{g}") for n in ("QK", "vb", "kbe", "kst0", "kst1", "qd", "WT") for g in range(2)}
    usb = Ring(C, 2, [128, 4, 128], F32, "usb")
    ofin = Ring(C, 2, [128, 512], BF16, "ofin")
    sqbr = Ring(C, 2, [128, 512], BF16, "sqbr")
    onesb = C.sb([128, 128], BF16, "onesb")
    pp = Ring(C, 2, [128, 512], F32, "pp", psum=True)
    pC = C.ps([128, 512], F32, "pC")
    pD = C.ps([128, 1024], BF16, "pD")
    pE = Ring(C, 2, [128, 512], F32, "pE", psum=True)
    pG = C.ps([128, 512], F32, "pG")
    pH = C.ps([128, 512], F32, "pH")

    class _MixRing:
        def __init__(self, t, k):
            self.t, self.k, self.i = t, k, 0

        def next(self):
            j = self.i % len(self.t)
            self.i += 1
            return self.t[j], self.k[j]

    bigR = _MixRing(pE.t + [pG, pH, pC] + pp.t, pE.k + ['pG', 'pH', 'pC'] + pp.k)

    for n, d_, in (('q', wq_d), ('k', wk_d), ('v', wv_d), ('z', wz_d), ('ba', wba_d)):
        R.add('pool', I('dma_start', out=W[n][:], in_=d_.rearrange("(kc p) c -> p kc c", p=128)), writes=['w_' + n], dma='w')
    for t_, d_, k_ in ((cw, cw_d, 'cw'), (alog, na_d, 'alog'), (dtb, dtb_d, 'dtb'), (gng, ng_d, 'gng'), (msk, msk_d, 'msk'),
                       (identb, idb_d, 'identb'), (chm, chm_d, 'chm')):
        R.add('sp', I('dma_start', out=t_[:], in_=d_), writes=[k_], dma='c')
    R.add('pool', I('memset', ap=ones[:], constant=1.0), writes=['ones'])
    R.add('pool', I('memset', ap=onesb[:], constant=1.0), writes=['onesb'])
    R.add('pool', I('memset', ap=halo[:], constant=0.0), writes=['halo'])
    for g in range(2):
        R.add('pool', I('memset', ap=S4[g][:], constant=0.0), writes=[f'S4_{g}'])
        R.add('pool', I('memset', ap=Sb4[g][:], constant=0.0), writes=[f'Sb4_{g}'])
        R.add('pool', I('memset', ap=vnew[g][:], constant=0.0), writes=[f'vnew{g}'])
    R.add('act', I('activation', out=alog[:], in_=alog[:], func=AF.Exp), reads=['alog'], writes=['alog'])
    R.add('dve', I('tensor_scalar', out=alog[:], in0=alog[:], scalar1=-1.0, scalar2=None, op0=ALU.mult), reads=['alog'], writes=['alog'])
    M_NEGT, M_POS, M_SNEG, M_ID, M_TRI, M_CH = range(6)

    def bc4(ap2d):
        return ap2d.unsqueeze(1).to_broadcast([128, 4, 128])

    def colbc(col4):
        return col4.unsqueeze(2).to_broadcast([128, 4, 128])

    fin = []
    eng_rr = [0]

    def evac(out, in_, pkey, okey, extra_reads=()):
        e = 'act'
        eng_rr[0] += 1
        if e == 'act':
            R.add('act', I('activation', out=out, in_=in_, func=AF.Copy), reads=list(extra_reads), writes=[pkey, okey])
        else:
            R.add('dve', I('tensor_copy', out=out, in_=in_), reads=list(extra_reads), writes=[pkey, okey])

    for sb_ in range(L // 512):
        bs = slice(sb_ * 512, (sb_ + 1) * 512)
        hT, hkey = hTb.next()
        for kc in range(8):
            r_ = sb_ // 4
            R.add('sp', I('dma_start', out=hT[:, kc, :], in_=h_d[hrow(r_, kc):hrow(r_, kc) + 128, (sb_ % 4) * 512:(sb_ % 4 + 1) * 512]),
                  writes=[hkey], dma='h' + hkey)

        jobs = ([('q', h, h) for h in range(4)] + [('k', h, 4 + h) for h in range(4)] + [('v', h, 8 + h) for h in range(8)]
                + [('z', h, None) for h in range(8)])
        stg = {}

        def stage_a(n):
            wname, h, ci = jobs[n]
            p, kp = pp.next()
            R.add('pe', [I('matmul', out=p[:], lhsT=W[wname][:, kc, h * 128:(h + 1) * 128], rhs=hT[:, kc, :], start=(kc == 0), stop=(kc == 7))
                         for kc in range(8)], reads=[hkey, 'w_' + wname], writes=[kp])
            if wname == 'z':
                R.add('act', I('activation', out=zs[:, h, :], in_=p[:], func=AF.Silu), writes=[kp, 'zs'])
                return
            x, kx = xs.next()
            R.add('pool', I('tensor_copy', out=x[:, 0:3], in_=halo[:, ci, 0:3]), reads=['halo'], writes=[kx])
            R.add('act', I('activation', out=x[:, 3:515], in_=p[:], func=AF.Copy), writes=[kp, kx])
            R.add('pool', I('tensor_copy', out=halo[:, ci, 0:3], in_=x[:, 512:515]), reads=[kx], writes=['halo'])
            stg[n] = (x, kx)

        def stage_b(n):
            wname, h, ci = jobs[n]
            if wname == 'z':
                return
            x, kx = stg[n]
            y, ky = cv.next()
            R.add('dve', I('tensor_scalar', out=y[:], in0=x[:, 3:515], scalar1=cw[:, ci, 3:4], scalar2=None, op0=ALU.mult),
                  reads=[kx, 'cw'], writes=[ky])
            for j in (2, 1, 0):
                R.add('dve', I('scalar_tensor_tensor', out=y[:], in0=x[:, j:j + 512], scalar=cw[:, ci, j:j + 1], in1=y[:],
                               op0=ALU.mult, op1=ALU.add), reads=[kx, 'cw'], writes=[ky])
            stg[n] = (y, ky)

        def stage_c(n):
            wname, h, ci = jobs[n]
            if wname == 'z':
                return
            y, ky = stg.pop(n)
            if wname == 'v':
                R.add('act', I('activation', out=vT[:, h, :], in_=y[:], func=AF.Silu), reads=[ky], writes=['vT'])
                return
            R.add('act', I('activation', out=y[:], in_=y[:], func=AF.Silu), writes=[ky])
            sq, ksq = sc.next()
            sqb, ksqb = sqbr.next()
            R.add('act', I('activation', out=sqb[:], in_=y[:], func=AF.Square), reads=[ky], writes=[ksqb])
            p, kp = pp.next()
            R.add('pe', I('matmul', out=p[:], lhsT=onesb[:], rhs=sqb[:], start=True, stop=True), reads=['onesb', ksqb], writes=[kp])
            R.add('act', I('activation', out=sq[:], in_=p[:], func=AF.Ln, bias=1e-6), writes=[kp, ksq])
            R.add('act', I('activation', out=sq[:], in_=sq[:], func=AF.Exp, scale=-0.5), writes=[ksq])
            dst, dkey, scale = (qT[:, h, :], 'qT', 128.0 ** -0.5) if wname == 'q' else (kT[:, h, :], 'kT', 1.0)
            R.add('dve', I('scalar_tensor_tensor', out=dst, in0=y[:], scalar=scale, in1=sq[:], op0=ALU.mult, op1=ALU.mult),
                  reads=[ky, ksq], writes=[dkey])

        nj = len(jobs)
        for n in range(nj + 2):
            if n < nj:
                stage_a(n)
            if 0 <= n - 1 < nj:
                stage_b(n - 1)
            if 0 <= n - 2 < nj:
                stage_c(n - 2)

        def prep(tt):
            ls = slice(tt * 128, (tt + 1) * 128)
            p, kp = pp.next()
            R.add('pe', [I('matmul', out=p[:, 0:16], lhsT=hT[:, kc, ls], rhs=W['ba'][:, kc, :], start=(kc == 0), stop=(kc == 7))
                         for kc in range(8)], reads=[hkey, 'w_ba'], writes=[kp])
            ba, kba = ba16.next()
            R.add('dve', I('tensor_copy', out=ba[:], in_=p[:, 0:16]), writes=[kp, kba])
            cc = {n: cols[n].next() for n in cols}
            beta, kbeta = cc['beta']
            R.add('act', I('activation', out=beta[:], in_=ba[:, 0:8], func=AF.Exp, scale=-1.0), reads=[kba], writes=[kbeta])
            R.add('dve', I('tensor_scalar', out=beta[:], in0=beta[:], scalar1=1.0, scalar2=None, op0=ALU.add), writes=[kbeta])
            R.add('dve', I('reciprocal', out=beta[:], in_=beta[:]), writes=[kbeta])
            nbeta, knbeta = cc['nbeta']
            R.add('dve', I('tensor_scalar', out=nbeta[:], in0=beta[:], scalar1=-1.0, scalar2=None, op0=ALU.mult), reads=[kbeta], writes=[knbeta])
            ax, kax = cc['ax']
            gg, kgg = cc['g']
            R.add('dve', I('tensor_tensor', out=ax[:], in0=ba[:, 8:16], in1=dtb[:], op=ALU.add), reads=[kba, 'dtb'], writes=[kax])
            R.add('act', I('activation', out=gg[:], in_=ax[:], func=AF.Abs), reads=[kax], writes=[kgg])
            R.add('act', I('activation', out=gg[:], in_=gg[:], func=AF.Exp, scale=-1.0), writes=[kgg])
            R.add('act', I('activation', out=gg[:], in_=gg[:], func=AF.Ln, bias=1.0), writes=[kgg])
            R.add('dve', I('scalar_tensor_tensor', out=gg[:], in0=ax[:], scalar=0.0, in1=gg[:], op0=ALU.max, op1=ALU.add),
                  reads=[kax], writes=[kgg])
            R.add('dve', I('tensor_tensor', out=gg[:], in0=gg[:], in1=alog[:], op=ALU.mult), reads=['alog'], writes=[kgg])
            p, kp = pp.next()
            R.add('pe', [I('matmul', out=p[:, 0:8], lhsT=msk[:, M_TRI, :], rhs=gg[:], start=True, stop=True),
                         I('matmul', out=p[:, 8:16], lhsT=msk[:, M_CH, :], rhs=gg[:], start=True, stop=True)],
                  reads=['msk', kgg], writes=[kp])
            bcol, kb = cc['b']
            blc, kbl = cc['bl']
            R.add('dve', I('tensor_copy', out=bcol[:], in_=p[:, 0:8]), writes=[kp, kb])
            R.add('dve', I('tensor_tensor', out=blc[:], in0=p[:, 8:16], in1=bcol[:], op=ALU.subtract), reads=[kb], writes=[kp, kbl])
            ebc, keb = cc['eb']
            R.add('act', I('activation', out=ebc[:], in_=bcol[:], func=AF.Exp), reads=[kb], writes=[keb])
            bexp, kbexp = cc['bexp']
            R.add('dve', I('tensor_tensor', out=bexp[:], in0=ebc[:], in1=beta[:], op=ALU.mult), reads=[keb, kbeta], writes=[kbexp])
            eblb, keblb = cc['eblb']
            R.add('act', I('activation', out=eblb[:], in_=blc[:], func=AF.Exp), reads=[kbl], writes=[keblb])
            e01 = []
            for c in range(2):
                e_, ke_ = cc[f'e{c}']
                R.add('dve', I('tensor_scalar', out=e_[:], in0=eblb[:], scalar1=chm[:, c:c + 1], scalar2=None, op0=ALU.mult),
                      reads=[keblb, 'chm'], writes=[ke_])
                e01.append((e_, ke_))
            br, kbr = brep.next()
            bn, kbn = bns.next()
            for (src, ksrc, dst, kdst) in ((bcol, kb, br, kbr), (beta, kbeta, bn, kbn)):
                for g in range(2):
                    d_, kd_ = dg.next()
                    R.add('dve', I('tensor_tensor', out=d_[:].rearrange("p (h t) -> p h t", h=4), in0=bc4(msk[:, M_ID, :]),
                                   in1=colbc(src[:, g * 4:(g + 1) * 4]), op=ALU.mult), reads=['msk', ksrc], writes=[kd_])
                    p, kp = pp.next()
                    R.add('pe', I('matmul', out=p[:], lhsT=ones[:], rhs=d_[:], start=True, stop=True), reads=['ones', kd_], writes=[kp])
                    evac(dst[:, g * 4:(g + 1) * 4, :], p[:].rearrange("p (h t) -> p h t", h=4), kp, kdst)
            er, ker = ebrep.next()
            R.add('act', I('activation', out=er[:], in_=br[:], func=AF.Exp), reads=[kbr], writes=[ker])
            for g in range(2):
                R.add('pool', I('tensor_tensor', out=bn[:, g * 4:(g + 1) * 4, :], in0=bn[:, g * 4:(g + 1) * 4, :],
                                in1=bc4(msk[:, M_SNEG, :]), op=ALU.mult), reads=['msk'], writes=[kbn])
            return dict(ls=ls, bcol=bcol, kb=kb, beta=beta, kbeta=kbeta, nbeta=nbeta, knbeta=knbeta, bexp=bexp, kbexp=kbexp,
                        e01=e01, br=br, kbr=kbr, bn=bn, kbn=kbn, er=er, ker=ker)

        for tt in range(4):
            cur_ = prep(tt)
            ls, bcol, kb, beta, kbeta, nbeta, knbeta = (cur_[k_] for k_ in ('ls', 'bcol', 'kb', 'beta', 'kbeta', 'nbeta', 'knbeta'))
            bexp, kbexp, e01, br, kbr, bn, kbn, er, ker = (cur_[k_] for k_ in ('bexp', 'kbexp', 'e01', 'br', 'kbr', 'bn', 'kbn', 'er', 'ker'))
            grp = []
            for g in range(2):
                hv4 = slice(g * 4, (g + 1) * 4)
                mm = []
                for kk_ in range(2):
                    kh = 2 * g + kk_
                    mm.append(I('matmul', out=pC[:, kk_ * 128:(kk_ + 1) * 128], lhsT=kT[:, kh, ls], rhs=kT[:, kh, ls], start=True, stop=True))
                    mm.append(I('matmul', out=pC[:, 256 + kk_ * 128:256 + (kk_ + 1) * 128], lhsT=kT[:, kh, ls], rhs=qT[:, kh, ls],
                                start=True, stop=True))
                R.add('pe', mm, reads=['kT', 'qT'], writes=['pC'])
                kk_bc = pC[:, 0:256].rearrange("p (k t) -> p k t", k=2).unsqueeze(2).to_broadcast([128, 2, 2, 128])
                qk_bc = pC[:, 256:512].rearrange("p (k t) -> p k t", k=2).unsqueeze(2).to_broadcast([128, 2, 2, 128])

                def v4(t):
                    return t.rearrange("p (k j) t -> p k j t", k=2)

                Dm, kDm = f4.next()
                R.add('dve', I('tensor_tensor', out=Dm[:], in0=br[:, hv4, :], in1=colbc(bcol[:, hv4]), op=ALU.subtract),
                      reads=[kbr, kb], writes=[kDm])
                E1, kE1 = f4.next()
                R.add('dve', I('tensor_tensor', out=E1[:], in0=Dm[:], in1=bc4(msk[:, M_NEGT, :]), op=ALU.add), reads=[kDm, 'msk'], writes=[kE1])
                R.add('act', I('activation', out=E1[:], in_=E1[:], func=AF.Exp), writes=[kE1])
                E2, kE2 = f4.next()
                R.add('pool', I('tensor_tensor', out=E2[:], in0=Dm[:], in1=bc4(msk[:, M_POS, :]), op=ALU.add), reads=[kDm, 'msk'], writes=[kE2])
                R.add('act', I('activation', out=E2[:], in_=E2[:], func=AF.Exp, scale=-1.0), writes=[kE2])
                QK, kQK = ded[('QK', g)]
                R.add('dve', I('tensor_tensor', out=v4(QK[:]), in0=qk_bc, in1=v4(E1[:]), op=ALU.mult), reads=[kE1], writes=['pC', kQK])
                tmp, ktmp = f4.next()
                R.add('dve', I('tensor_tensor', out=v4(tmp[:]), in0=kk_bc, in1=v4(E1[:]), op=ALU.mult), reads=[kE1], writes=['pC', ktmp])
                NT, kNT = b4.next()
                R.add('pool', I('tensor_tensor', out=NT[:], in0=tmp[:], in1=bn[:, hv4, :], op=ALU.mult), reads=[ktmp, kbn], writes=[kNT])
                AT, kAT = b4.next()
                R.add('pool', I('tensor_tensor', out=AT[:], in0=NT[:], in1=bc4(msk[:, M_ID, :]), op=ALU.add), reads=[kNT, 'msk'], writes=[kAT])
                t2, kt2 = f4.next()
                R.add('dve', I('tensor_tensor', out=v4(t2[:]), in0=kk_bc, in1=v4(colbc(nbeta[:, hv4])), op=ALU.mult),
                      reads=[knbeta], writes=['pC', kt2])
                N_, kN = b4.next()
                R.add('pool', I('tensor_tensor', out=N_[:], in0=t2[:], in1=E2[:], op=ALU.mult), reads=[kt2, kE2], writes=[kN])
                mm = [I('transpose', out=pD[:, kk_ * 128:(kk_ + 1) * 128], in_=kT[:, 2 * g + kk_, ls], identity=identb[:]) for kk_ in range(2)]
                mm += [I('transpose', out=pD[:, 256 + j * 128:256 + (j + 1) * 128], in_=vT[:, g * 4 + j, ls], identity=identb[:]) for j in range(4)]
                R.add('pe', mm, reads=['kT', 'vT', 'identb'], writes=['pD'])
                ktm_bc = pD[:, 0:256].rearrange("p (k t) -> p k t", k=2).unsqueeze(2).to_broadcast([128, 2, 2, 128])
                vb, kvb = ded[('vb', g)]
                R.add('dve', I('tensor_tensor', out=vb[:], in0=pD[:, 256:768].rearrange("p (h t) -> p h t", h=4),
                               in1=colbc(beta[:, hv4]), op=ALU.mult), reads=[kbeta], writes=['pD', kvb])
                kbe, kkbe = ded[('kbe', g)]
                R.add('dve', I('tensor_tensor', out=v4(kbe[:]), in0=ktm_bc, in1=v4(colbc(bexp[:, hv4])), op=ALU.mult),
                      reads=[kbexp], writes=['pD', kkbe])
                kst = []
                for c in range(2):
                    k_, kk2 = ded[(f'kst{c}', g)]
                    R.add('dve', I('tensor_tensor', out=v4(k_[:]), in0=ktm_bc, in1=v4(colbc(e01[c][0][:, hv4])), op=ALU.mult),
                          reads=[e01[c][1]], writes=['pD', kk2])
                    kst.append((k_, kk2))
                qd, kqd = ded[('qd', g)]
                R.add('pool', I('tensor_tensor', out=v4(qd[:]), in0=qT[:, 2 * g:2 * g + 2, ls].unsqueeze(2).to_broadcast([128, 2, 2, 128]),
                                in1=v4(er[:, hv4, :]), op=ALU.mult), reads=['qT', ker], writes=[kqd])
                grp.append(dict(N=(N_, kN), NT=(NT, kNT), AT=(AT, kAT), QK=(QK, kQK), vb=(vb, kvb), kbe=(kbe, kkbe), kst=kst, qd=(qd, kqd)))

            def r4(t):
                return t[:].rearrange("p (h t) -> p h t", h=4)

            for lev in range(5):
                st_ = []
                for g in range(2):
                    G_ = grp[g]
                    (N_, kN), (NT, kNT) = G_['N'], G_['NT']
                    p1, kp1 = bigR.next()
                    R.add('pe', [I('matmul', out=p1[:, j * 128:(j + 1) * 128], lhsT=NT[:, j, :], rhs=N_[:, j, :], start=True, stop=True)
                                 for j in range(4)], reads=[kN, kNT], writes=[kp1])
                    p2 = kp2 = None
                    if lev < 4:
                        p2, kp2 = bigR.next()
                        R.add('pe', [I('matmul', out=p2[:, j * 128:(j + 1) * 128], lhsT=N_[:, j, :], rhs=NT[:, j, :], start=True, stop=True)
                                     for j in range(4)], reads=[kN, kNT], writes=[kp2])
                    st_.append((p1, kp1, p2, kp2))
                n2s = []
                for g in range(2):
                    p1, kp1, p2, kp2 = st_[g]
                    N2, kN2 = b4.next()
                    R.add('act', I('activation', out=N2[:], in_=r4(p1), func=AF.Copy), writes=[kp1, kN2])
                    n2s.append((N2, kN2))
                for g in range(2):
                    p1, kp1, p2, kp2 = st_[g]
                    if p2 is not None:
                        NT2, kNT2 = b4.next()
                        R.add('act', I('activation', out=NT2[:], in_=r4(p2), func=AF.Copy), writes=[kp2, kNT2])
                        grp[g]['NT'] = (NT2, kNT2)
                pas = []
                for g in range(2):
                    (AT, kAT) = grp[g]['AT']
                    N2, kN2 = n2s[g]
                    p3, kp3 = bigR.next()
                    R.add('pe', [I('matmul', out=p3[:, j * 128:(j + 1) * 128], lhsT=N2[:, j, :], rhs=AT[:, j, :], start=True, stop=True)
                                 for j in range(4)], reads=[kN2, kAT], writes=[kp3])
                    pas.append((p3, kp3))
                for g in range(2):
                    (AT, kAT) = grp[g]['AT']
                    p3, kp3 = pas[g]
                    AT2, kAT2 = b4.next()
                    R.add('dve', I('tensor_tensor', out=AT2[:], in0=r4(p3), in1=AT[:], op=ALU.add), reads=[kAT], writes=[kp3, kAT2])
                    grp[g]['N'] = n2s[g]
                    grp[g]['AT'] = (AT2, kAT2)

            pu_ = []
            for g in range(2):
                (AT, kAT), (vb, kvb) = grp[g]['AT'], grp[g]['vb']
                p, kp = bigR.next()
                R.add('pe', [I('matmul', out=p[:, j * 128:(j + 1) * 128], lhsT=AT[:, j, :], rhs=vb[:, j, :], start=True, stop=True)
                             for j in range(4)], reads=[kAT, kvb], writes=[kp])
                pu_.append((p, kp))
            pw_ = []
            for g in range(2):
                (AT, kAT), (kbe, kkbe) = grp[g]['AT'], grp[g]['kbe']
                p, kp = bigR.next()
                R.add('pe', [I('matmul', out=p[:, j * 128:(j + 1) * 128], lhsT=kbe[:, j, :], rhs=AT[:, j, :], start=True, stop=True)
                             for j in range(4)], reads=[kAT, kkbe], writes=[kp])
                pw_.append((p, kp))
            for g in range(2):
                U, kU = usb.next()
                R.add('act', I('activation', out=U[:], in_=r4(pu_[g][0]), func=AF.Copy), writes=[pu_[g][1], kU])
                grp[g]['U'] = (U, kU)
            for g in range(2):
                WT, kWT = ded[('WT', g)]
                R.add('act', I('activation', out=WT[:], in_=r4(pw_[g][0]), func=AF.Copy), writes=[pw_[g][1], kWT])
                grp[g]['WT'] = (WT, kWT)

            for c in range(2):
                rows = slice(c * 64, (c + 1) * 64)
                ccols = slice(c * 64, (c + 1) * 64)
                ws = []
                for g in range(2):
                    (WT, kWT) = grp[g]['WT']
                    p, kp = bigR.next()
                    R.add('pe', [I('matmul', out=p[:, j * 128:(j + 1) * 128], lhsT=WT[:, j, :], rhs=Sb4[g][:, j, :], start=True, stop=True)
                                 for j in range(4)], reads=[kWT, f'Sb4_{g}'], writes=[kp])
                    ws.append((p, kp))
                for g in range(2):
                    (U, kU) = grp[g]['U']
                    p, kp = ws[g]
                    R.add('dve', I('tensor_tensor', out=vnew[g][rows, :, :], in0=U[rows, :, :],
                                   in1=p[rows, :].rearrange("p (h t) -> p h t", h=4), op=ALU.subtract),
                          reads=[kU], writes=[kp, f'vnew{g}'])
                po_, ps_ = [], []
                for g in range(2):
                    (QK, kQK), (qd, kqd) = grp[g]['QK'], grp[g]['qd']
                    (ks_, kks) = grp[g]['kst'][c]
                    p, kp = bigR.next()
                    mm = []
                    for j in range(4):
                        oc = slice(j * 64, (j + 1) * 64)
                        mm.append(I('matmul', out=p[:, oc], lhsT=Sb4[g][:, j, :], rhs=qd[:, j, ccols], start=True, stop=False))
                        mm.append(I('matmul', out=p[:, oc], lhsT=vnew[g][:, j, :], rhs=QK[:, j, ccols], start=False, stop=True))
                    R.add('pe', mm, reads=[f'Sb4_{g}', kqd, f'vnew{g}', kQK], writes=[kp])
                    po_.append((p, kp))
                    p, kp = bigR.next()
                    R.add('pe', [I('matmul', out=p[:, j * 128:(j + 1) * 128], lhsT=ks_[:, j, :], rhs=vnew[g][:, j, :], start=True, stop=True)
                                 for j in range(4)], reads=[kks, f'vnew{g}'], writes=[kp])
                    ps_.append((p, kp))
                for g in range(2):
                    hv4 = slice(g * 4, (g + 1) * 4)
                    dl = er[:, hv4, c * 64 + 63:c * 64 + 64].to_broadcast([128, 4, 128])
                    R.add('dve', I('tensor_tensor', out=S4[g][:], in0=S4[g][:], in1=dl, op=ALU.mult), reads=[ker], writes=[f'S4_{g}'])
                    R.add('dve', I('tensor_tensor', out=S4[g][:], in0=S4[g][:], in1=r4(ps_[g][0]), op=ALU.add),
                          writes=[ps_[g][1], f'S4_{g}'])
                    R.add('act', I('activation', out=Sb4[g][:], in_=S4[g][:], func=AF.Copy), reads=[f'S4_{g}'], writes=[f'Sb4_{g}'])
                for g in range(2):
                    hv4 = slice(g * 4, (g + 1) * 4)
                    R.add('act', I('activation', out=oblk[:, hv4, tt * 128 + c * 64:tt * 128 + (c + 1) * 64],
                                   in_=po_[g][0][:, 0:256].rearrange("p (h t) -> p h t", h=4), func=AF.Copy), writes=[po_[g][1], 'oblk'])
        for h in range(8):
            sq, ksq = sc.next()
            sqb, ksqb = sqbr.next()
            R.add('act', I('activation', out=sqb[:], in_=oblk[:, h, :], func=AF.Square), reads=['oblk'], writes=[ksqb])
            p, kp = pp.next()
            R.add('pe', I('matmul', out=p[:], lhsT=onesb[:], rhs=sqb[:], start=True, stop=True), reads=[ksqb, 'onesb'], writes=[kp])
            R.add('act', I('activation', out=sq[:], in_=p[:], func=AF.Ln, scale=1.0 / 128, bias=EPS), writes=[kp, ksq])
            R.add('act', I('activation', out=sq[:], in_=sq[:], func=AF.Exp, scale=-0.5), writes=[ksq])
            R.add('dve', I('scalar_tensor_tensor', out=sq[:], in0=oblk[:, h, :], scalar=gng[:, 0:1], in1=sq[:],
                           op0=ALU.mult, op1=ALU.mult), reads=['oblk', 'gng'], writes=[ksq])
            of, kof = ofin.next()
            R.add('dve', I('tensor_tensor', out=of[:], in0=sq[:], in1=zs[:, h, :], op=ALU.mult), reads=[ksq, 'zs'], writes=[kof])
            fin.append(R.add('sp', I('dma_start', out=o_d[h * 128:(h + 1) * 128, bs], in_=of[:]), reads=[kof], dma='oout'))
    return C.finish()


def gdn_inputs(inp, hf, hT_full):
    j = 0
    w = inp['gdn_w_in'][j]
    kq0 = 512 * hf
    v0 = 1024 * hf
    cwf = inp['gdn_conv_w'][j]
    chans = np.concatenate([np.arange(kq0, kq0 + 512), 1024 + np.arange(kq0, kq0 + 512), 2048 + np.arange(v0, v0 + 1024)])
    cw = cwf[:, chans].reshape(4, 16, 128).transpose(2, 1, 0)
    s = np.arange(128)
    same = (s[:, None] // 64 == s[None, :] // 64)
    le = s[:, None] <= s[None, :]
    lt = s[:, None] < s[None, :]
    masks = np.zeros((128, 6, 128), np.float32)
    masks[:, 0, :] = np.where(same & le, 0.0, NEG)
    masks[:, 1, :] = np.where(same & (s[None, :] < s[:, None]), 0.0, -NEG)
    masks[:, 2, :] = np.where(same & lt, -1.0, 0.0)
    masks[:, 3, :] = np.eye(128)
    masks[:, 4, :] = (same & le).astype(np.float32)
    masks[:, 5, :] = same.astype(np.float32)
    return dict(hT=hT_full, w_q=np.ascontiguousarray(w[:, kq0:kq0 + 512]), w_k=np.ascontiguousarray(w[:, 1024 + kq0:1024 + kq0 + 512]),
                w_v=np.ascontiguousarray(w[:, 2048 + v0:2048 + v0 + 1024]), w_z=np.ascontiguousarray(w[:, 4096 + v0:4096 + v0 + 1024]),
                w_ba=np.ascontiguousarray(np.concatenate([w[:, 6144 + 8 * hf:6144 + 8 * hf + 8], w[:, 6160 + 8 * hf:6160 + 8 * hf + 8]], axis=1)),
                convw=np.ascontiguousarray(cw.astype(np.float32)),
                alog=np.ascontiguousarray(np.broadcast_to(inp['gdn_a_log'][j][8 * hf:8 * hf + 8][None, :], (128, 8)).astype(np.float32)),
                dtb=np.ascontiguousarray(np.broadcast_to(inp['gdn_dt_bias'][j][8 * hf:8 * hf + 8][None, :], (128, 8)).astype(np.float32)),
                gng=np.ascontiguousarray(inp['gdn_norm_g'][j].reshape(128, 1).astype(np.float32)),
                masks=masks, identb=np.eye(128, dtype=np.float32).astype(ml_dtypes.bfloat16),
                chm2=(np.arange(128)[:, None] // 64 == np.arange(2)[None, :]).astype(np.float32))


_PROG = {}


def _prog(key, fn):
    if key not in _PROG:
        _PROG[key] = fn()
    return _PROG[key]


def kernel_unfused(**inp):
    inp = {k: np.asarray(v) for k, v in inp.items()}
    x = inp['x']
    ng = inp['norm_g']
    f32 = np.float32

    def ffn_w(i, j, f):
        return {f'wg{f}': inp['ffn_w_gate'][i, j], f'wu{f}': inp['ffn_w_up'][i, j], f'wd{f}': inp['ffn_w_down'][i, j]}

    nc = _prog('tp0', lambda: build_token_phase(dict(mix=None, n_ffn=1, tail='h')))
    maps = []
    for c in range(8):
        b, hf = c // 2, c % 2
        m = dict(xT=np.ascontiguousarray(x[b, hf * TOK:(hf + 1) * TOK].T.astype(f32)), ng0=fm(ng[0, 0]), ngt=fm(ng[0, 1]))
        m.update(ffn_w(0, 0, 0))
        maps.append(m)
    res = run(nc, maps)
    xs = [r['xo'] for r in res]
    hs = [r['ho'] for r in res]

    mixers = [
        ('s5', build_s5, s5_inputs, 'ygT', 'glu', 1024, inp['s5_w_glu'][0]),
        ('hg', lambda: build_hgrn(1), hgrn_inputs, 'oT', 'lin', 1024, inp['hg_w_out'][0]),
        ('gdn', build_gdn, gdn_inputs, 'oT', 'lin', 2048, inp['gdn_w_out'][0]),
        ('moba', build_moba, moba_inputs, 'oT', 'lin', 1024, inp['moba_w_out'][0]),
    ]
    out = None
    for i, (name, bfn, ifn, okey, mixkind, mix_k, w_mix) in enumerate(mixers):
        ncm = _prog(name, bfn)
        maps = []
        for c in range(8):
            b, hf = c // 2, c % 2
            hT_full = h_gather_layout(hs[2 * b], hs[2 * b + 1])
            if name == 's5':
                maps.append(ifn(inp, b, hf, hT_full))
            else:
                maps.append(ifn(inp, hf, hT_full))
        res = run(ncm, maps)
        os_ = [r[okey] for r in res]
        last = (i == 3)
        cfg = dict(mix=mixkind, mix_k=mix_k, n_ffn=(1 if last else 2), tail=('final' if last else 'h'))
        nct = _prog(('tp', mixkind, mix_k, last), lambda: build_token_phase(cfg))
        maps = []
        for c in range(8):
            b, hf = c // 2, c % 2
            oT = o_gather_layout(os_[2 * b], os_[2 * b + 1])
            m = dict(xT=xs[c], oT=oT, hfm=hfm_arr(hf), w_mix=w_mix, ng0=fm(ng[i, 2]))
            m.update(ffn_w(i, 1, 0))
            if last:
                m['ngt'] = fm(inp['final_norm_g'])
            else:
                m['ng1'] = fm(ng[i + 1, 0])
                m['ngt'] = fm(ng[i + 1, 1])
                m.update(ffn_w(i + 1, 0, 1))
            maps.append(m)
        res = run(nct, maps)
        if last:
            out = np.zeros((NB, L, D), f32)
            for c in range(8):
                b, hf = c // 2, c % 2
                out[b, hf * TOK:(hf + 1) * TOK] = res[c]['out'].T
        else:
            xs = [r['xo'] for r in res]
            hs = [r['ho'] for r in res]
    return out


RG = [[0, 1], [2, 3], [4, 5], [6, 7]]
MIXERS = [('s5', 'ygT', 512, 'glu', 1024), ('hg', 'oT', 512, 'lin', 1024), ('gdn', 'oT', 1024, 'lin', 2048), ('moba', 'oT', 512, 'lin', 1024)]


def _mixer_build(name, **kw):
    if name == 's5':
        return build_s5(**kw)
    if name == 'hg':
        return build_hgrn(1, **kw)
    if name == 'gdn':
        return build_gdn(**kw)
    return build_moba(**kw)


def build_fused(stop=99):
    P = Prog()
    nc, R = P.nc, P.rec
    x_in = nc.dram_tensor("xT_in", [D, TOK], F32, kind="ExternalInput").ap()
    out = nc.dram_tensor("out", [D, TOK], F32, kind="ExternalOutput").ap()
    x_scr = nc.dram_tensor("x_scr", [D, TOK], F32)
    h_loc = nc.dram_tensor("h_loc", [D, TOK], BF16)
    h_all = nc.dram_tensor("h_all", [2 * D, TOK], BF16)
    o_loc = {n: nc.dram_tensor(f"o_loc{n}", [n, L], BF16) for n in (512, 1024)}
    o_all = {n: nc.dram_tensor(f"o_all{n}", [n, L], BF16) for n in (1024, 2048)}

    def ag(src, dst):
        rows, cols = src.ap().shape
        pr = (2 * 1024 * 1024) // (cols * 2)
        for p in range(rows // pr):
            R.add('pool', I('collective_compute', kind="AllGather", op=ALU.bypass, replica_groups=RG,
                            ins=[src.ap()[p * pr:(p + 1) * pr, :].opt()], outs=[dst.ap()[2 * p * pr:2 * (p + 1) * pr, :].opt()]),
                  dma='cc', inc=1)
        R.flush()

    build_token_phase(dict(mix=None, n_ffn=1, tail='h'), P, 't0_', {'xT': x_in, 'xo': x_scr.ap(), 'ho': h_loc.ap()})
    step = [0]

    def done():
        step[0] += 1
        return step[0] >= stop

    if done():
        R.flush(final=True)
        return P
    ag(h_loc, h_all)
    if done():
        R.flush(final=True)
        return P
    for i, (name, okey, half, mixkind, mix_k) in enumerate(MIXERS):
        _mixer_build(name, P=P, prefix=f'm{i}_', bind={'hT': h_all.ap(), okey: o_loc[half].ap()})
        if done():
            break
        ag(o_loc[half], o_all[2 * half])
        if done():
            break
        last = (i == 3)
        cfg = dict(mix=mixkind, mix_k=mix_k, n_ffn=(1 if last else 2), tail=('final' if last else 'h'))
        bind = {'xT': x_scr.ap(), 'oT': o_all[2 * half].ap()}
        if last:
            bind['out'] = out
        else:
            bind['xo'] = x_scr.ap()
            bind['ho'] = h_loc.ap()
        build_token_phase(cfg, P, f't{i + 1}_', bind)
        if done():
            break
        if not last:
            ag(h_loc, h_all)
            if done():
                break
    R.flush(final=True)
    return P


def fused_inputs(inp, c):
    b, hf = c // 2, c % 2
    ng = inp['norm_g']
    f32 = np.float32
    m = {'xT_in': np.ascontiguousarray(inp['x'][b, hf * TOK:(hf + 1) * TOK].T.astype(f32))}

    def put(prefix, d):
        for k, v in d.items():
            m[prefix + k] = v

    def ffn_w(i, j, f):
        return {f'wg{f}': inp['ffn_w_gate'][i, j], f'wu{f}': inp['ffn_w_up'][i, j], f'wd{f}': inp['ffn_w_down'][i, j]}

    d = dict(ng0=fm(ng[0, 0]), ngt=fm(ng[0, 1]))
    d.update(ffn_w(0, 0, 0))
    put('t0_', d)
    wmix = [inp['s5_w_glu'][0], inp['hg_w_out'][0], inp['gdn_w_out'][0], inp['moba_w_out'][0]]
    for i, (name, okey, half, mixkind, mix_k) in enumerate(MIXERS):
        if name == 's5':
            d = s5_inputs(inp, b, hf, None)
        elif name == 'hg':
            d = hgrn_inputs(inp, hf, None)
        elif name == 'gdn':
            d = gdn_inputs(inp, hf, None)
        else:
            d = moba_inputs(inp, hf, None)
        d.pop('hT')
        put(f'm{i}_', d)
        last = (i == 3)
        d = dict(hfm=hfm_arr(hf), w_mix=wmix[i], ng0=fm(ng[i, 2]))
        d.update(ffn_w(i, 1, 0))
        if last:
            d['ngt'] = fm(inp['final_norm_g'])
        else:
            d['ng1'] = fm(ng[i + 1, 0])
            d['ngt'] = fm(ng[i + 1, 1])
            d.update(ffn_w(i + 1, 0, 1))
        put(f't{i + 1}_', d)
    return m


def kernel(**inp):
    inp = {k: np.asarray(v) for k, v in inp.items()}
    P = _prog('fused', build_fused)
    maps = []
    for c in range(8):
        m = fused_inputs(inp, c)
        assert set(m.keys()) == set(P.ext.keys()) | {'xT_in'}, (sorted(set(m.keys()) ^ (set(P.ext.keys()) | {'xT_in'})))
        maps.append(m)
    res = run(P.nc, maps)
    out = np.zeros((NB, L, D), np.float32)
    for c in range(8):
        b, hf = c // 2, c % 2
        out[b, hf * TOK:(hf + 1) * TOK] = res[c]['out'].T
    return out
```
